# Optimizing a Trainium2 kernel written in Bass

```python
import jax, jax.numpy as jnp
from jax import lax
import numpy as np

D_MODEL = 4096
BATCH = 1
SEQ = 16384
DEPTH = 1
DEC_BATCH = 1
DEC_SEQ = 8192
PAST_LEN = 128

GRID_W = 64
HEAD_DIM = 128
NA_HEADS = D_MODEL // 256
NA_WIDTH = NA_HEADS * HEAD_DIM
WIN_ROWS = 8
WIN_COLS = 16
LRU_WIDTH = D_MODEL // 2
LRU_BLOCKS = 16
LRU_BLOCK_W = LRU_WIDTH // LRU_BLOCKS
CONV_W = 4
LRU_C = 8.0
D_FF = -(-8 * D_MODEL // (3 * 256)) * 256
IN_WIDTH = 3 * NA_WIDTH + 2 * LRU_WIDTH
EPS = 1e-6

kernel_name = "hybrid_natten_rglru_encoder"


def rmsnorm(x, g):
    x32 = x.astype(jnp.float32)
    y = x32 * lax.rsqrt(jnp.mean(x32 * x32, axis=-1, keepdims=True) + EPS)
    return y.astype(x.dtype) * g


def neighbourhood_attention(q, k, v, rpb):
    b, L, h, dh = q.shape
    rows = L // GRID_W
    kr = min(WIN_ROWS, rows)
    qg = q.reshape(b, rows, GRID_W, h, dh)
    kg = k.reshape(b, rows, GRID_W, h, dh)
    vg = v.reshape(b, rows, GRID_W, h, dh)
    cols = jnp.arange(GRID_W)
    col_start = jnp.clip(cols - WIN_COLS // 2, 0, GRID_W - WIN_COLS)
    col_idx = col_start[:, None] + jnp.arange(WIN_COLS)[None, :]
    dc = col_idx - cols[:, None] + (WIN_COLS - 1)
    scale = dh ** -0.5

    def row_block(args):
        r, q_row = args
        rs = jnp.clip(r - kr // 2, 0, rows - kr)
        k_rows = lax.dynamic_slice_in_dim(kg, rs, kr, axis=1)
        v_rows = lax.dynamic_slice_in_dim(vg, rs, kr, axis=1)
        k_win = k_rows[:, :, col_idx]
        v_win = v_rows[:, :, col_idx]
        dr = rs + jnp.arange(kr) - r + (WIN_ROWS - 1)
        bias = rpb[:, dr[None, :, None], dc[:, None, :]]
        s = jnp.einsum('bqhd,brqwhd->bhqrw', q_row, k_win).astype(jnp.float32) * scale
        s = s + bias.astype(jnp.float32)[None]
        p = jax.nn.softmax(s.reshape(b, h, GRID_W, kr * WIN_COLS), axis=-1)
        p = p.reshape(s.shape).astype(v.dtype)
        return jnp.einsum('bhqrw,brqwhd->bqhd', p, v_win)

    out = lax.map(row_block, (jnp.arange(rows), jnp.moveaxis(qg, 1, 0)))
    return jnp.moveaxis(out, 0, 1).reshape(b, L, h * dh)


def depthwise_conv_centred(x, w, bias):
    L = x.shape[1]
    left = CONV_W // 2
    right = CONV_W - 1 - left
    xp = jnp.pad(x, ((0, 0), (left, right), (0, 0)))
    return sum(xp[:, j:j + L] * w[j] for j in range(CONV_W)) + bias


def rg_lru_direction(x, w_r, b_r, w_i, b_i, lam, reverse):
    b, L, c = x.shape
    xb = x.reshape(b, L, LRU_BLOCKS, LRU_BLOCK_W)
    r = jax.nn.sigmoid(jnp.einsum('blnc,ncm->blnm', xb, w_r).reshape(b, L, c) + b_r)
    i = jax.nn.sigmoid(jnp.einsum('blnc,ncm->blnm', xb, w_i).reshape(b, L, c) + b_i)
    log_a = (-LRU_C * r.astype(jnp.float32)) * jax.nn.softplus(-lam.astype(jnp.float32))
    a = jnp.exp(log_a)
    mult = jnp.sqrt(-jnp.expm1(2.0 * log_a))
    first = L - 1 if reverse else 0
    is_first = (jnp.arange(L) == first)[None, :, None]
    mult = jnp.where(is_first, 1.0, mult)
    u = mult * (i * x).astype(jnp.float32)

    def combine(e1, e2):
        a1, b1 = e1
        a2, b2 = e2
        return a1 * a2, a2 * b1 + b2

    _, h = lax.associative_scan(combine, (a, u), axis=1, reverse=reverse)
    return h


def swiglu(x, w_gate, w_up, w_down):
    return (jax.nn.silu(x @ w_gate) * (x @ w_up)) @ w_down


def encoder_layer(x, w_in, w_conv, b_conv, w_rgate, b_rgate, w_igate, b_igate, lru_lambda,
                  rpb, w_na_out, w_lru_out, w_merge, b_merge, w_out,
                  g_mix_pre, g_mix_post, g_ffn_pre, g_ffn_post, w_ffn_gate, w_ffn_up, w_ffn_down):
    b, L, _ = x.shape
    xn = rmsnorm(x, g_mix_pre)
    proj = xn @ w_in
    q, k, v, xr, gr = jnp.split(
        proj, [NA_WIDTH, 2 * NA_WIDTH, 3 * NA_WIDTH, 3 * NA_WIDTH + LRU_WIDTH], axis=-1)
    q = q.reshape(b, L, NA_HEADS, HEAD_DIM)
    k = k.reshape(b, L, NA_HEADS, HEAD_DIM)
    v = v.reshape(b, L, NA_HEADS, HEAD_DIM)
    na = neighbourhood_attention(q, k, v, rpb)

    xc = depthwise_conv_centred(xr, w_conv, b_conv)
    h = (rg_lru_direction(xc, w_rgate[0], b_rgate[0], w_igate[0], b_igate[0], lru_lambda[0], False)
         + rg_lru_direction(xc, w_rgate[1], b_rgate[1], w_igate[1], b_igate[1], lru_lambda[1], True))
    lru = h.astype(x.dtype) * jax.nn.gelu(gr)

    o_na = na @ w_na_out
    o_lru = lru @ w_lru_out
    gates = jax.nn.sigmoid(xn @ w_merge + b_merge)
    g_na, g_lru = jnp.split(gates, 2, axis=-1)
    mix = (g_na * o_na + g_lru * o_lru) @ w_out
    x = x + rmsnorm(mix, g_mix_post)
    f = swiglu(rmsnorm(x, g_ffn_pre), w_ffn_gate, w_ffn_up, w_ffn_down)
    return x + rmsnorm(f, g_ffn_post)


def setup_inputs(seed: int = 0) -> dict:
    key = jax.random.key(seed)
    ks = jax.random.split(key, 24)
    f32 = jnp.float32
    n = lambda k, shape, s: jax.random.normal(k, shape, f32) * s
    u = jax.random.uniform(ks[10], (DEPTH, 2, LRU_WIDTH), f32, minval=0.9, maxval=0.999)
    a0 = u ** (1.0 / LRU_C)
    lam = jnp.log(a0) - jnp.log1p(-a0)
    return {
        "x_prompt": n(ks[0], (BATCH, SEQ, D_MODEL), 1.0),
        "x_sample": n(ks[1], (DEC_BATCH, DEC_SEQ, D_MODEL), 1.0),
        "w_in": n(ks[2], (DEPTH, D_MODEL, IN_WIDTH), D_MODEL ** -0.5),
        "w_conv": n(ks[3], (DEPTH, CONV_W, LRU_WIDTH), CONV_W ** -0.5),
        "b_conv": n(ks[4], (DEPTH, LRU_WIDTH), 0.01),
        "w_rgate": n(ks[5], (DEPTH, 2, LRU_BLOCKS, LRU_BLOCK_W, LRU_BLOCK_W), LRU_BLOCK_W ** -0.5),
        "b_rgate": n(ks[6], (DEPTH, 2, LRU_WIDTH), 0.1),
        "w_igate": n(ks[7], (DEPTH, 2, LRU_BLOCKS, LRU_BLOCK_W, LRU_BLOCK_W), LRU_BLOCK_W ** -0.5),
        "b_igate": n(ks[8], (DEPTH, 2, LRU_WIDTH), 0.1),
        "lru_lambda": lam,
        "rpb": n(ks[9], (DEPTH, NA_HEADS, 2 * WIN_ROWS - 1, 2 * WIN_COLS - 1), 0.1),
        "w_na_out": n(ks[11], (DEPTH, NA_WIDTH, D_MODEL), NA_WIDTH ** -0.5),
        "w_lru_out": n(ks[12], (DEPTH, LRU_WIDTH, D_MODEL), LRU_WIDTH ** -0.5),
        "w_merge": n(ks[13], (DEPTH, D_MODEL, 2 * D_MODEL), D_MODEL ** -0.5),
        "b_merge": n(ks[14], (DEPTH, 2 * D_MODEL), 0.1),
        "w_out": n(ks[15], (DEPTH, D_MODEL, D_MODEL), D_MODEL ** -0.5),
        "g_mix_pre": 1.0 + n(ks[16], (DEPTH, D_MODEL), 0.02),
        "g_mix_post": 1.0 + n(ks[17], (DEPTH, D_MODEL), 0.02),
        "g_ffn_pre": 1.0 + n(ks[18], (DEPTH, D_MODEL), 0.02),
        "g_ffn_post": 1.0 + n(ks[19], (DEPTH, D_MODEL), 0.02),
        "w_ffn_gate": n(ks[20], (DEPTH, D_MODEL, D_FF), D_MODEL ** -0.5),
        "w_ffn_up": n(ks[21], (DEPTH, D_MODEL, D_FF), D_MODEL ** -0.5),
        "w_ffn_down": n(ks[22], (DEPTH, D_FF, D_MODEL), D_FF ** -0.5),
    }


def reference(x_prompt, x_sample, w_in, w_conv, b_conv, w_rgate, b_rgate, w_igate, b_igate,
              lru_lambda, rpb, w_na_out, w_lru_out, w_merge, b_merge, w_out,
              g_mix_pre, g_mix_post, g_ffn_pre, g_ffn_post, w_ffn_gate, w_ffn_up, w_ffn_down):
    y_prompt = x_prompt
    y_sample = x_sample
    for l in range(DEPTH):
        params = (w_in[l], w_conv[l], b_conv[l], w_rgate[l], b_rgate[l], w_igate[l], b_igate[l],
                  lru_lambda[l], rpb[l], w_na_out[l], w_lru_out[l], w_merge[l], b_merge[l], w_out[l],
                  g_mix_pre[l], g_mix_post[l], g_ffn_pre[l], g_ffn_post[l],
                  w_ffn_gate[l], w_ffn_up[l], w_ffn_down[l])
        y_prompt = encoder_layer(y_prompt, *params)
        y_sample = encoder_layer(y_sample, *params)
    return (y_prompt, y_sample)
```

```python
import contextlib
import numpy as np
import concourse.bass as bass
import concourse.mybir as mybir
from concourse.bass_utils import run_bass_kernel_spmd

F32 = mybir.dt.float32
BF16 = mybir.dt.bfloat16
AF = mybir.ActivationFunctionType
ALU = mybir.AluOpType

NCORES = 8
D = 4096
KC = D // 128
LP, LS = 16384, 8192
OWN_P, OWN_S = LP // NCORES, LS // NCORES
OWN = OWN_P + OWN_S
EXT_P, EXT_S = OWN_P + 512, OWN_S + 512
EXT = EXT_P + EXT_S
NAW = 2048
LRW = 2048
DFF = 11008
NH = 16
EPS = 1e-6
SEM_LIMIT = 20000
NEG = -30000.0


class Buf:
    ALL = []

    def __init__(self, name, dram=False):
        if not dram:
            Buf.ALL.append(self)
        self.name = name
        self.dram = dram
        self.w = {}
        self.r = {}
        self.dsem = None
        self.dcount = 0


def _merge(d, ev):
    if ev is None:
        return
    k = id(ev[0])
    if k not in d or d[k][1] < ev[1]:
        d[k] = ev


class TR:
    def __init__(self, nc, stack):
        self.nc = nc
        self.stack = stack
        self.engs = {"pe": nc.tensor, "act": nc.scalar, "dve": nc.vector, "pool": nc.gpsimd, "sp": nc.sync}
        self.cur = {}
        self.seen = {e: {} for e in self.engs}
        self.nsem = 0
        self.keep = []
        self.last_swdge = None

    def new_sem(self, name):
        self.nsem += 1
        s = self.stack.enter_context(self.nc.semaphore(f"{name}_{self.nsem}"))
        self.keep.append(s)
        return s

    def wait(self, e, ev):
        if ev is None:
            return
        s, v = ev
        d = self.seen[e]
        if d.get(id(s), 0) >= v:
            return
        self.engs[e].wait_ge(s, v)
        d[id(s)] = v

    def wait_all(self, e, evs):
        for ev in list(evs.values()):
            self.wait(e, ev)

    def inc(self, e, ins):
        st = self.cur.get(e)
        if st is None or st[1] >= SEM_LIMIT:
            st = [self.new_sem("c" + e), 0]
            self.cur[e] = st
        st[1] += 1
        ins.then_inc(st[0], 1)
        return (st[0], st[1])

    def deps(self, e, reads, writes):
        for b in reads:
            self.wait_all(e, b.w)
        for b in writes:
            if not b.dram:
                self.wait_all(e, b.w)
                self.wait_all(e, b.r)

    def done(self, ev, reads, writes):
        for b in reads:
            if not b.dram:
                _merge(b.r, ev)
        for b in writes:
            if b.dram:
                _merge(b.w, ev)
            else:
                b.w = {id(ev[0]): ev}
                b.r = {}

    def op(self, e, reads, writes, fn):
        self.deps(e, reads, writes)
        ins = fn(self.engs[e])
        ev = self.inc(e, ins)
        self.done(ev, reads, writes)
        return ev

    def pe_group(self, reads, writes, fns):
        self.deps("pe", reads, writes)
        ins = None
        for fn in fns:
            ins = fn(self.engs["pe"])
        ev = self.inc("pe", ins)
        self.done(ev, reads, writes)
        return ev

    def dma(self, q, out_ap, in_ap, reads, writes, slot):
        self.deps(q, reads, writes)
        if q == "pool" and self.last_swdge is not None:
            self.wait("pool", self.last_swdge)
        if slot.dsem is None:
            slot.dsem = self.new_sem("d")
        slot.dcount += 16
        self.engs[q].dma_start(out=out_ap, in_=in_ap).then_inc(slot.dsem, 16)
        ev = (slot.dsem, slot.dcount)
        if q == "pool":
            self.last_swdge = ev
        self.done(ev, reads, writes)
        return ev


class K:
    pass


def barrier(k):
    tr = k.tr
    evs = {}
    for b in Buf.ALL:
        for ev in list(b.w.values()) + list(b.r.values()):
            _merge(evs, ev)
    for e in tr.engs:
        tr.wait_all(e, evs)
    Buf.ALL[:] = list(k.psb) + [k.b_ident, k.b_flags, k.b_eps] + [b for b in (getattr(k, 'ssq1_b', None), getattr(k, 'ssq2_b', None), getattr(k, 'sumA_b', None), getattr(k, 'sumH_b', None)) if b]


def build(debug_outs=(), phases=("proj",)):
    Buf.ALL[:] = []
    nc = bass.Bass("TRN2", target_bir_lowering=False)
    k = K()
    k.nc = nc
    dt_in = lambda name, shape, dt=F32: nc.dram_tensor(name, list(shape), dt, kind="ExternalInput").ap()

    def scratch(name, shape, dt):
        kind = "ExternalOutput" if name in debug_outs else "Internal"
        return nc.dram_tensor(name, list(shape), dt, kind=kind).ap()

    shapes = {
        "xp": [LP, D], "xs": [LS, D], "xo": [EXT, D], "w_in": [D, 10240], "w_merge": [D, 2 * D],
        "w_na": [NAW, D], "w_lru": [LRW, D], "w_out": [D, D], "w_fg": [D, DFF], "w_fu": [D, DFF],
        "w_fd": [DFF, D], "g4": [4, D], "b_merge": [128, 64], "convp": [128, 16, 5],
        "wgate": [128, 64, 128], "gatep": [128, 16, 6], "tt": [NH, 128, 9, 128], "flags": [128, 8],
        "sel": [128, 4, 17], "ident": [128, 128],
    }
    k.used = {}

    def din(name):
        if name not in k.used:
            k.used[name] = dt_in(name, shapes[name])
        return k.used[name]
    k.din = din
    k.y = nc.dram_tensor("y", [OWN, D], F32, kind="ExternalOutput").ap()

    k.QT = scratch("QT", [NAW, EXT], BF16)
    k.KT = scratch("KT", [NAW, EXT], BF16)
    k.VS = scratch("VS", [EXT, NAW], BF16)
    k.XR = scratch("XR", [LRW, EXT], F32)
    k.GR = scratch("GR", [LRW, EXT], F32)
    k.GT = scratch("GT", [2 * D, EXT], BF16)
    k.NA = scratch("NA", [NAW, OWN], BF16)
    k.LR = scratch("LR", [LRW, OWN], BF16)
    k.GD = scratch("GD", [D, OWN], BF16)
    k.MIX = scratch("MIX", [OWN, D], F32)
    k.X1 = scratch("X1", [OWN, D], F32)
    k.XN2 = scratch("XN2", [D, OWN], BF16)
    k.HH = scratch("HH", [DFF, OWN], BF16)
    k.FF = scratch("FF", [OWN, D], F32)

    with contextlib.ExitStack() as stack:
        tr = TR(nc, stack)
        k.tr = tr
        k.stack = stack
        sb = lambda name, shape, dt: stack.enter_context(nc.sbuf_tensor(name, list(shape), dt))
        k.sb = sb
        k.sb_glob = sb
        k.ps = [stack.enter_context(nc.psum_tensor(f"ps{i}", [128, 512], F32)) for i in range(8)]
        k.psb = [Buf(f"ps{i}") for i in range(8)]
        k.bank_rr = 0
        k.ident_f = sb("ident_f", [128, 128], F32)
        k.ident = sb("ident_b", [128, 128], BF16)
        k.b_ident = Buf("ident")
        k.flags_t = sb("flags_sb", [128, 8], F32)
        k.b_flags = Buf("flags")
        k.d_in = Buf("inputs", dram=True)
        tr.dma("sp", k.ident_f[:], k.din("ident")[:, :], [k.d_in], [k.b_ident], k.b_ident)
        tr.op("dve", [k.b_ident], [k.b_ident], lambda e: e.tensor_copy(out=k.ident[:], in_=k.ident_f[:]))
        tr.dma("sp", k.flags_t[:], k.din("flags")[:, :], [k.d_in], [k.b_flags], k.b_flags)
        k.eps_t = sb("eps_t", [128, 2], F32)
        k.b_eps = Buf("eps")
        tr.op("dve", [], [k.b_eps], lambda e: e.memset(k.eps_t[:, 0:1], EPS))
        tr.op("dve", [k.b_eps], [k.b_eps], lambda e: e.memset(k.eps_t[:, 1:2], 1.0))

        k.dbufs = {}
        for nm in ("QT", "KT", "VS", "XR", "GR", "GT", "NA", "LR", "GD", "MIX", "X1", "XN2", "HH", "FF", "y"):
            k.dbufs[nm] = Buf(nm, dram=True)
            setattr(k, nm + "_b", k.dbufs[nm])
        k.ssq1 = sb("ssq1", [128, 384], F32); k.ssq1_b = Buf("ssq1")
        k.ssq2 = sb("ssq2", [128, 384], F32); k.ssq2_b = Buf("ssq2")
        tr.op("dve", [], [k.ssq1_b], lambda e: e.memset(k.ssq1[:], 0.0))
        tr.op("dve", [], [k.ssq2_b], lambda e: e.memset(k.ssq2[:], 0.0))
        need = []
        for ph, nms in (("proj", ("w_in", "w_merge")), ("lru1", ("w_xr",)), ("gate", ("w_na", "w_lru")),
                        ("wout", ("w_out",)), ("ffn", ("w_fg", "w_fu", "w_fd"))):
            if ph in phases:
                need += [n for n in nms if n not in need]
        phase_cast(k, need)
        barrier(k)
        if "proj" in phases:
            phase_proj(k)
            barrier(k)
        if "lru1" in phases:
            phase_lru_summ(k)
            barrier(k)
        if "lru2" in phases:
            phase_lru_own(k)
            barrier(k)
        if "attn" in phases:
            phase_attn(k)
            barrier(k)
        if "gate" in phases:
            phase_gate(k)
            barrier(k)
        if "wout" in phases:
            phase_wout(k)
            barrier(k)
            import os
            if not os.environ.get("WO_SKIP_NORM"):
              norm_resid(k, "n1", k.MIX, k.MIX_b, k.ssq1, k.ssq1_b, 1,
                       lambda i: (k.din("xo")[own_ext(i * 128):own_ext(i * 128) + 128, :], k.d_in), k.X1, k.X1_b)
            barrier(k)
        if "ffn" in phases:
            phase_ffn_up(k)
            barrier(k)
            phase_ffn_down(k)
            barrier(k)
            norm_resid(k, "n2", k.FF, k.FF_b, k.ssq2, k.ssq2_b, 3,
                       lambda i: (k.X1[i * 128:(i + 1) * 128, :], k.X1_b), k.y, k.y_b)
            barrier(k)

        for b in k.dbufs.values():
            tr.wait_all("sp", b.w)
    return nc, sorted(k.used)


_evac_rr = [0]


def evac_engine():
    _evac_rr[0] ^= 1
    return "act" if _evac_rr[0] else "dve"


def copy_op(k, e, out_ap, in_ap, reads, writes):
    if e == "act":
        return k.tr.op("act", reads, writes, lambda g: g.activation(out=out_ap, in_=in_ap, func=AF.Copy))
    return k.tr.op(e, reads, writes, lambda g: g.tensor_copy(out=out_ap, in_=in_ap))


class RmsT:
    def __init__(self, k, pfx, g_row, nx=2):
        sb, tr = k.sb, k.tr
        self.nx = nx
        self.xt = [sb(f"{pfx}_x{i}", [128, D], F32) for i in range(nx)]
        self.xt_b = [Buf(f"{pfx}_x{i}") for i in range(nx)]
        self.xn = sb(f"{pfx}_xn", [128, D], BF16)
        self.xn_b = Buf(f"{pfx}_xn")
        self.gbc = sb(f"{pfx}_gbc", [128, D], F32)
        self.gbc_b = Buf(f"{pfx}_gbc")
        self.st = sb(f"{pfx}_st", [128, 4], F32)
        self.st_b = Buf(f"{pfx}_st")
        self.i = 0
        tr.dma("sp", self.gbc[:], k.din("g4")[g_row:g_row + 1, :].broadcast_to([128, D]), [k.d_in], [self.gbc_b], self.gbc_b)


def rms_tile(k, rt, src_ap, src_bufs, dstT, dstT_b, col0, ncols=128, zero_first=False, srcs=None):
    tr = k.tr
    i = rt.i
    rt.i = (rt.i + 1) % rt.nx
    xt, xb = rt.xt[i], rt.xt_b[i]
    if zero_first:
        tr.op("dve", [], [xb], lambda e: e.memset(xt[:], 0.0))
    if src_ap is not None:
        tr.dma("sp", xt[:, :], src_ap, src_bufs, [xb], xb)
    for (r0, nr, ap) in (srcs or []):
        tr.dma("sp", xt[r0:r0 + nr, :], ap, src_bufs, [xb], xb)
    tr.op("act", [xb], [rt.xn_b, rt.st_b],
          lambda e: e.activation(out=rt.xn[:], in_=xt[:], func=AF.Square, accum_out=rt.st[:, 0:1]))
    tr.op("act", [rt.st_b, k.b_eps], [rt.st_b],
          lambda e: e.activation(out=rt.st[:, 1:2], in_=rt.st[:, 0:1], func=AF.Sqrt, scale=1.0 / D, bias=k.eps_t[:, 0:1]))
    tr.op("dve", [rt.st_b], [rt.st_b], lambda e: e.reciprocal(out=rt.st[:, 2:3], in_=rt.st[:, 1:2]))
    tr.op("dve", [xb, rt.st_b, rt.gbc_b], [rt.xn_b],
          lambda e: e.scalar_tensor_tensor(out=rt.xn[:], in0=xt[:], scalar=rt.st[:, 2:3], in1=rt.gbc[:],
                                           op0=ALU.mult, op1=ALU.mult))
    for g in range(4):
        bi = k.bank_rr
        k.bank_rr = (k.bank_rr + 1) % 8
        pb = k.ps[bi][:].bitcast(BF16)
        fns = []
        for j in range(8):
            kc = g * 8 + j
            fns.append(lambda e, kc=kc, j=j: e.transpose(out=pb[:, j * 128:(j + 1) * 128],
                                                         in_=rt.xn[:, kc * 128:(kc + 1) * 128],
                                                         identity=k.ident[:]))
        tr.pe_group([rt.xn_b, k.b_ident], [k.psb[bi]], fns)
        copy_op(k, evac_engine(), dstT[:, g * 8:(g + 1) * 8, col0:col0 + ncols],
                pb.rearrange("p (j t) -> p j t", j=8)[:, :, 0:ncols], [k.psb[bi]], [dstT_b])


class Slabs:
    def __init__(self, k, pfx, kc, cols, n=2):
        self.t = [k.sb(f"{pfx}{i}", [128, kc, cols], BF16) for i in range(n)]
        self.b = [Buf(f"{pfx}{i}") for i in range(n)]
        self.i = 0
        self.n = n

    def next(self):
        i = self.i
        self.i = (self.i + 1) % self.n
        return self.t[i], self.b[i]


def load_slab(k, slabs, name, kg, cg, nkc=None):
    t, b = slabs.next()
    ap, buf, wnkc, gc = k.wb[name]
    nkc = nkc or wnkc
    k.tr.dma("pool", t[:, 0:nkc, 0:gc], ap[kg, cg, :, 0:nkc, :], [buf], [b], b)
    return t, b


WB_SPECS = {
    "w_in": (D, 10240, 32, 256), "w_merge": (D, 2 * D, 32, 256), "w_na": (NAW, D, 16, 256), "w_lru": (LRW, D, 16, 256),
    "w_out": (D, D, 32, 512), "w_fg": (D, DFF, 32, 256), "w_fu": (D, DFF, 32, 256), "w_fd": (DFF, D, 8, 512),
    "w_xr": (D, LRW, 32, 128),
}
WB_SRC = {"w_xr": ("w_in", 6144)}


def phase_cast(k, names):
    tr = k.tr
    k.wb = {}
    with phase_scope(k):
        sb = k.sb
        fst = Stage(k, "cs_f", [128, 4096], F32, n=3)
        bst = Stage(k, "cs_b", [128, 4096], BF16, n=3)
        items = []
        for nm in names:
            K_, N_, nkc, gc = WB_SPECS[nm]
            tkc = K_ // 128
            n_kg = -(-tkc // nkc)
            n_cg = N_ // gc
            ap = k.nc.dram_tensor("wb_" + nm, [n_kg, n_cg, 128, nkc, gc], BF16, kind="Internal").ap()
            buf = Buf("wb_" + nm, dram=True)
            k.wb[nm] = (ap, buf, nkc, gc)
            k.dbufs["wb_" + nm] = buf
            src, sc0 = WB_SRC.get(nm, (nm, 0))
            wv = k.din(src).rearrange("(kc p) n -> p kc n", p=128)
            for kg in range(n_kg):
                nk = min(nkc, tkc - kg * nkc)
                for cg in range(n_cg):
                    for q in range(0, nk, 8):
                        items.append((wv, ap, buf, kg, cg, q, min(8, nk - q), nkc, gc, sc0))
        LA = 2
        pend = {}
        engs = ("dve", "act", "pool", "act", "dve")
        for i in range(len(items) + LA):
            if i < len(items):
                (wv, ap, buf, kg, cg, q, m, nkc, gc, sc0) = items[i]
                f, f_b = fst.next()
                fv = f[:].rearrange("p (a n) -> p a n", n=gc)
                tr.dma("sp", fv[:, 0:m, :], wv[:, kg * nkc + q:kg * nkc + q + m, sc0 + cg * gc:sc0 + (cg + 1) * gc],
                       [k.d_in], [f_b], f_b)
                pend[i] = (f, f_b)
            j = i - LA
            if j >= 0:
                (wv, ap, buf, kg, cg, q, m, nkc, gc, sc0) = items[j]
                f, f_b = pend.pop(j)
                o, o_b = bst.next()
                copy_op(k, engs[j % len(engs)], o[:, 0:m * gc], f[:, 0:m * gc], [f_b], [o_b])
                ov = o[:].rearrange("p (a n) -> p a n", n=gc)
                tr.dma("sp", ap[kg, cg, :, q:q + m, :], ov[:, 0:m, :], [o_b], [buf], o_b)


def linear_F(k, xT, xT_b, nkc, T, wname, col0, ncols, slabs, epilogue, gcols=256):
    tr = k.tr
    nh = T // 512
    for c0 in range(col0, col0 + ncols, gcols):
        gc = min(gcols, col0 + ncols - c0)
        assert gc == gcols and c0 % gcols == 0
        slab, slab_b = load_slab(k, slabs, wname, 0, c0 // gcols)
        nfc = gc // 128
        for h in range(nh):
            banks = []
            for fc in range(nfc):
                bi = k.bank_rr
                k.bank_rr = (k.bank_rr + 1) % 8
                banks.append(bi)
            fns = []
            for kc in range(nkc):
                for fc in range(nfc):
                    fns.append(lambda e, kc=kc, fc=fc: e.matmul(
                        k.ps[banks[fc]][:], lhsT=slab[:, kc, fc * 128:(fc + 1) * 128],
                        rhs=xT[:, kc, h * 512:(h + 1) * 512], start=(kc == 0), stop=(kc == nkc - 1)))
            tr.pe_group([xT_b, slab_b], [k.psb[b] for b in banks], fns)
            for fc in range(nfc):
                epilogue((c0 + fc * 128) // 128, h, banks[fc])


def linear_T(k, xT, xT_b, nkc, T, wname, col0, ncols, slabs, epilogue, gcols=256):
    tr = k.tr
    nst = T // 128
    for c0 in range(col0, col0 + ncols, gcols):
        gc = min(gcols, col0 + ncols - c0)
        assert gc == gcols and c0 % gcols == 0
        slab, slab_b = load_slab(k, slabs, wname, 0, c0 // gcols)
        for s0 in range(0, nst, 4):
            banks = []
            for st in range(4):
                bi = k.bank_rr
                k.bank_rr = (k.bank_rr + 1) % 8
                banks.append(bi)
            fns = []
            for kc in range(nkc):
                for st in range(4):
                    fns.append(lambda e, kc=kc, st=st: e.matmul(
                        k.ps[banks[st]][:, 0:gc], lhsT=xT[:, kc, (s0 + st) * 128:(s0 + st + 1) * 128],
                        rhs=slab[:, kc, 0:gc], start=(kc == 0), stop=(kc == nkc - 1)))
            tr.pe_group([xT_b, slab_b], [k.psb[b] for b in banks], fns)
            for st in range(4):
                epilogue(c0, gc, s0 + st, banks[st])


class Stage:
    def __init__(self, k, pfx, shape, dt, n=2):
        self.t = [k.sb(f"{pfx}{i}", shape, dt) for i in range(n)]
        self.b = [Buf(f"{pfx}{i}") for i in range(n)]
        self.i = 0
        self.n = n

    def next(self):
        i = self.i
        self.i = (self.i + 1) % self.n
        return self.t[i], self.b[i]


def phase_proj(k):
    tr, sb = k.tr, k.sb
    with contextlib.ExitStack() as ph:
        old_stack, old_sb = k.stack, k.sb
        k.sb = lambda name, shape, dt: ph.enter_context(k.nc.sbuf_tensor(name, list(shape), dt))
        sb = k.sb
        rt = RmsT(k, "pj", 0)
        xnT = sb("pj_xnT", [128, KC, 1024], BF16)
        xnT_b = Buf("pj_xnT")
        slabs = Slabs(k, "pj_slab", KC, 256, n=2)
        stg_f = Stage(k, "pj_sf", [128, 512], F32, n=3)
        stg_h = Stage(k, "pj_sh", [128, 512], BF16, n=3)
        bm = sb("pj_bm", [128, 64], F32)
        bm_b = Buf("pj_bm")
        tr.dma("sp", bm[:], k.din("b_merge")[:, :], [k.d_in], [bm_b], bm_b)

        import os
        for blk in range(int(os.environ.get('PJ_BLOCKS', EXT // 1024))):
            t0 = blk * 1024
            stages = os.environ.get('PJ_STAGES', 'qkvxgm')
            for j in range(8):
                rms_tile(k, rt, k.din("xo")[t0 + j * 128:t0 + (j + 1) * 128, :], [k.d_in], xnT, xnT_b, j * 128)

            def ep_f(dst, dst_b, dt_is_f32, sigm_bias=None):
                def ep(f, h, bank, dst=dst, dst_b=dst_b):
                    stg = stg_f if dt_is_f32 else stg_h
                    t, b = stg.next()
                    if sigm_bias is not None:
                        tr.op("act", [k.psb[bank], bm_b], [b],
                              lambda e: e.activation(out=t[:], in_=k.ps[bank][:], func=AF.Sigmoid,
                                                     bias=bm[:, sigm_bias + f:sigm_bias + f + 1], scale=1.0))
                    else:
                        copy_op(k, evac_engine(), t[:], k.ps[bank][:], [k.psb[bank]], [b])
                    tr.dma("sp", dst[f * 128:(f + 1) * 128, t0 + h * 512:t0 + (h + 1) * 512], t[:], [b], [dst_b], b)
                return ep

            if 'q' in stages:
                linear_F(k, xnT, xnT_b, KC, 1024, "w_in", 0, int(os.environ.get('PJ_QCOLS', 2048)), slabs, ep_f(k.QT, k.QT_b, False))
            kt_ep = ep_f(k.KT, k.KT_b, False)
            if 'k' in stages:
                linear_F(k, xnT, xnT_b, KC, 1024, "w_in", 2048, 2048, slabs,
                         lambda f, h, bank: kt_ep(f - 16, h, bank))
            def ep_v(c0, gc, st, bank):
                t, b = stg_h.next()
                copy_op(k, evac_engine(), t[:, 0:gc], k.ps[bank][:, 0:gc], [k.psb[bank]], [b])
                tr.dma("sp", k.VS[t0 + st * 128:t0 + (st + 1) * 128, c0 - 4096:c0 - 4096 + gc], t[:, 0:gc],
                       [b], [k.VS_b], b)
            if 'v' in stages:
                linear_T(k, xnT, xnT_b, KC, 1024, "w_in", 4096, 2048, slabs, ep_v)
            xr_ep = ep_f(k.XR, k.XR_b, True)
            if 'x' in stages:
                linear_F(k, xnT, xnT_b, KC, 1024, "w_in", 6144, 2048, slabs, lambda f, h, bank: xr_ep(f - 48, h, bank))
            gr_ep = ep_f(k.GR, k.GR_b, True)
            if 'g' in stages:
                linear_F(k, xnT, xnT_b, KC, 1024, "w_in", 8192, 2048, slabs, lambda f, h, bank: gr_ep(f - 64, h, bank))
            if 'm' in stages:
                linear_F(k, xnT, xnT_b, KC, 1024, "w_merge", 0, 2 * D, slabs, ep_f(k.GT, k.GT_b, False, sigm_bias=0))
        k.sb = old_sb


def own_ext(o):
    return 256 + o if o < OWN_P else EXT_P + 256 + (o - OWN_P)


@contextlib.contextmanager
def phase_scope(k):
    with contextlib.ExitStack() as ph:
        old = k.sb
        k.sb = lambda name, shape, dt: ph.enter_context(k.nc.sbuf_tensor(name, list(shape), dt))
        try:
            yield
        finally:
            k.sb = old


def load_T(k, dst, dst_b, src_ap, src_b, nkc, c0, n):
    sv = src_ap.rearrange("(kc p) t -> p kc t", p=128)
    for q in range(0, nkc, 8):
        m = min(8, nkc - q)
        k.tr.dma("sp", dst[:, q:q + m, 0:n], sv[:, q:q + m, c0:c0 + n], [src_b], [dst_b], dst_b)


def phase_gate(k):
    tr = k.tr
    with phase_scope(k):
        sb = k.sb
        naT = sb("gt_naT", [128, 16, 1024], BF16); naT_b = Buf("gt_naT")
        lrT = sb("gt_lrT", [128, 16, 1024], BF16); lrT_b = Buf("gt_lrT")
        slabs = Slabs(k, "gt_slab", 16, 256, n=4)
        gts = Stage(k, "gt_g", [128, 2, 512], BF16, n=4)
        tmp = Stage(k, "gt_tmp", [128, 2, 512], F32, n=3)
        outs = Stage(k, "gt_out", [128, 512], BF16, n=3)
        for blk in range(OWN // 1024):
            o0 = blk * 1024
            e0 = own_ext(o0)
            load_T(k, naT, naT_b, k.NA, k.NA_b, 16, o0, 1024)
            load_T(k, lrT, lrT_b, k.LR, k.LR_b, 16, o0, 1024)
            for c0 in range(0, D, 256):
                s_na, s_na_b = load_slab(k, slabs, "w_na", 0, c0 // 256)
                s_lr, s_lr_b = load_slab(k, slabs, "w_lru", 0, c0 // 256)
                for h in range(2):
                    banks = []
                    for _ in range(4):
                        banks.append(k.bank_rr)
                        k.bank_rr = (k.bank_rr + 1) % 8
                    fns = []
                    for kc in range(16):
                        for fc in range(2):
                            fns.append(lambda e, kc=kc, fc=fc: e.matmul(
                                k.ps[banks[fc]][:], lhsT=s_na[:, kc, fc * 128:(fc + 1) * 128],
                                rhs=naT[:, kc, h * 512:(h + 1) * 512], start=(kc == 0), stop=(kc == 15)))
                    for kc in range(16):
                        for fc in range(2):
                            fns.append(lambda e, kc=kc, fc=fc: e.matmul(
                                k.ps[banks[2 + fc]][:], lhsT=s_lr[:, kc, fc * 128:(fc + 1) * 128],
                                rhs=lrT[:, kc, h * 512:(h + 1) * 512], start=(kc == 0), stop=(kc == 15)))
                    tr.pe_group([naT_b, lrT_b, s_na_b, s_lr_b], [k.psb[b] for b in banks], fns)
                    for fc in range(2):
                        f = c0 // 128 + fc
                        g, g_b = gts.next()
                        tr.dma("sp", g[:, 0, :], k.GT[f * 128:(f + 1) * 128, e0 + h * 512:e0 + (h + 1) * 512],
                               [k.GT_b], [g_b], g_b)
                        tr.dma("sp", g[:, 1, :], k.GT[(32 + f) * 128:(33 + f) * 128, e0 + h * 512:e0 + (h + 1) * 512],
                               [k.GT_b], [g_b], g_b)
                        t, t_b = tmp.next()
                        tr.op("dve", [k.psb[banks[fc]], g_b], [t_b],
                              lambda e: e.tensor_tensor(out=t[:, 0, :], in0=k.ps[banks[fc]][:], in1=g[:, 0, :], op=ALU.mult))
                        tr.op("dve", [k.psb[banks[2 + fc]], g_b], [t_b],
                              lambda e: e.tensor_tensor(out=t[:, 1, :], in0=k.ps[banks[2 + fc]][:], in1=g[:, 1, :], op=ALU.mult))
                        o, o_b = outs.next()
                        tr.op("pool", [t_b], [o_b],
                              lambda e: e.tensor_tensor(out=o[:], in0=t[:, 0, :], in1=t[:, 1, :], op=ALU.add))
                        tr.dma("sp", k.GD[f * 128:(f + 1) * 128, o0 + h * 512:o0 + (h + 1) * 512], o[:], [o_b], [k.GD_b], o_b)


class LruCommon:
    def rotate_piece(self):
        self.pi = (self.pi + 1) % self.nb
        for kk, v in self.psets[self.pi].items():
            setattr(self, kk, v)

    def rotate_conv(self):
        self.ci = (self.ci + 1) % self.nb
        for kk, v in self.csets[self.ci].items():
            setattr(self, kk, v)

    def __init__(self, k, pfx, T, nb=1):
        sb, tr = k.sb, k.tr
        self.nb = nb
        self.wgf = None
        self.wg = sb(pfx + "_wg", [128, 64, 128], BF16); self.wg_b = Buf(pfx + "_wg")
        tr.dma("pool", self.wg[:], k.din("wgate")[:, :, :], [k.d_in], [self.wg_b], self.wg_b)
        self.gp = sb(pfx + "_gp", [128, 16, 6], F32); self.gp_b = Buf(pfx + "_gp")
        tr.dma("sp", self.gp[:], k.din("gatep")[:, :, :], [k.d_in], [self.gp_b], self.gp_b)
        self.cp = sb(pfx + "_cp", [128, 16, 5], F32); self.cp_b = Buf(pfx + "_cp")
        tr.dma("sp", self.cp[:], k.din("convp")[:, :, :], [k.d_in], [self.cp_b], self.cp_b)
        self.cs = sb(pfx + "_cs", [128, 16, 2], F32); self.cs_b = Buf(pfx + "_cs")
        tr.op("act", [self.gp_b], [self.cs_b],
              lambda e: e.activation(out=self.cs[:], in_=self.gp[:, :, 4:6], func=AF.Exp, scale=-1.0))
        tr.op("act", [self.cs_b, k.b_eps], [self.cs_b],
              lambda e: e.activation(out=self.cs[:], in_=self.cs[:], func=AF.Ln, bias=k.eps_t[:, 1:2], scale=1.0))
        tr.op("dve", [self.cs_b], [self.cs_b], lambda e: e.tensor_scalar_mul(out=self.cs[:], in0=self.cs[:], scalar1=-8.0))
        self.psets, self.csets = [], []
        for i in range(nb):
            sfx = f"{pfx}_{i}"
            self.csets.append({"xc": sb(sfx + "_xc", [128, T], F32), "xc_b": Buf(sfx + "_xc"),
                               "xcb": sb(sfx + "_xcb", [128, T], BF16), "xcb_b": Buf(sfx + "_xcb")})
            self.psets.append({"A": sb(sfx + "_A", [128, 1024], F32), "A_b": Buf(sfx + "_A"),
                               "B": sb(sfx + "_B", [128, 1024], F32), "B_b": Buf(sfx + "_B"),
                               "C": sb(sfx + "_C", [128, 1024], F32), "C_b": Buf(sfx + "_C"),
                               "sr": sb(sfx + "_sr", [128, 4], F32), "sr_b": Buf(sfx + "_sr")})
        self.pi = self.ci = nb - 1
        self.rotate_piece()
        self.rotate_conv()


def lru_conv(k, L, xp, xp_b, n, T):
    tr = k.tr
    cp = L.cp
    tr.op("dve", [xp_b, L.cp_b], [L.xc_b],
          lambda e: e.tensor_scalar(out=L.xc[:, 0:T], in0=xp[:, 0:T], scalar1=cp[:, n, 0:1], scalar2=cp[:, n, 4:5],
                                    op0=ALU.mult, op1=ALU.add))
    for j in (1, 2, 3):
        tr.op("dve", [xp_b, L.cp_b, L.xc_b], [L.xc_b],
              lambda e: e.scalar_tensor_tensor(out=L.xc[:, 0:T], in0=xp[:, j:j + T], scalar=cp[:, n, j:j + 1],
                                               in1=L.xc[:, 0:T], op0=ALU.mult, op1=ALU.add))
    tr.op("pool", [L.xc_b], [L.xcb_b], lambda e: e.tensor_copy(out=L.xcb[:, 0:T], in_=L.xc[:, 0:T]))


def lru_piece(k, L, n, d, t0, first_fix, init_ap, init_bufs, out_tile, out_b, out_col0=0):
    tr = k.tr
    banks = []
    for _ in range(4):
        banks.append(k.bank_rr)
        k.bank_rr = (k.bank_rr + 1) % 8
    fns = []
    for g in range(2):
        for h in range(2):
            fns.append(lambda e, g=g, h=h: e.matmul(
                k.ps[banks[g * 2 + h]][:], lhsT=L.wg[:, g * 32 + d * 16 + n, :],
                rhs=L.xcb[:, t0 + h * 512:t0 + (h + 1) * 512], start=True, stop=True))
    tr.pe_group([L.wg_b, L.xcb_b], [k.psb[b] for b in banks], fns)
    for h in range(2):
        tr.op("act", [k.psb[banks[h]], L.gp_b], [L.A_b, L.sr_b],
              lambda e: e.activation(out=L.A[:, h * 512:(h + 1) * 512], in_=k.ps[banks[h]][:], func=AF.Sigmoid,
                                     bias=L.gp[:, n, d:d + 1], scale=1.0, accum_out=L.sr[:, h:h + 1]))
    for h in range(2):
        tr.op("act", [k.psb[banks[2 + h]], L.gp_b], [L.B_b],
              lambda e: e.activation(out=L.B[:, h * 512:(h + 1) * 512], in_=k.ps[banks[2 + h]][:], func=AF.Sigmoid,
                                     bias=L.gp[:, n, 2 + d:3 + d], scale=1.0))
    tr.op("act", [L.A_b, L.cs_b], [L.A_b],
          lambda e: e.activation(out=L.A[:], in_=L.A[:], func=AF.Exp, scale=L.cs[:, n, d:d + 1]))
    tr.op("pool", [L.A_b], [L.C_b], lambda e: e.tensor_tensor(out=L.C[:], in0=L.A[:], in1=L.A[:], op=ALU.mult))
    tr.op("act", [L.C_b, k.b_eps], [L.C_b],
          lambda e: e.activation(out=L.C[:], in_=L.C[:], func=AF.Sqrt, scale=-1.0, bias=k.eps_t[:, 1:2]))
    if first_fix is not None:
        first_fix(L)
    tr.op("dve", [L.B_b, L.xc_b], [L.B_b],
          lambda e: e.tensor_tensor(out=L.B[:], in0=L.B[:], in1=L.xc[:, t0:t0 + 1024], op=ALU.mult))
    tr.op("pool", [L.B_b, L.C_b], [L.B_b], lambda e: e.tensor_tensor(out=L.B[:], in0=L.B[:], in1=L.C[:], op=ALU.mult))
    o = out_tile[:, out_col0:out_col0 + 1024]
    if d == 0:
        tr.op("dve", [L.A_b, L.B_b] + init_bufs, [out_b],
              lambda e: e.tensor_tensor_scan(out=o, data0=L.A[:], data1=L.B[:], initial=init_ap, op0=ALU.mult, op1=ALU.add))
    else:
        tr.op("dve", [L.A_b, L.B_b] + init_bufs, [out_b],
              lambda e: e.tensor_tensor_scan(out=o[:, ::-1], data0=L.A[:, ::-1], data1=L.B[:, ::-1], initial=init_ap,
                                             op0=ALU.mult, op1=ALU.add))


def phase_lru_summ(k):
    tr = k.tr
    k.sumA = k.sb_glob("sumA", [128, 2, 24, 16], F32); k.sumA_b = Buf("sumA")
    k.sumH = k.sb_glob("sumH", [128, 2, 24, 16], F32); k.sumH_b = Buf("sumH")
    with phase_scope(k):
        sb = k.sb
        rt = RmsT(k, "ls", 0, nx=1)
        xnT = sb("ls_xnT", [128, KC, 1028], BF16); xnT_b = Buf("ls_xnT")
        slabs = Slabs(k, "ls_slab", KC, 128, n=2)
        xps = Stage(k, "ls_xp", [128, 1028], F32, n=2)
        L = LruCommon(k, "ls", 1024, nb=2)
        slots = [("xp", s, 16) for s in range(16)] + [("xs", s, 8) for s in range(8)]
        import os
        nslots = int(os.environ.get("LS_SLOTS", 24))
        for si, (nm, s, ns) in enumerate(slots[:nslots]):
            X = k.din(nm)
            base = s * 1024
            for j in range(8):
                rms_tile(k, rt, X[base + j * 128:base + (j + 1) * 128, :], [k.d_in], xnT, xnT_b, j * 128)
            srcs = []
            if s > 0:
                srcs.append((0, 2, X[base - 2:base, :]))
            if s < ns - 1:
                srcs.append((2, 1, X[base + 1024:base + 1025, :]))
            rms_tile(k, rt, None, [k.d_in], xnT, xnT_b, 1024, ncols=3, zero_first=True, srcs=srcs)
            for n in range(16):
                slab, slab_b = load_slab(k, slabs, "w_xr", 0, n)
                hc = 0
                banks = []
                for _ in range(3):
                    banks.append(k.bank_rr)
                    k.bank_rr = (k.bank_rr + 1) % 8
                fns = []
                for kc in range(KC):
                    for h in range(2):
                        fns.append(lambda e, kc=kc, h=h: e.matmul(
                            k.ps[banks[h]][:], lhsT=slab[:, kc, hc:hc + 128], rhs=xnT[:, kc, h * 512:(h + 1) * 512],
                            start=(kc == 0), stop=(kc == KC - 1)))
                    fns.append(lambda e, kc=kc: e.matmul(
                        k.ps[banks[2]][:, 0:3], lhsT=slab[:, kc, hc:hc + 128], rhs=xnT[:, kc, 1024:1027],
                        start=(kc == 0), stop=(kc == KC - 1)))
                tr.pe_group([xnT_b, slab_b], [k.psb[b] for b in banks], fns)
                xp, xp_b = xps.next()
                copy_op(k, "act", xp[:, 2:514], k.ps[banks[0]][:], [k.psb[banks[0]]], [xp_b])
                copy_op(k, "dve", xp[:, 514:1026], k.ps[banks[1]][:], [k.psb[banks[1]]], [xp_b])
                copy_op(k, "dve", xp[:, 0:2], k.ps[banks[2]][:, 0:2], [k.psb[banks[2]]], [xp_b])
                copy_op(k, "dve", xp[:, 1026:1027], k.ps[banks[2]][:, 2:3], [k.psb[banks[2]]], [xp_b])
                L.rotate_conv()
                lru_conv(k, L, xp, xp_b, n, 1024)
                for d in range(2):
                    L.rotate_piece()
                    fix = None
                    if (d == 0 and s == 0) or (d == 1 and s == ns - 1):
                        col = 0 if d == 0 else 1023
                        fix = lambda L, col=col: tr.op("dve", [], [L.C_b], lambda e: e.memset(L.C[:, col:col + 1], 1.0))
                    lru_piece(k, L, n, d, 0, fix, 0.0, [], L.C, L.C_b)
                    hcol = 1023 if d == 0 else 0
                    tr.op("dve", [L.C_b], [k.sumH_b],
                          lambda e: e.tensor_copy(out=k.sumH[:, d, si, n:n + 1], in_=L.C[:, hcol:hcol + 1]))
                    tr.op("dve", [L.sr_b], [L.sr_b],
                          lambda e: e.tensor_tensor(out=L.sr[:, 2:3], in0=L.sr[:, 0:1], in1=L.sr[:, 1:2], op=ALU.add))
                    tr.op("act", [L.sr_b, L.cs_b], [k.sumA_b],
                          lambda e: e.activation(out=k.sumA[:, d, si, n:n + 1], in_=L.sr[:, 2:3], func=AF.Exp,
                                                 scale=L.cs[:, n, d:d + 1]))


def phase_lru_own(k):
    tr = k.tr
    with phase_scope(k):
        sb = k.sb
        selt = sb("lo_sel", [128, 4, 17], F32); sel_b = Buf("lo_sel")
        tr.dma("sp", selt[:], k.din("sel")[:, :, :], [k.d_in], [sel_b], sel_b)
        car = sb("lo_car", [128, 4, 16], F32); car_b = Buf("lo_car")
        S = sb("lo_S", [128, 16], F32); S_b = Buf("lo_S")
        tr.op("dve", [], [car_b], lambda e: e.memset(car[:], 0.0))
        for (q, slot0, nb) in ((0, 0, 16), (1, 16, 8)):
            tr.op("dve", [], [S_b], lambda e: e.memset(S[:], 0.0))
            for j in range(nb):
                tr.op("dve", [S_b, k.sumA_b], [S_b], lambda e: e.tensor_tensor(out=S[:], in0=S[:], in1=k.sumA[:, 0, slot0 + j, :], op=ALU.mult))
                tr.op("dve", [S_b, k.sumH_b], [S_b], lambda e: e.tensor_tensor(out=S[:], in0=S[:], in1=k.sumH[:, 0, slot0 + j, :], op=ALU.add))
                tr.op("dve", [S_b, sel_b, car_b], [car_b],
                      lambda e: e.scalar_tensor_tensor(out=car[:, 2 * q, :], in0=S[:], scalar=selt[:, 2 * q, j + 1:j + 2],
                                                       in1=car[:, 2 * q, :], op0=ALU.mult, op1=ALU.add))
            tr.op("dve", [], [S_b], lambda e: e.memset(S[:], 0.0))
            for j in range(nb - 1, -1, -1):
                tr.op("dve", [S_b, k.sumA_b], [S_b], lambda e: e.tensor_tensor(out=S[:], in0=S[:], in1=k.sumA[:, 1, slot0 + j, :], op=ALU.mult))
                tr.op("dve", [S_b, k.sumH_b], [S_b], lambda e: e.tensor_tensor(out=S[:], in0=S[:], in1=k.sumH[:, 1, slot0 + j, :], op=ALU.add))
                tr.op("dve", [S_b, sel_b, car_b], [car_b],
                      lambda e: e.scalar_tensor_tensor(out=car[:, 2 * q + 1, :], in0=S[:], scalar=selt[:, 2 * q + 1, j:j + 1],
                                                       in1=car[:, 2 * q + 1, :], op0=ALU.mult, op1=ALU.add))
        L = LruCommon(k, "lo", 2048)
        xps = Stage(k, "lo_xp", [128, 2052], F32, n=2)
        grs = Stage(k, "lo_gr", [128, 2048], F32, n=2)
        HF = sb("lo_HF", [128, 2048], F32); HF_b = Buf("lo_HF")
        HR = sb("lo_HR", [128, 1024], F32); HR_b = Buf("lo_HR")
        G1 = sb("lo_G1", [128, 1024], F32); G1_b = Buf("lo_G1")
        G2 = sb("lo_G2", [128, 1024], F32); G2_b = Buf("lo_G2")
        outs = Stage(k, "lo_out", [128, 1024], BF16, n=2)
        tmp1 = sb("lo_t1", [128, 2], F32); tmp1_b = Buf("lo_t1")
        for n in range(16):
            for (q, e0, T, ob) in ((0, 256, OWN_P, 0), (1, EXT_P + 256, OWN_S, OWN_P)):
                xp, xp_b = xps.next()
                tr.dma("sp", xp[:, 0:T + 3], k.XR[n * 128:(n + 1) * 128, e0 - 2:e0 + T + 1], [k.XR_b], [xp_b], xp_b)
                gr, gr_b = grs.next()
                tr.dma("sp", gr[:, 0:T], k.GR[n * 128:(n + 1) * 128, e0:e0 + T], [k.GR_b], [gr_b], gr_b)
                lru_conv(k, L, xp, xp_b, n, T)
                npc = T // 1024

                def mkfix(col, fcol):
                    def fix(L):
                        tr.op("dve", [L.C_b], [tmp1_b],
                              lambda e: e.tensor_scalar(out=tmp1[:, 0:1], in0=L.C[:, col:col + 1], scalar1=-1.0, scalar2=1.0,
                                                        op0=ALU.mult, op1=ALU.add))
                        tr.op("dve", [tmp1_b, k.b_flags, L.C_b], [L.C_b],
                              lambda e: e.scalar_tensor_tensor(out=L.C[:, col:col + 1], in0=tmp1[:, 0:1],
                                                               scalar=k.flags_t[:, fcol:fcol + 1], in1=L.C[:, col:col + 1],
                                                               op0=ALU.mult, op1=ALU.add))
                    return fix
                for pc in range(npc):
                    init = car[:, 2 * q, n:n + 1] if pc == 0 else HF[:, pc * 1024 - 1:pc * 1024]
                    lru_piece(k, L, n, 0, pc * 1024, mkfix(0, 0) if pc == 0 else None, init,
                              [car_b] if pc == 0 else [HF_b], HF, HF_b, out_col0=pc * 1024)
                for pi, pc in enumerate(range(npc - 1, -1, -1)):
                    if pi == 0:
                        init, ib = car[:, 2 * q + 1, n:n + 1], [car_b]
                    else:
                        init, ib = tmp1[:, 1:2], [tmp1_b]
                    lru_piece(k, L, n, 1, pc * 1024, mkfix(1023, 2) if pi == 0 else None, init, ib, HR, HR_b)
                    if pi + 1 < npc:
                        tr.op("dve", [HR_b], [tmp1_b], lambda e: e.tensor_copy(out=tmp1[:, 1:2], in_=HR[:, 0:1]))
                    x = gr[:, pc * 1024:(pc + 1) * 1024]
                    tr.op("pool", [HR_b, HF_b], [HR_b],
                          lambda e: e.tensor_tensor(out=HR[:], in0=HR[:], in1=HF[:, pc * 1024:(pc + 1) * 1024], op=ALU.add))
                    tr.op("pool", [gr_b], [G1_b], lambda e: e.tensor_tensor(out=G1[:], in0=x, in1=x, op=ALU.mult))
                    tr.op("dve", [G1_b], [G1_b],
                          lambda e: e.tensor_scalar(out=G1[:], in0=G1[:], scalar1=0.044715, scalar2=1.0, op0=ALU.mult, op1=ALU.add))
                    tr.op("dve", [G1_b, gr_b], [G1_b], lambda e: e.tensor_tensor(out=G1[:], in0=G1[:], in1=x, op=ALU.mult))
                    tr.op("act", [G1_b], [G2_b], lambda e: e.activation(out=G2[:], in_=G1[:], func=AF.Tanh, scale=0.7978845608028654))
                    tr.op("dve", [G2_b, gr_b], [G2_b],
                          lambda e: e.scalar_tensor_tensor(out=G2[:], in0=G2[:], scalar=1.0, in1=x, op0=ALU.add, op1=ALU.mult))
                    o, o_b = outs.next()
                    tr.op("dve", [HR_b, G2_b], [o_b],
                          lambda e: e.scalar_tensor_tensor(out=o[:], in0=HR[:], scalar=0.5, in1=G2[:], op0=ALU.mult, op1=ALU.mult))
                    tr.dma("sp", k.LR[n * 128:(n + 1) * 128, ob + pc * 1024:ob + (pc + 1) * 1024], o[:], [o_b], [k.LR_b], o_b)


def attn_jobs():
    tab_u = {-6: 0, -4: 1, -2: 2, 0: 3, 2: 4, 4: 5, 6: 6}
    tab_g = {-4: 7, -2: 2, 0: 3, 2: 4, 4: 8}
    segs = []
    for (pb, n, ob) in ((0, 32, 0), (EXT_P // 128, 16, OWN_P)):
        npq = n // 2
        jobs = []
        for qi in range(npq):
            b = 2 + qi
            top, bot = qi < 2, qi >= npq - 2
            gflag = 1 if top else (3 if bot else 4)
            tiles = [(b + d2, tab_g[2 * d2], gflag) for d2 in (-2, -1, 0, 1, 2)]
            if top:
                tiles += [(a, tab_u[2 * (a - b)], 0) for a in (2, 3, 4, 5)]
            if bot:
                tiles += [(a, tab_u[2 * (a - b)], 2) for a in range(npq - 2, npq + 2)]
            jobs.append((pb, b, ob + qi * 128, tiles))
        segs.append(jobs)
    return segs[0] + segs[1]


def phase_attn(k):
    tr = k.tr
    scale = 128.0 ** -0.5
    HG = 4
    with phase_scope(k):
        sb = k.sb
        q_t = sb("at_q", [128, HG, EXT], BF16); q_b = Buf("at_q")
        k_t = sb("at_k", [128, HG, EXT], BF16); k_b = Buf("at_k")
        v_t = sb("at_v", [128, EXT // 128, HG * 128], BF16); v_b = Buf("at_v")
        tt = sb("at_tt", [128, HG, 9, 128], F32); tt_b = Buf("at_tt")
        ones = sb("at_ones", [128, 128], BF16); ones_b = Buf("at_ones")
        tr.op("dve", [], [ones_b], lambda e: e.memset(ones[:], 1.0))
        exs = Stage(k, "at_ex", [128, 128], F32, n=6)
        ets = Stage(k, "at_et", [128, 128], BF16, n=12)
        import os as _os
        recs = Stage(k, "at_rec", [128, 128], F32, n=int(_os.environ.get("AT_RECS", 2)))
        nas = Stage(k, "at_na", [128, OWN], BF16, n=2)
        s_slots = [(bank, j) for bank in range(4) for j in range(4)]
        s_bufs = [Buf(f"at_s{i}") for i in range(16)]
        s_rr = [0]
        import os
        jobs = attn_jobs()[int(os.environ.get('AT_JOB0', 0)):][:int(os.environ.get('AT_JOBS', 1000))]
        at_mode = int(os.environ.get('AT_MODE', 2))
        for hg in range(int(os.environ.get('AT_HG', NH // HG))):
            for hl in range(HG):
                tr.dma("sp", q_t[:, hl, :], k.QT[(hg * HG + hl) * 128:(hg * HG + hl + 1) * 128, :], [k.QT_b], [q_b], q_b)
                tr.dma("sp", k_t[:, hl, :], k.KT[(hg * HG + hl) * 128:(hg * HG + hl + 1) * 128, :], [k.KT_b], [k_b], k_b)
            for p0 in range(0, EXT // 128, 4):
                tr.dma("sp", v_t[:, p0:p0 + 4, :], k.VS.rearrange("(pr p) c -> p pr c", p=128)[:, p0:p0 + 4, hg * 512:(hg + 1) * 512],
                       [k.VS_b], [v_b], v_b)
            for hl in range(HG):
                tr.dma("sp", tt[:, hl, :, :], k.din("tt")[hg * HG + hl, :, :, :], [k.d_in], [tt_b], tt_b)
            tr.op("act", [tt_b], [tt_b], lambda e: e.activation(out=tt[:], in_=tt[:], func=AF.Exp))
            for hl in range(int(os.environ.get("AT_HL", HG)) if at_mode > 0 else 0):
                h = hg * HG + hl
                na, na_b = nas.next()

                flat = []
                for ji, job in enumerate(jobs):
                    pb, b, o, tiles = job
                    for i, (a, ti, fl) in enumerate(tiles):
                        flat.append((ji, pb, b, o, a, ti, fl, i, len(tiles)))
                LOOK = 3
                for idx in range(len(flat) + LOOK):
                    if idx < len(flat):
                        (ji, pb, b, o, a, ti, fl, i, nt) = flat[idx]
                        sbank = idx % 4
                        tr.pe_group([k_b, q_b], [k.psb[sbank]], [lambda e: e.matmul(
                            k.ps[sbank][:, 0:128], lhsT=k_t[:, hl, (pb + a) * 128:(pb + a + 1) * 128],
                            rhs=q_t[:, hl, (pb + b) * 128:(pb + b + 1) * 128], start=True, stop=True)])
                    j = idx - LOOK
                    if j < 0:
                        continue
                    (ji, pb, b, o, a, ti, fl, i, nt) = flat[j]
                    sbank = j % 4
                    ob, db = 4 + ji % 2, 6 + ji % 2
                    ex, ex_b = exs.next()
                    tr.op("act", [k.psb[sbank]], [ex_b],
                          lambda e: e.activation(out=ex[:], in_=k.ps[sbank][:, 0:128], func=AF.Exp, scale=scale))
                    et, et_b = ets.next()
                    tr.op("dve", [ex_b, tt_b, k.b_flags], [et_b],
                          lambda e: e.scalar_tensor_tensor(out=et[:], in0=ex[:], scalar=k.flags_t[:, fl:fl + 1],
                                                           in1=tt[:, hl, ti, :], op0=ALU.mult, op1=ALU.mult))
                    tr.pe_group([v_b, ones_b, et_b], [k.psb[ob], k.psb[db]], [
                        lambda e: e.matmul(k.ps[ob][:, 0:128], lhsT=v_t[:, pb + a, hl * 128:(hl + 1) * 128], rhs=et[:],
                                           start=(i == 0), stop=(i == nt - 1)),
                        lambda e: e.matmul(k.ps[db][:, 0:128], lhsT=ones[:], rhs=et[:], start=(i == 0), stop=(i == nt - 1))])
                    if i == nt - 1:
                        rc, rc_b = recs.next()
                        tr.op("dve", [k.psb[db]], [rc_b], lambda e: e.reciprocal(out=rc[:], in_=k.ps[db][:, 0:128]))
                        tr.op("dve", [k.psb[ob], rc_b], [na_b],
                              lambda e: e.tensor_tensor(out=na[:, o:o + 128], in0=k.ps[ob][:, 0:128], in1=rc[:], op=ALU.mult))
                tr.dma("sp", k.NA[h * 128:(h + 1) * 128, :], na[:], [na_b], [k.NA_b], na_b)


def evac_sq(k, bank, gc, stg, junk, junk_b, ssq, ssq_b, col):
    tr = k.tr
    t, b = stg.next()
    import os
    if os.environ.get("WO_EVAC", "1") == "0":
        tr.op("act", [k.psb[bank]], [junk_b, ssq_b],
              lambda e: e.activation(out=junk[:, 0:gc], in_=k.ps[bank][:, 0:gc], func=AF.Square, accum_out=ssq[:, col:col + 1]))
        tr.op("dve", [k.psb[bank]], [b], lambda e: e.tensor_copy(out=t[:, 0:gc], in_=k.ps[bank][:, 0:gc]))
    else:
        tr.op("dve", [k.psb[bank]], [b], lambda e: e.tensor_copy(out=t[:, 0:gc], in_=k.ps[bank][:, 0:gc]))
        tr.op("act", [b], [junk_b, ssq_b],
              lambda e: e.activation(out=junk[:, 0:gc], in_=t[:, 0:gc], func=AF.Square, accum_out=ssq[:, col:col + 1]))
    return t, b


def phase_wout(k):
    tr = k.tr
    with phase_scope(k):
        sb = k.sb
        gdT = sb("wo_gdT", [128, KC, 1024], BF16); gdT_b = Buf("wo_gdT")
        import os
        WGC = 512
        slabs = Slabs(k, "wo_slab", KC, WGC, n=2)
        stg = Stage(k, "wo_stg", [128, 512], F32, n=4)
        junk = sb("wo_junk", [128, 512], BF16); junk_b = Buf("wo_junk")
        for blk in range(OWN // 1024):
            o0 = blk * 1024
            load_T(k, gdT, gdT_b, k.GD, k.GD_b, KC, o0, 1024)

            def ep(c0, gc, st, bank):
                col = (blk * 8 + st) * 16 + c0 // WGC
                t, b = evac_sq(k, bank, gc, stg, junk, junk_b, k.ssq1, k.ssq1_b, col)
                tr.dma("sp", k.MIX[o0 + st * 128:o0 + (st + 1) * 128, c0:c0 + gc], t[:, 0:gc], [b], [k.MIX_b], b)
            linear_T(k, gdT, gdT_b, KC, 1024, "w_out", 0, D, slabs, ep, gcols=WGC)


def norm_resid(k, pfx, src, src_b, ssq, ssq_b, g_row, res_fn, dst, dst_b):
    tr = k.tr
    with phase_scope(k):
        sb = k.sb
        gbc = sb(pfx + "_gbc", [128, D], F32); gbc_b = Buf(pfx + "_gbc")
        tr.dma("sp", gbc[:], k.din("g4")[g_row:g_row + 1, :].broadcast_to([128, D]), [k.d_in], [gbc_b], gbc_b)
        ft = [sb(f"{pfx}_f{i}", [128, D], F32) for i in range(2)]; ft_b = [Buf(f"{pfx}_f{i}") for i in range(2)]
        rs = [sb(f"{pfx}_r{i}", [128, D], F32) for i in range(2)]; rs_b = [Buf(f"{pfx}_r{i}") for i in range(2)]
        st = sb(pfx + "_st", [128, 24, 4], F32); st_b = Buf(pfx + "_st")
        for i in range(OWN // 128):
            p = i % 2
            tr.dma("sp", ft[p][:], src[i * 128:(i + 1) * 128, :], [src_b], [ft_b[p]], ft_b[p])
            r_ap, r_buf = res_fn(i)
            tr.dma("sp", rs[p][:], r_ap, [r_buf], [rs_b[p]], rs_b[p])
            tr.op("dve", [ssq_b], [st_b],
                  lambda e: e.tensor_reduce(out=st[:, i, 0:1], in_=ssq[:, i * 16:(i + 1) * 16], axis=mybir.AxisListType.X, op=ALU.add))
            tr.op("act", [st_b, k.b_eps], [st_b],
                  lambda e: e.activation(out=st[:, i, 1:2], in_=st[:, i, 0:1], func=AF.Sqrt, scale=1.0 / D, bias=k.eps_t[:, 0:1]))
            tr.op("dve", [st_b], [st_b], lambda e: e.reciprocal(out=st[:, i, 2:3], in_=st[:, i, 1:2]))
            tr.op("dve", [ft_b[p], st_b, gbc_b], [ft_b[p]],
                  lambda e: e.scalar_tensor_tensor(out=ft[p][:], in0=ft[p][:], scalar=st[:, i, 2:3], in1=gbc[:],
                                                   op0=ALU.mult, op1=ALU.mult))
            tr.op("pool", [ft_b[p], rs_b[p]], [ft_b[p]],
                  lambda e: e.tensor_tensor(out=ft[p][:], in0=ft[p][:], in1=rs[p][:], op=ALU.add))
            tr.dma("sp", dst[i * 128:(i + 1) * 128, :], ft[p][:], [ft_b[p]], [dst_b], ft_b[p])


def phase_ffn_up(k):
    tr = k.tr
    with phase_scope(k):
        sb = k.sb
        rt = RmsT(k, "fu", 2, nx=1)
        xnT = sb("fu_xnT", [128, KC, 1024], BF16); xnT_b = Buf("fu_xnT")
        slabs = Slabs(k, "fu_slab", KC, 256, n=3)
        sgs = Stage(k, "fu_sg", [128, 512], F32, n=3)
        outs = Stage(k, "fu_out", [128, 512], BF16, n=3)
        for blk in range(OWN // 1024):
            o0 = blk * 1024
            for j in range(8):
                rms_tile(k, rt, k.X1[o0 + j * 128:o0 + (j + 1) * 128, :], [k.X1_b], xnT, xnT_b, j * 128)
            for c0 in range(0, DFF, 256):
                s_g, s_g_b = load_slab(k, slabs, "w_fg", 0, c0 // 256)
                s_u, s_u_b = load_slab(k, slabs, "w_fu", 0, c0 // 256)
                for h in range(2):
                    banks = []
                    for _ in range(4):
                        banks.append(k.bank_rr)
                        k.bank_rr = (k.bank_rr + 1) % 8
                    fns = []
                    for (sl, off) in ((s_g, 0), (s_u, 2)):
                        for kc in range(KC):
                            for fc in range(2):
                                fns.append(lambda e, kc=kc, fc=fc, sl=sl, off=off: e.matmul(
                                    k.ps[banks[off + fc]][:], lhsT=sl[:, kc, fc * 128:(fc + 1) * 128],
                                    rhs=xnT[:, kc, h * 512:(h + 1) * 512], start=(kc == 0), stop=(kc == KC - 1)))
                    tr.pe_group([xnT_b, s_g_b, s_u_b], [k.psb[b] for b in banks], fns)
                    for fc in range(2):
                        f = c0 // 128 + fc
                        sg, sg_b = sgs.next()
                        tr.op("act", [k.psb[banks[fc]]], [sg_b],
                              lambda e: e.activation(out=sg[:], in_=k.ps[banks[fc]][:], func=AF.Silu))
                        o, o_b = outs.next()
                        tr.op("dve", [k.psb[banks[2 + fc]], sg_b], [o_b],
                              lambda e: e.tensor_tensor(out=o[:], in0=k.ps[banks[2 + fc]][:], in1=sg[:], op=ALU.mult))
                        tr.dma("sp", k.HH[f * 128:(f + 1) * 128, o0 + h * 512:o0 + (h + 1) * 512], o[:], [o_b], [k.HH_b], o_b)


def phase_ffn_down(k):
    tr = k.tr
    NK = DFF // 128
    with phase_scope(k):
        sb = k.sb
        hT = sb("fd_hT", [128, NK, 512], BF16); hT_b = Buf("fd_hT")
        slabs = Slabs(k, "fd_slab", 8, 512, n=3)
        stg = Stage(k, "fd_stg", [128, 512], F32, n=4)
        junk = sb("fd_junk", [128, 512], BF16); junk_b = Buf("fd_junk")
        for grp in range(OWN // 512):
            o0 = grp * 512
            for q in range(0, NK, 22):
                n = min(22, NK - q)
                k.tr.dma("sp", hT[:, q:q + n, :], k.HH.rearrange("(kc p) t -> p kc t", p=128)[:, q:q + n, o0:o0 + 512],
                         [k.HH_b], [hT_b], hT_b)
            for c0 in range(0, D, 512):
                banks = []
                for _ in range(4):
                    banks.append(k.bank_rr)
                    k.bank_rr = (k.bank_rr + 1) % 8
                for kp in range(0, NK, 8):
                    nk = min(8, NK - kp)
                    sl, sl_b = load_slab(k, slabs, "w_fd", kp // 8, c0 // 512, nkc=nk)
                    fns = []
                    for j in range(nk):
                        for st in range(4):
                            fns.append(lambda e, j=j, st=st: e.matmul(
                                k.ps[banks[st]][:], lhsT=hT[:, kp + j, st * 128:(st + 1) * 128], rhs=sl[:, j, :],
                                start=(kp + j == 0), stop=(kp + j == NK - 1)))
                    tr.pe_group([hT_b, sl_b], [k.psb[b] for b in banks], fns)
                for st in range(4):
                    col = (grp * 4 + st) * 16 + c0 // 512
                    t, b = evac_sq(k, banks[st], 512, stg, junk, junk_b, k.ssq2, k.ssq2_b, col)
                    tr.dma("sp", k.FF[o0 + st * 128:o0 + (st + 1) * 128, c0:c0 + 512], t[:], [b], [k.FF_b], b)


def make_inputs(inp):
    f = lambda a: np.ascontiguousarray(np.asarray(a, dtype=np.float32))
    xp = f(inp["x_prompt"])[0]
    xs = f(inp["x_sample"])[0]
    shared = {
        "xp": xp, "xs": xs,
        "w_in": f(inp["w_in"])[0], "w_merge": f(inp["w_merge"])[0], "w_na": f(inp["w_na_out"])[0],
        "w_lru": f(inp["w_lru_out"])[0], "w_out": f(inp["w_out"])[0], "w_fg": f(inp["w_ffn_gate"])[0],
        "w_fu": f(inp["w_ffn_up"])[0], "w_fd": f(inp["w_ffn_down"])[0],
        "g4": f(np.stack([inp["g_mix_pre"][0], inp["g_mix_post"][0], inp["g_ffn_pre"][0], inp["g_ffn_post"][0]])),
        "b_merge": f(np.asarray(inp["b_merge"])[0].reshape(64, 128).T),
        "ident": np.eye(128, dtype=np.float32),
    }
    wc = np.asarray(inp["w_conv"])[0]
    bc = np.asarray(inp["b_conv"])[0]
    convp = np.concatenate([wc, bc[None]], 0)
    shared["convp"] = f(convp.reshape(5, 16, 128).transpose(2, 1, 0))
    wr = np.asarray(inp["w_rgate"])[0]
    wi = np.asarray(inp["w_igate"])[0]
    wg = np.stack([wr, wi], 0)
    shared["wgate"] = f(wg.transpose(3, 0, 1, 2, 4).reshape(128, 64, 128))
    gp = np.stack([np.asarray(inp["b_rgate"])[0][0], np.asarray(inp["b_rgate"])[0][1],
                   np.asarray(inp["b_igate"])[0][0], np.asarray(inp["b_igate"])[0][1],
                   np.asarray(inp["lru_lambda"])[0][0], np.asarray(inp["lru_lambda"])[0][1]], 0)
    shared["gatep"] = f(gp.reshape(6, 16, 128).transpose(2, 1, 0))
    shared["tt"] = make_tt(np.asarray(inp["rpb"], dtype=np.float32)[0])
    maps = []
    for c in range(NCORES):
        m = dict(shared)
        xo = np.zeros((EXT, D), np.float32)
        for (src, L, own, e0) in ((xp, LP, OWN_P, 0), (xs, LS, OWN_S, EXT_P)):
            lo, hi = c * own - 256, (c + 1) * own + 256
            slo, shi = max(lo, 0), min(hi, L)
            xo[e0 + (slo - lo):e0 + (shi - lo)] = src[slo:shi]
        m["xo"] = xo
        fl = np.zeros((128, 8), np.float32)
        ftop, fbot = float(c == 0), float(c == NCORES - 1)
        fl[:, 0], fl[:, 1], fl[:, 2], fl[:, 3], fl[:, 4] = ftop, 1 - ftop, fbot, 1 - fbot, 1.0
        m["flags"] = fl
        sel = np.zeros((128, 4, 17), np.float32)
        sel[:, 0, 2 * c] = 1.0
        sel[:, 1, 2 * c + 2] = 1.0
        sel[:, 2, c] = 1.0
        sel[:, 3, c + 1] = 1.0
        m["sel"] = sel
        maps.append(m)
    return maps


TT_D = [(-6, False), (-4, False), (-2, False), (0, False), (2, False), (4, False), (6, False), (-4, True), (4, True)]


def make_tt(rpb):
    kc = np.arange(64)[:, None]
    qc = np.arange(64)[None, :]
    cs = np.clip(qc - 8, 0, 48)
    colok = (kc >= cs) & (kc < cs + 16)
    cidx = np.clip(kc - qc + 15, 0, 30)
    out = np.full((NH, 128, 9, 128), NEG, np.float32)
    for ti, (d, generic) in enumerate(TT_D):
        for kr in range(2):
            for qr in range(2):
                dr = d + kr - qr
                if dr < -7 or dr > 7:
                    continue
                if generic and not (-4 <= dr <= 3):
                    continue
                blk = np.where(colok[None], rpb[:, dr + 7][:, cidx], NEG)
                out[:, kr * 64:(kr + 1) * 64, ti, qr * 64:(qr + 1) * 64] = blk
    return out


ALL_PHASES = ("proj", "lru1", "lru2", "attn", "gate", "wout", "ffn")
_NC_CACHE = {}


def kernel(**inputs):
    maps = make_inputs(inputs)
    if "nc" not in _NC_CACHE:
        _NC_CACHE["nc"] = build(phases=ALL_PHASES)
    nc, used = _NC_CACHE["nc"]
    maps = [{n: m[n] for n in used} for m in maps]
    res = run_bass_kernel_spmd(nc, maps, core_ids=list(range(NCORES)))
    yp = np.zeros((1, LP, D), np.float32)
    ys = np.zeros((1, LS, D), np.float32)
    for c in range(NCORES):
        y = res.results[c]["y"]
        yp[0, c * OWN_P:(c + 1) * OWN_P] = y[:OWN_P]
        ys[0, c * OWN_S:(c + 1) * OWN_S] = y[OWN_P:]
    return yp, ys
```

```python
import contextlib
import numpy as np
import concourse.bass as bass
import concourse.mybir as mybir
from concourse.bass_utils import run_bass_kernel_spmd

F32 = mybir.dt.float32
BF16 = mybir.dt.bfloat16
AF = mybir.ActivationFunctionType
ALU = mybir.AluOpType

NCORES = 8
D = 4096
KC = D // 128
LP, LS = 16384, 8192
OWN_P, OWN_S = LP // NCORES, LS // NCORES
OWN = OWN_P + OWN_S
EXT_P, EXT_S = OWN_P + 512, OWN_S + 512
EXT = EXT_P + EXT_S
NAW = 2048
LRW = 2048
DFF = 11008
NH = 16
EPS = 1e-6
SEM_LIMIT = 20000
NEG = -30000.0


class Buf:
    ALL = []

    def __init__(self, name, dram=False):
        if not dram:
            Buf.ALL.append(self)
        self.name = name
        self.dram = dram
        self.w = {}
        self.r = {}
        self.dsem = None
        self.dcount = 0


def _merge(d, ev):
    if ev is None:
        return
    k = id(ev[0])
    if k not in d or d[k][1] < ev[1]:
        d[k] = ev


class TR:
    def __init__(self, nc, stack):
        self.nc = nc
        self.stack = stack
        self.engs = {"pe": nc.tensor, "act": nc.scalar, "dve": nc.vector, "pool": nc.gpsimd, "sp": nc.sync}
        self.cur = {}
        self.seen = {e: {} for e in self.engs}
        self.nsem = 0
        self.keep = []
        self.last_swdge = None

    def new_sem(self, name):
        self.nsem += 1
        s = self.stack.enter_context(self.nc.semaphore(f"{name}_{self.nsem}"))
        self.keep.append(s)
        return s

    def wait(self, e, ev):
        if ev is None:
            return
        s, v = ev
        d = self.seen[e]
        if d.get(id(s), 0) >= v:
            return
        self.engs[e].wait_ge(s, v)
        d[id(s)] = v

    def wait_all(self, e, evs):
        for ev in list(evs.values()):
            self.wait(e, ev)

    def inc(self, e, ins):
        st = self.cur.get(e)
        if st is None or st[1] >= SEM_LIMIT:
            st = [self.new_sem("c" + e), 0]
            self.cur[e] = st
        st[1] += 1
        ins.then_inc(st[0], 1)
        return (st[0], st[1])

    def deps(self, e, reads, writes):
        for b in reads:
            self.wait_all(e, b.w)
        for b in writes:
            if not b.dram:
                self.wait_all(e, b.w)
                self.wait_all(e, b.r)

    def done(self, ev, reads, writes):
        for b in reads:
            if not b.dram:
                _merge(b.r, ev)
        for b in writes:
            if b.dram:
                _merge(b.w, ev)
            else:
                b.w = {id(ev[0]): ev}
                b.r = {}

    def op(self, e, reads, writes, fn):
        self.deps(e, reads, writes)
        ins = fn(self.engs[e])
        ev = self.inc(e, ins)
        self.done(ev, reads, writes)
        return ev

    def pe_group(self, reads, writes, fns):
        self.deps("pe", reads, writes)
        ins = None
        for fn in fns:
            ins = fn(self.engs["pe"])
        ev = self.inc("pe", ins)
        self.done(ev, reads, writes)
        return ev

    def dma(self, q, out_ap, in_ap, reads, writes, slot):
        self.deps(q, reads, writes)
        if q == "pool" and self.last_swdge is not None:
            self.wait("pool", self.last_swdge)
        if slot.dsem is None:
            slot.dsem = self.new_sem("d")
        slot.dcount += 16
        self.engs[q].dma_start(out=out_ap, in_=in_ap).then_inc(slot.dsem, 16)
        ev = (slot.dsem, slot.dcount)
        if q == "pool":
            self.last_swdge = ev
        self.done(ev, reads, writes)
        return ev


class K:
    pass


def barrier(k):
    tr = k.tr
    evs = {}
    for b in Buf.ALL:
        for ev in list(b.w.values()) + list(b.r.values()):
            _merge(evs, ev)
    for e in tr.engs:
        tr.wait_all(e, evs)
    Buf.ALL[:] = list(k.psb) + [k.b_ident, k.b_flags, k.b_eps] + [b for b in (getattr(k, 'ssq1_b', None), getattr(k, 'ssq2_b', None), getattr(k, 'sumA_b', None), getattr(k, 'sumH_b', None)) if b]


def build(debug_outs=(), phases=("proj",)):
    Buf.ALL[:] = []
    nc = bass.Bass("TRN2", target_bir_lowering=False)
    k = K()
    k.nc = nc
    dt_in = lambda name, shape, dt=F32: nc.dram_tensor(name, list(shape), dt, kind="ExternalInput").ap()

    def scratch(name, shape, dt):
        kind = "ExternalOutput" if name in debug_outs else "Internal"
        return nc.dram_tensor(name, list(shape), dt, kind=kind).ap()

    shapes = {
        "xp": [LP, D], "xs": [LS, D], "xo": [EXT, D], "w_in": [D, 10240], "w_merge": [D, 2 * D],
        "w_na": [NAW, D], "w_lru": [LRW, D], "w_out": [D, D], "w_fg": [D, DFF], "w_fu": [D, DFF],
        "w_fd": [DFF, D], "g4": [4, D], "b_merge": [128, 64], "convp": [128, 16, 5],
        "wgate": [128, 64, 128], "gatep": [128, 16, 6], "tt": [NH, 128, 9, 128], "flags": [128, 8],
        "sel": [128, 4, 17], "ident": [128, 128],
    }
    k.used = {}

    def din(name):
        if name not in k.used:
            k.used[name] = dt_in(name, shapes[name])
        return k.used[name]
    k.din = din
    k.y = nc.dram_tensor("y", [OWN, D], F32, kind="ExternalOutput").ap()

    k.QT = scratch("QT", [NAW, EXT], BF16)
    k.KT = scratch("KT", [NAW, EXT], BF16)
    k.VS = scratch("VS", [EXT, NAW], BF16)
    k.XR = scratch("XR", [LRW, EXT], F32)
    k.GR = scratch("GR", [LRW, EXT], F32)
    k.GT = scratch("GT", [2 * D, EXT], BF16)
    k.NA = scratch("NA", [NAW, OWN], BF16)
    k.LR = scratch("LR", [LRW, OWN], BF16)
    k.GD = scratch("GD", [D, OWN], BF16)
    k.MIX = scratch("MIX", [OWN, D], F32)
    k.X1 = scratch("X1", [OWN, D], F32)
    k.XN2 = scratch("XN2", [D, OWN], BF16)
    k.HH = scratch("HH", [DFF, OWN], BF16)
    k.FF = scratch("FF", [OWN, D], F32)

    with contextlib.ExitStack() as stack:
        tr = TR(nc, stack)
        k.tr = tr
        k.stack = stack
        sb = lambda name, shape, dt: stack.enter_context(nc.sbuf_tensor(name, list(shape), dt))
        k.sb = sb
        k.sb_glob = sb
        k.ps = [stack.enter_context(nc.psum_tensor(f"ps{i}", [128, 512], F32)) for i in range(8)]
        k.psb = [Buf(f"ps{i}") for i in range(8)]
        k.bank_rr = 0
        k.ident_f = sb("ident_f", [128, 128], F32)
        k.ident = sb("ident_b", [128, 128], BF16)
        k.b_ident = Buf("ident")
        k.flags_t = sb("flags_sb", [128, 8], F32)
        k.b_flags = Buf("flags")
        k.d_in = Buf("inputs", dram=True)
        tr.dma("sp", k.ident_f[:], k.din("ident")[:, :], [k.d_in], [k.b_ident], k.b_ident)
        tr.op("dve", [k.b_ident], [k.b_ident], lambda e: e.tensor_copy(out=k.ident[:], in_=k.ident_f[:]))
        tr.dma("sp", k.flags_t[:], k.din("flags")[:, :], [k.d_in], [k.b_flags], k.b_flags)
        k.eps_t = sb("eps_t", [128, 2], F32)
        k.b_eps = Buf("eps")
        tr.op("dve", [], [k.b_eps], lambda e: e.memset(k.eps_t[:, 0:1], EPS))
        tr.op("dve", [k.b_eps], [k.b_eps], lambda e: e.memset(k.eps_t[:, 1:2], 1.0))

        k.dbufs = {}
        for nm in ("QT", "KT", "VS", "XR", "GR", "GT", "NA", "LR", "GD", "MIX", "X1", "XN2", "HH", "FF", "y"):
            k.dbufs[nm] = Buf(nm, dram=True)
            setattr(k, nm + "_b", k.dbufs[nm])
        k.ssq1 = sb("ssq1", [128, 384], F32); k.ssq1_b = Buf("ssq1")
        k.ssq2 = sb("ssq2", [128, 384], F32); k.ssq2_b = Buf("ssq2")
        tr.op("dve", [], [k.ssq1_b], lambda e: e.memset(k.ssq1[:], 0.0))
        tr.op("dve", [], [k.ssq2_b], lambda e: e.memset(k.ssq2[:], 0.0))
        need = []
        for ph, nms in (("proj", ("w_in", "w_merge")), ("lru1", ("w_xr",)), ("gate", ("w_na", "w_lru")),
                        ("wout", ("w_out",)), ("ffn", ("w_fg", "w_fu", "w_fd"))):
            if ph in phases:
                need += [n for n in nms if n not in need]
        phase_cast(k, need)
        barrier(k)
        if "proj" in phases:
            phase_proj(k)
            barrier(k)
        if "lru1" in phases:
            phase_lru_summ(k)
            barrier(k)
        if "lru2" in phases:
            phase_lru_own(k)
            barrier(k)
        if "attn" in phases:
            phase_attn(k)
            barrier(k)
        if "gate" in phases:
            phase_gate(k)
            barrier(k)
        if "wout" in phases:
            phase_wout(k)
            barrier(k)
            import os
            if not os.environ.get("WO_SKIP_NORM"):
              norm_resid(k, "n1", k.MIX, k.MIX_b, k.ssq1, k.ssq1_b, 1,
                       lambda i: (k.din("xo")[own_ext(i * 128):own_ext(i * 128) + 128, :], k.d_in), k.X1, k.X1_b)
            barrier(k)
        if "ffn" in phases:
            phase_ffn_up(k)
            barrier(k)
            phase_ffn_down(k)
            barrier(k)
            norm_resid(k, "n2", k.FF, k.FF_b, k.ssq2, k.ssq2_b, 3,
                       lambda i: (k.X1[i * 128:(i + 1) * 128, :], k.X1_b), k.y, k.y_b)
            barrier(k)

        for b in k.dbufs.values():
            tr.wait_all("sp", b.w)
    return nc, sorted(k.used)


_evac_rr = [0]


def evac_engine():
    _evac_rr[0] ^= 1
    return "act" if _evac_rr[0] else "dve"


def copy_op(k, e, out_ap, in_ap, reads, writes):
    if e == "act":
        return k.tr.op("act", reads, writes, lambda g: g.activation(out=out_ap, in_=in_ap, func=AF.Copy))
    return k.tr.op(e, reads, writes, lambda g: g.tensor_copy(out=out_ap, in_=in_ap))


class RmsT:
    def __init__(self, k, pfx, g_row, nx=2):
        sb, tr = k.sb, k.tr
        self.nx = nx
        self.xt = [sb(f"{pfx}_x{i}", [128, D], F32) for i in range(nx)]
        self.xt_b = [Buf(f"{pfx}_x{i}") for i in range(nx)]
        self.xn = sb(f"{pfx}_xn", [128, D], BF16)
        self.xn_b = Buf(f"{pfx}_xn")
        self.gbc = sb(f"{pfx}_gbc", [128, D], F32)
        self.gbc_b = Buf(f"{pfx}_gbc")
        self.st = sb(f"{pfx}_st", [128, 4], F32)
        self.st_b = Buf(f"{pfx}_st")
        self.i = 0
        tr.dma("sp", self.gbc[:], k.din("g4")[g_row:g_row + 1, :].broadcast_to([128, D]), [k.d_in], [self.gbc_b], self.gbc_b)


def rms_tile(k, rt, src_ap, src_bufs, dstT, dstT_b, col0, ncols=128, zero_first=False, srcs=None):
    tr = k.tr
    i = rt.i
    rt.i = (rt.i + 1) % rt.nx
    xt, xb = rt.xt[i], rt.xt_b[i]
    if zero_first:
        tr.op("dve", [], [xb], lambda e: e.memset(xt[:], 0.0))
    if src_ap is not None:
        tr.dma("sp", xt[:, :], src_ap, src_bufs, [xb], xb)
    for (r0, nr, ap) in (srcs or []):
        tr.dma("sp", xt[r0:r0 + nr, :], ap, src_bufs, [xb], xb)
    tr.op("act", [xb], [rt.xn_b, rt.st_b],
          lambda e: e.activation(out=rt.xn[:], in_=xt[:], func=AF.Square, accum_out=rt.st[:, 0:1]))
    tr.op("act", [rt.st_b, k.b_eps], [rt.st_b],
          lambda e: e.activation(out=rt.st[:, 1:2], in_=rt.st[:, 0:1], func=AF.Sqrt, scale=1.0 / D, bias=k.eps_t[:, 0:1]))
    tr.op("dve", [rt.st_b], [rt.st_b], lambda e: e.reciprocal(out=rt.st[:, 2:3], in_=rt.st[:, 1:2]))
    tr.op("dve", [xb, rt.st_b, rt.gbc_b], [rt.xn_b],
          lambda e: e.scalar_tensor_tensor(out=rt.xn[:], in0=xt[:], scalar=rt.st[:, 2:3], in1=rt.gbc[:],
                                           op0=ALU.mult, op1=ALU.mult))
    for g in range(4):
        bi = k.bank_rr
        k.bank_rr = (k.bank_rr + 1) % 8
        pb = k.ps[bi][:].bitcast(BF16)
        fns = []
        for j in range(8):
            kc = g * 8 + j
            fns.append(lambda e, kc=kc, j=j: e.transpose(out=pb[:, j * 128:(j + 1) * 128],
                                                         in_=rt.xn[:, kc * 128:(kc + 1) * 128],
                                                         identity=k.ident[:]))
        tr.pe_group([rt.xn_b, k.b_ident], [k.psb[bi]], fns)
        copy_op(k, evac_engine(), dstT[:, g * 8:(g + 1) * 8, col0:col0 + ncols],
                pb.rearrange("p (j t) -> p j t", j=8)[:, :, 0:ncols], [k.psb[bi]], [dstT_b])


class Slabs:
    def __init__(self, k, pfx, kc, cols, n=2):
        self.t = [k.sb(f"{pfx}{i}", [128, kc, cols], BF16) for i in range(n)]
        self.b = [Buf(f"{pfx}{i}") for i in range(n)]
        self.i = 0
        self.n = n

    def next(self):
        i = self.i
        self.i = (self.i + 1) % self.n
        return self.t[i], self.b[i]


def load_slab(k, slabs, name, kg, cg, nkc=None):
    t, b = slabs.next()
    ap, buf, wnkc, gc = k.wb[name]
    nkc = nkc or wnkc
    k.tr.dma("pool", t[:, 0:nkc, 0:gc], ap[kg, cg, :, 0:nkc, :], [buf], [b], b)
    return t, b


WB_SPECS = {
    "w_in": (D, 10240, 32, 256), "w_merge": (D, 2 * D, 32, 256), "w_na": (NAW, D, 16, 256), "w_lru": (LRW, D, 16, 256),
    "w_out": (D, D, 32, 512), "w_fg": (D, DFF, 32, 256), "w_fu": (D, DFF, 32, 256), "w_fd": (DFF, D, 8, 512),
    "w_xr": (D, LRW, 32, 128),
}
WB_SRC = {"w_xr": ("w_in", 6144)}


def phase_cast(k, names):
    tr = k.tr
    k.wb = {}
    with phase_scope(k):
        sb = k.sb
        fst = Stage(k, "cs_f", [128, 4096], F32, n=3)
        bst = Stage(k, "cs_b", [128, 4096], BF16, n=3)
        items = []
        for nm in names:
            K_, N_, nkc, gc = WB_SPECS[nm]
            tkc = K_ // 128
            n_kg = -(-tkc // nkc)
            n_cg = N_ // gc
            ap = k.nc.dram_tensor("wb_" + nm, [n_kg, n_cg, 128, nkc, gc], BF16, kind="Internal").ap()
            buf = Buf("wb_" + nm, dram=True)
            k.wb[nm] = (ap, buf, nkc, gc)
            k.dbufs["wb_" + nm] = buf
            src, sc0 = WB_SRC.get(nm, (nm, 0))
            wv = k.din(src).rearrange("(kc p) n -> p kc n", p=128)
            for kg in range(n_kg):
                nk = min(nkc, tkc - kg * nkc)
                for cg in range(n_cg):
                    for q in range(0, nk, 8):
                        items.append((wv, ap, buf, kg, cg, q, min(8, nk - q), nkc, gc, sc0))
        LA = 2
        pend = {}
        engs = ("dve", "act", "pool", "act", "dve")
        for i in range(len(items) + LA):
            if i < len(items):
                (wv, ap, buf, kg, cg, q, m, nkc, gc, sc0) = items[i]
                f, f_b = fst.next()
                fv = f[:].rearrange("p (a n) -> p a n", n=gc)
                tr.dma("sp", fv[:, 0:m, :], wv[:, kg * nkc + q:kg * nkc + q + m, sc0 + cg * gc:sc0 + (cg + 1) * gc],
                       [k.d_in], [f_b], f_b)
                pend[i] = (f, f_b)
            j = i - LA
            if j >= 0:
                (wv, ap, buf, kg, cg, q, m, nkc, gc, sc0) = items[j]
                f, f_b = pend.pop(j)
                o, o_b = bst.next()
                copy_op(k, engs[j % len(engs)], o[:, 0:m * gc], f[:, 0:m * gc], [f_b], [o_b])
                ov = o[:].rearrange("p (a n) -> p a n", n=gc)
                tr.dma("sp", ap[kg, cg, :, q:q + m, :], ov[:, 0:m, :], [o_b], [buf], o_b)


def linear_F(k, xT, xT_b, nkc, T, wname, col0, ncols, slabs, epilogue, gcols=256):
    tr = k.tr
    nh = T // 512
    for c0 in range(col0, col0 + ncols, gcols):
        gc = min(gcols, col0 + ncols - c0)
        assert gc == gcols and c0 % gcols == 0
        slab, slab_b = load_slab(k, slabs, wname, 0, c0 // gcols)
        nfc = gc // 128
        for h in range(nh):
            banks = []
            for fc in range(nfc):
                bi = k.bank_rr
                k.bank_rr = (k.bank_rr + 1) % 8
                banks.append(bi)
            fns = []
            for kc in range(nkc):
                for fc in range(nfc):
                    fns.append(lambda e, kc=kc, fc=fc: e.matmul(
                        k.ps[banks[fc]][:], lhsT=slab[:, kc, fc * 128:(fc + 1) * 128],
                        rhs=xT[:, kc, h * 512:(h + 1) * 512], start=(kc == 0), stop=(kc == nkc - 1)))
            tr.pe_group([xT_b, slab_b], [k.psb[b] for b in banks], fns)
            for fc in range(nfc):
                epilogue((c0 + fc * 128) // 128, h, banks[fc])


def linear_T(k, xT, xT_b, nkc, T, wname, col0, ncols, slabs, epilogue, gcols=256):
    tr = k.tr
    nst = T // 128
    for c0 in range(col0, col0 + ncols, gcols):
        gc = min(gcols, col0 + ncols - c0)
        assert gc == gcols and c0 % gcols == 0
        slab, slab_b = load_slab(k, slabs, wname, 0, c0 // gcols)
        for s0 in range(0, nst, 4):
            banks = []
            for st in range(4):
                bi = k.bank_rr
                k.bank_rr = (k.bank_rr + 1) % 8
                banks.append(bi)
            fns = []
            for kc in range(nkc):
                for st in range(4):
                    fns.append(lambda e, kc=kc, st=st: e.matmul(
                        k.ps[banks[st]][:, 0:gc], lhsT=xT[:, kc, (s0 + st) * 128:(s0 + st + 1) * 128],
                        rhs=slab[:, kc, 0:gc], start=(kc == 0), stop=(kc == nkc - 1)))
            tr.pe_group([xT_b, slab_b], [k.psb[b] for b in banks], fns)
            for st in range(4):
                epilogue(c0, gc, s0 + st, banks[st])


class Stage:
    def __init__(self, k, pfx, shape, dt, n=2):
        self.t = [k.sb(f"{pfx}{i}", shape, dt) for i in range(n)]
        self.b = [Buf(f"{pfx}{i}") for i in range(n)]
        self.i = 0
        self.n = n

    def next(self):
        i = self.i
        self.i = (self.i + 1) % self.n
        return self.t[i], self.b[i]


def phase_proj(k):
    tr, sb = k.tr, k.sb
    with contextlib.ExitStack() as ph:
        old_stack, old_sb = k.stack, k.sb
        k.sb = lambda name, shape, dt: ph.enter_context(k.nc.sbuf_tensor(name, list(shape), dt))
        sb = k.sb
        rt = RmsT(k, "pj", 0)
        xnT = sb("pj_xnT", [128, KC, 1024], BF16)
        xnT_b = Buf("pj_xnT")
        slabs = Slabs(k, "pj_slab", KC, 256, n=2)
        stg_f = Stage(k, "pj_sf", [128, 512], F32, n=3)
        stg_h = Stage(k, "pj_sh", [128, 512], BF16, n=3)
        bm = sb("pj_bm", [128, 64], F32)
        bm_b = Buf("pj_bm")
        tr.dma("sp", bm[:], k.din("b_merge")[:, :], [k.d_in], [bm_b], bm_b)

        import os
        for blk in range(int(os.environ.get('PJ_BLOCKS', EXT // 1024))):
            t0 = blk * 1024
            stages = os.environ.get('PJ_STAGES', 'qkvxgm')
            for j in range(8):
                rms_tile(k, rt, k.din("xo")[t0 + j * 128:t0 + (j + 1) * 128, :], [k.d_in], xnT, xnT_b, j * 128)

            def ep_f(dst, dst_b, dt_is_f32, sigm_bias=None):
                def ep(f, h, bank, dst=dst, dst_b=dst_b):
                    stg = stg_f if dt_is_f32 else stg_h
                    t, b = stg.next()
                    if sigm_bias is not None:
                        tr.op("act", [k.psb[bank], bm_b], [b],
                              lambda e: e.activation(out=t[:], in_=k.ps[bank][:], func=AF.Sigmoid,
                                                     bias=bm[:, sigm_bias + f:sigm_bias + f + 1], scale=1.0))
                    else:
                        copy_op(k, evac_engine(), t[:], k.ps[bank][:], [k.psb[bank]], [b])
                    tr.dma("sp", dst[f * 128:(f + 1) * 128, t0 + h * 512:t0 + (h + 1) * 512], t[:], [b], [dst_b], b)
                return ep

            if 'q' in stages:
                linear_F(k, xnT, xnT_b, KC, 1024, "w_in", 0, int(os.environ.get('PJ_QCOLS', 2048)), slabs, ep_f(k.QT, k.QT_b, False))
            kt_ep = ep_f(k.KT, k.KT_b, False)
            if 'k' in stages:
                linear_F(k, xnT, xnT_b, KC, 1024, "w_in", 2048, 2048, slabs,
                         lambda f, h, bank: kt_ep(f - 16, h, bank))
            def ep_v(c0, gc, st, bank):
                t, b = stg_h.next()
                copy_op(k, evac_engine(), t[:, 0:gc], k.ps[bank][:, 0:gc], [k.psb[bank]], [b])
                tr.dma("sp", k.VS[t0 + st * 128:t0 + (st + 1) * 128, c0 - 4096:c0 - 4096 + gc], t[:, 0:gc],
                       [b], [k.VS_b], b)
            if 'v' in stages:
                linear_T(k, xnT, xnT_b, KC, 1024, "w_in", 4096, 2048, slabs, ep_v)
            xr_ep = ep_f(k.XR, k.XR_b, True)
            if 'x' in stages:
                linear_F(k, xnT, xnT_b, KC, 1024, "w_in", 6144, 2048, slabs, lambda f, h, bank: xr_ep(f - 48, h, bank))
            gr_ep = ep_f(k.GR, k.GR_b, True)
            if 'g' in stages:
                linear_F(k, xnT, xnT_b, KC, 1024, "w_in", 8192, 2048, slabs, lambda f, h, bank: gr_ep(f - 64, h, bank))
            if 'm' in stages:
                linear_F(k, xnT, xnT_b, KC, 1024, "w_merge", 0, 2 * D, slabs, ep_f(k.GT, k.GT_b, False, sigm_bias=0))
        k.sb = old_sb


def own_ext(o):
    return 256 + o if o < OWN_P else EXT_P + 256 + (o - OWN_P)


@contextlib.contextmanager
def phase_scope(k):
    with contextlib.ExitStack() as ph:
        old = k.sb
        k.sb = lambda name, shape, dt: ph.enter_context(k.nc.sbuf_tensor(name, list(shape), dt))
        try:
            yield
        finally:
            k.sb = old


def load_T(k, dst, dst_b, src_ap, src_b, nkc, c0, n):
    sv = src_ap.rearrange("(kc p) t -> p kc t", p=128)
    for q in range(0, nkc, 8):
        m = min(8, nkc - q)
        k.tr.dma("sp", dst[:, q:q + m, 0:n], sv[:, q:q + m, c0:c0 + n], [src_b], [dst_b], dst_b)


def phase_gate(k):
    tr = k.tr
    with phase_scope(k):
        sb = k.sb
        naT = sb("gt_naT", [128, 16, 1024], BF16); naT_b = Buf("gt_naT")
        lrT = sb("gt_lrT", [128, 16, 1024], BF16); lrT_b = Buf("gt_lrT")
        slabs = Slabs(k, "gt_slab", 16, 256, n=4)
        gts = Stage(k, "gt_g", [128, 2, 512], BF16, n=4)
        tmp = Stage(k, "gt_tmp", [128, 2, 512], F32, n=3)
        outs = Stage(k, "gt_out", [128, 512], BF16, n=3)
        for blk in range(OWN // 1024):
            o0 = blk * 1024
            e0 = own_ext(o0)
            load_T(k, naT, naT_b, k.NA, k.NA_b, 16, o0, 1024)
            load_T(k, lrT, lrT_b, k.LR, k.LR_b, 16, o0, 1024)
            for c0 in range(0, D, 256):
                s_na, s_na_b = load_slab(k, slabs, "w_na", 0, c0 // 256)
                s_lr, s_lr_b = load_slab(k, slabs, "w_lru", 0, c0 // 256)
                for h in range(2):
                    banks = []
                    for _ in range(4):
                        banks.append(k.bank_rr)
                        k.bank_rr = (k.bank_rr + 1) % 8
                    fns = []
                    for kc in range(16):
                        for fc in range(2):
                            fns.append(lambda e, kc=kc, fc=fc: e.matmul(
                                k.ps[banks[fc]][:], lhsT=s_na[:, kc, fc * 128:(fc + 1) * 128],
                                rhs=naT[:, kc, h * 512:(h + 1) * 512], start=(kc == 0), stop=(kc == 15)))
                    for kc in range(16):
                        for fc in range(2):
                            fns.append(lambda e, kc=kc, fc=fc: e.matmul(
                                k.ps[banks[2 + fc]][:], lhsT=s_lr[:, kc, fc * 128:(fc + 1) * 128],
                                rhs=lrT[:, kc, h * 512:(h + 1) * 512], start=(kc == 0), stop=(kc == 15)))
                    tr.pe_group([naT_b, lrT_b, s_na_b, s_lr_b], [k.psb[b] for b in banks], fns)
                    for fc in range(2):
                        f = c0 // 128 + fc
                        g, g_b = gts.next()
                        tr.dma("sp", g[:, 0, :], k.GT[f * 128:(f + 1) * 128, e0 + h * 512:e0 + (h + 1) * 512],
                               [k.GT_b], [g_b], g_b)
                        tr.dma("sp", g[:, 1, :], k.GT[(32 + f) * 128:(33 + f) * 128, e0 + h * 512:e0 + (h + 1) * 512],
                               [k.GT_b], [g_b], g_b)
                        t, t_b = tmp.next()
                        tr.op("dve", [k.psb[banks[fc]], g_b], [t_b],
                              lambda e: e.tensor_tensor(out=t[:, 0, :], in0=k.ps[banks[fc]][:], in1=g[:, 0, :], op=ALU.mult))
                        tr.op("dve", [k.psb[banks[2 + fc]], g_b], [t_b],
                              lambda e: e.tensor_tensor(out=t[:, 1, :], in0=k.ps[banks[2 + fc]][:], in1=g[:, 1, :], op=ALU.mult))
                        o, o_b = outs.next()
                        tr.op("pool", [t_b], [o_b],
                              lambda e: e.tensor_tensor(out=o[:], in0=t[:, 0, :], in1=t[:, 1, :], op=ALU.add))
                        tr.dma("sp", k.GD[f * 128:(f + 1) * 128, o0 + h * 512:o0 + (h + 1) * 512], o[:], [o_b], [k.GD_b], o_b)


class LruCommon:
    def rotate_piece(self):
        self.pi = (self.pi + 1) % self.nb
        for kk, v in self.psets[self.pi].items():
            setattr(self, kk, v)

    def rotate_conv(self):
        self.ci = (self.ci + 1) % self.nb
        for kk, v in self.csets[self.ci].items():
            setattr(self, kk, v)

    def __init__(self, k, pfx, T, nb=1):
        sb, tr = k.sb, k.tr
        self.nb = nb
        self.wgf = None
        self.wg = sb(pfx + "_wg", [128, 64, 128], BF16); self.wg_b = Buf(pfx + "_wg")
        tr.dma("pool", self.wg[:], k.din("wgate")[:, :, :], [k.d_in], [self.wg_b], self.wg_b)
        self.gp = sb(pfx + "_gp", [128, 16, 6], F32); self.gp_b = Buf(pfx + "_gp")
        tr.dma("sp", self.gp[:], k.din("gatep")[:, :, :], [k.d_in], [self.gp_b], self.gp_b)
        self.cp = sb(pfx + "_cp", [128, 16, 5], F32); self.cp_b = Buf(pfx + "_cp")
        tr.dma("sp", self.cp[:], k.din("convp")[:, :, :], [k.d_in], [self.cp_b], self.cp_b)
        self.cs = sb(pfx + "_cs", [128, 16, 2], F32); self.cs_b = Buf(pfx + "_cs")
        tr.op("act", [self.gp_b], [self.cs_b],
              lambda e: e.activation(out=self.cs[:], in_=self.gp[:, :, 4:6], func=AF.Exp, scale=-1.0))
        tr.op("act", [self.cs_b, k.b_eps], [self.cs_b],
              lambda e: e.activation(out=self.cs[:], in_=self.cs[:], func=AF.Ln, bias=k.eps_t[:, 1:2], scale=1.0))
        tr.op("dve", [self.cs_b], [self.cs_b], lambda e: e.tensor_scalar_mul(out=self.cs[:], in0=self.cs[:], scalar1=-8.0))
        self.psets, self.csets = [], []
        for i in range(nb):
            sfx = f"{pfx}_{i}"
            self.csets.append({"xc": sb(sfx + "_xc", [128, T], F32), "xc_b": Buf(sfx + "_xc"),
                               "xcb": sb(sfx + "_xcb", [128, T], BF16), "xcb_b": Buf(sfx + "_xcb")})
            self.psets.append({"A": sb(sfx + "_A", [128, 1024], F32), "A_b": Buf(sfx + "_A"),
                               "B": sb(sfx + "_B", [128, 1024], F32), "B_b": Buf(sfx + "_B"),
                               "C": sb(sfx + "_C", [128, 1024], F32), "C_b": Buf(sfx + "_C"),
                               "sr": sb(sfx + "_sr", [128, 4], F32), "sr_b": Buf(sfx + "_sr")})
        self.pi = self.ci = nb - 1
        self.rotate_piece()
        self.rotate_conv()


def lru_conv(k, L, xp, xp_b, n, T):
    tr = k.tr
    cp = L.cp
    tr.op("dve", [xp_b, L.cp_b], [L.xc_b],
          lambda e: e.tensor_scalar(out=L.xc[:, 0:T], in0=xp[:, 0:T], scalar1=cp[:, n, 0:1], scalar2=cp[:, n, 4:5],
                                    op0=ALU.mult, op1=ALU.add))
    for j in (1, 2, 3):
        tr.op("dve", [xp_b, L.cp_b, L.xc_b], [L.xc_b],
              lambda e: e.scalar_tensor_tensor(out=L.xc[:, 0:T], in0=xp[:, j:j + T], scalar=cp[:, n, j:j + 1],
                                               in1=L.xc[:, 0:T], op0=ALU.mult, op1=ALU.add))
    tr.op("pool", [L.xc_b], [L.xcb_b], lambda e: e.tensor_copy(out=L.xcb[:, 0:T], in_=L.xc[:, 0:T]))


def lru_piece_gen(k, L, W, n, d, t0, first_fix, init_ap, init_bufs, out_tile, out_b, out_col0=0):
    tr = k.tr
    banks = []
    for _ in range(4):
        banks.append(k.bank_rr)
        k.bank_rr = (k.bank_rr + 1) % 8
    fns = []
    for g in range(2):
        for h in range(2):
            fns.append(lambda e, g=g, h=h: e.matmul(
                k.ps[banks[g * 2 + h]][:], lhsT=L.wg[:, g * 32 + d * 16 + n, :],
                rhs=L.xcb[:, t0 + h * 512:t0 + (h + 1) * 512], start=True, stop=True))
    tr.pe_group([L.wg_b, L.xcb_b], [k.psb[b] for b in banks], fns)
    yield
    for h in range(2):
        tr.op("act", [k.psb[banks[h]], L.gp_b], [W.A_b, W.sr_b],
              lambda e: e.activation(out=W.A[:, h * 512:(h + 1) * 512], in_=k.ps[banks[h]][:], func=AF.Sigmoid,
                                     bias=L.gp[:, n, d:d + 1], scale=1.0, accum_out=W.sr[:, h:h + 1]))
    yield
    for h in range(2):
        tr.op("act", [k.psb[banks[2 + h]], L.gp_b], [W.B_b],
              lambda e: e.activation(out=W.B[:, h * 512:(h + 1) * 512], in_=k.ps[banks[2 + h]][:], func=AF.Sigmoid,
                                     bias=L.gp[:, n, 2 + d:3 + d], scale=1.0))
    yield
    tr.op("act", [W.A_b, L.cs_b], [W.A_b],
          lambda e: e.activation(out=W.A[:], in_=W.A[:], func=AF.Exp, scale=L.cs[:, n, d:d + 1]))
    yield
    tr.op("pool", [W.A_b], [W.C_b], lambda e: e.tensor_tensor(out=W.C[:], in0=W.A[:], in1=W.A[:], op=ALU.mult))
    yield
    tr.op("act", [W.C_b, k.b_eps], [W.C_b],
          lambda e: e.activation(out=W.C[:], in_=W.C[:], func=AF.Sqrt, scale=-1.0, bias=k.eps_t[:, 1:2]))
    if first_fix is not None:
        first_fix(W)
    yield
    tr.op("dve", [W.B_b, L.xc_b], [W.B_b],
          lambda e: e.tensor_tensor(out=W.B[:], in0=W.B[:], in1=L.xc[:, t0:t0 + 1024], op=ALU.mult))
    yield
    tr.op("pool", [W.B_b, W.C_b], [W.B_b], lambda e: e.tensor_tensor(out=W.B[:], in0=W.B[:], in1=W.C[:], op=ALU.mult))
    yield
    o = out_tile[:, out_col0:out_col0 + 1024]
    if d == 0:
        tr.op("dve", [W.A_b, W.B_b] + init_bufs, [out_b],
              lambda e: e.tensor_tensor_scan(out=o, data0=W.A[:], data1=W.B[:], initial=init_ap, op0=ALU.mult, op1=ALU.add))
    else:
        tr.op("dve", [W.A_b, W.B_b] + init_bufs, [out_b],
              lambda e: e.tensor_tensor_scan(out=o[:, ::-1], data0=W.A[:, ::-1], data1=W.B[:, ::-1], initial=init_ap,
                                             op0=ALU.mult, op1=ALU.add))


class WS:
    def __init__(self, L):
        for nm in ("A", "B", "C", "sr"):
            setattr(self, nm, getattr(L, nm))
            setattr(self, nm + "_b", getattr(L, nm + "_b"))


def lru_piece(k, L, n, d, t0, first_fix, init_ap, init_bufs, out_tile, out_b, out_col0=0):
    for _ in lru_piece_gen(k, L, WS(L), n, d, t0, first_fix, init_ap, init_bufs, out_tile, out_b, out_col0):
        pass


def phase_lru_summ(k):
    tr = k.tr
    k.sumA = k.sb_glob("sumA", [128, 2, 24, 16], F32); k.sumA_b = Buf("sumA")
    k.sumH = k.sb_glob("sumH", [128, 2, 24, 16], F32); k.sumH_b = Buf("sumH")
    with phase_scope(k):
        sb = k.sb
        rt = RmsT(k, "ls", 0, nx=1)
        xnT = sb("ls_xnT", [128, KC, 1028], BF16); xnT_b = Buf("ls_xnT")
        slabs = Slabs(k, "ls_slab", KC, 128, n=2)
        xps = Stage(k, "ls_xp", [128, 1028], F32, n=2)
        L = LruCommon(k, "ls", 1024, nb=2)
        slots = [("xp", s, 16) for s in range(16)] + [("xs", s, 8) for s in range(8)]
        import os
        nslots = int(os.environ.get("LS_SLOTS", 24))
        for si, (nm, s, ns) in enumerate(slots[:nslots]):
            X = k.din(nm)
            base = s * 1024
            for j in range(8):
                rms_tile(k, rt, X[base + j * 128:base + (j + 1) * 128, :], [k.d_in], xnT, xnT_b, j * 128)
            srcs = []
            if s > 0:
                srcs.append((0, 2, X[base - 2:base, :]))
            if s < ns - 1:
                srcs.append((2, 1, X[base + 1024:base + 1025, :]))
            rms_tile(k, rt, None, [k.d_in], xnT, xnT_b, 1024, ncols=3, zero_first=True, srcs=srcs)
            for n in range(16):
                slab, slab_b = load_slab(k, slabs, "w_xr", 0, n)
                hc = 0
                banks = []
                for _ in range(3):
                    banks.append(k.bank_rr)
                    k.bank_rr = (k.bank_rr + 1) % 8
                fns = []
                for kc in range(KC):
                    for h in range(2):
                        fns.append(lambda e, kc=kc, h=h: e.matmul(
                            k.ps[banks[h]][:], lhsT=slab[:, kc, hc:hc + 128], rhs=xnT[:, kc, h * 512:(h + 1) * 512],
                            start=(kc == 0), stop=(kc == KC - 1)))
                    fns.append(lambda e, kc=kc: e.matmul(
                        k.ps[banks[2]][:, 0:3], lhsT=slab[:, kc, hc:hc + 128], rhs=xnT[:, kc, 1024:1027],
                        start=(kc == 0), stop=(kc == KC - 1)))
                tr.pe_group([xnT_b, slab_b], [k.psb[b] for b in banks], fns)
                xp, xp_b = xps.next()
                copy_op(k, "act", xp[:, 2:514], k.ps[banks[0]][:], [k.psb[banks[0]]], [xp_b])
                copy_op(k, "dve", xp[:, 514:1026], k.ps[banks[1]][:], [k.psb[banks[1]]], [xp_b])
                copy_op(k, "dve", xp[:, 0:2], k.ps[banks[2]][:, 0:2], [k.psb[banks[2]]], [xp_b])
                copy_op(k, "dve", xp[:, 1026:1027], k.ps[banks[2]][:, 2:3], [k.psb[banks[2]]], [xp_b])
                L.rotate_conv()
                lru_conv(k, L, xp, xp_b, n, 1024)
                gens = []
                for d in range(2):
                    L.rotate_piece()

                    def gen(W=WS(L), d=d, n=n, si=si, s=s, ns=ns):
                        fix = None
                        if (d == 0 and s == 0) or (d == 1 and s == ns - 1):
                            col = 0 if d == 0 else 1023
                            fix = lambda W, col=col: tr.op("dve", [], [W.C_b], lambda e: e.memset(W.C[:, col:col + 1], 1.0))
                        yield from lru_piece_gen(k, L, W, n, d, 0, fix, 0.0, [], W.C, W.C_b)
                        hcol = 1023 if d == 0 else 0
                        tr.op("dve", [W.C_b], [k.sumH_b],
                              lambda e: e.tensor_copy(out=k.sumH[:, d, si, n:n + 1], in_=W.C[:, hcol:hcol + 1]))
                        tr.op("dve", [W.sr_b], [W.sr_b],
                              lambda e: e.tensor_tensor(out=W.sr[:, 2:3], in0=W.sr[:, 0:1], in1=W.sr[:, 1:2], op=ALU.add))
                        tr.op("act", [W.sr_b, L.cs_b], [k.sumA_b],
                              lambda e: e.activation(out=k.sumA[:, d, si, n:n + 1], in_=W.sr[:, 2:3], func=AF.Exp,
                                                     scale=L.cs[:, n, d:d + 1]))
                    gens.append(gen())
                while gens:
                    for g in list(gens):
                        try:
                            next(g)
                        except StopIteration:
                            gens.remove(g)


def phase_lru_own(k):
    tr = k.tr
    with phase_scope(k):
        sb = k.sb
        selt = sb("lo_sel", [128, 4, 17], F32); sel_b = Buf("lo_sel")
        tr.dma("sp", selt[:], k.din("sel")[:, :, :], [k.d_in], [sel_b], sel_b)
        car = sb("lo_car", [128, 4, 16], F32); car_b = Buf("lo_car")
        S = sb("lo_S", [128, 16], F32); S_b = Buf("lo_S")
        tr.op("dve", [], [car_b], lambda e: e.memset(car[:], 0.0))
        for (q, slot0, nb) in ((0, 0, 16), (1, 16, 8)):
            tr.op("dve", [], [S_b], lambda e: e.memset(S[:], 0.0))
            for j in range(nb):
                tr.op("dve", [S_b, k.sumA_b], [S_b], lambda e: e.tensor_tensor(out=S[:], in0=S[:], in1=k.sumA[:, 0, slot0 + j, :], op=ALU.mult))
                tr.op("dve", [S_b, k.sumH_b], [S_b], lambda e: e.tensor_tensor(out=S[:], in0=S[:], in1=k.sumH[:, 0, slot0 + j, :], op=ALU.add))
                tr.op("dve", [S_b, sel_b, car_b], [car_b],
                      lambda e: e.scalar_tensor_tensor(out=car[:, 2 * q, :], in0=S[:], scalar=selt[:, 2 * q, j + 1:j + 2],
                                                       in1=car[:, 2 * q, :], op0=ALU.mult, op1=ALU.add))
            tr.op("dve", [], [S_b], lambda e: e.memset(S[:], 0.0))
            for j in range(nb - 1, -1, -1):
                tr.op("dve", [S_b, k.sumA_b], [S_b], lambda e: e.tensor_tensor(out=S[:], in0=S[:], in1=k.sumA[:, 1, slot0 + j, :], op=ALU.mult))
                tr.op("dve", [S_b, k.sumH_b], [S_b], lambda e: e.tensor_tensor(out=S[:], in0=S[:], in1=k.sumH[:, 1, slot0 + j, :], op=ALU.add))
                tr.op("dve", [S_b, sel_b, car_b], [car_b],
                      lambda e: e.scalar_tensor_tensor(out=car[:, 2 * q + 1, :], in0=S[:], scalar=selt[:, 2 * q + 1, j:j + 1],
                                                       in1=car[:, 2 * q + 1, :], op0=ALU.mult, op1=ALU.add))
        L = LruCommon(k, "lo", 2048)
        xps = Stage(k, "lo_xp", [128, 2052], F32, n=2)
        grs = Stage(k, "lo_gr", [128, 2048], F32, n=2)
        HF = sb("lo_HF", [128, 2048], F32); HF_b = Buf("lo_HF")
        HR = sb("lo_HR", [128, 1024], F32); HR_b = Buf("lo_HR")
        G1 = sb("lo_G1", [128, 1024], F32); G1_b = Buf("lo_G1")
        G2 = sb("lo_G2", [128, 1024], F32); G2_b = Buf("lo_G2")
        outs = Stage(k, "lo_out", [128, 1024], BF16, n=2)
        tmp1 = sb("lo_t1", [128, 2], F32); tmp1_b = Buf("lo_t1")
        for n in range(16):
            for (q, e0, T, ob) in ((0, 256, OWN_P, 0), (1, EXT_P + 256, OWN_S, OWN_P)):
                xp, xp_b = xps.next()
                tr.dma("sp", xp[:, 0:T + 3], k.XR[n * 128:(n + 1) * 128, e0 - 2:e0 + T + 1], [k.XR_b], [xp_b], xp_b)
                gr, gr_b = grs.next()
                tr.dma("sp", gr[:, 0:T], k.GR[n * 128:(n + 1) * 128, e0:e0 + T], [k.GR_b], [gr_b], gr_b)
                lru_conv(k, L, xp, xp_b, n, T)
                npc = T // 1024

                def mkfix(col, fcol):
                    def fix(L):
                        tr.op("dve", [L.C_b], [tmp1_b],
                              lambda e: e.tensor_scalar(out=tmp1[:, 0:1], in0=L.C[:, col:col + 1], scalar1=-1.0, scalar2=1.0,
                                                        op0=ALU.mult, op1=ALU.add))
                        tr.op("dve", [tmp1_b, k.b_flags, L.C_b], [L.C_b],
                              lambda e: e.scalar_tensor_tensor(out=L.C[:, col:col + 1], in0=tmp1[:, 0:1],
                                                               scalar=k.flags_t[:, fcol:fcol + 1], in1=L.C[:, col:col + 1],
                                                               op0=ALU.mult, op1=ALU.add))
                    return fix
                for pc in range(npc):
                    init = car[:, 2 * q, n:n + 1] if pc == 0 else HF[:, pc * 1024 - 1:pc * 1024]
                    lru_piece(k, L, n, 0, pc * 1024, mkfix(0, 0) if pc == 0 else None, init,
                              [car_b] if pc == 0 else [HF_b], HF, HF_b, out_col0=pc * 1024)
                for pi, pc in enumerate(range(npc - 1, -1, -1)):
                    if pi == 0:
                        init, ib = car[:, 2 * q + 1, n:n + 1], [car_b]
                    else:
                        init, ib = tmp1[:, 1:2], [tmp1_b]
                    lru_piece(k, L, n, 1, pc * 1024, mkfix(1023, 2) if pi == 0 else None, init, ib, HR, HR_b)
                    if pi + 1 < npc:
                        tr.op("dve", [HR_b], [tmp1_b], lambda e: e.tensor_copy(out=tmp1[:, 1:2], in_=HR[:, 0:1]))
                    x = gr[:, pc * 1024:(pc + 1) * 1024]
                    tr.op("pool", [HR_b, HF_b], [HR_b],
                          lambda e: e.tensor_tensor(out=HR[:], in0=HR[:], in1=HF[:, pc * 1024:(pc + 1) * 1024], op=ALU.add))
                    tr.op("pool", [gr_b], [G1_b], lambda e: e.tensor_tensor(out=G1[:], in0=x, in1=x, op=ALU.mult))
                    tr.op("dve", [G1_b], [G1_b],
                          lambda e: e.tensor_scalar(out=G1[:], in0=G1[:], scalar1=0.044715, scalar2=1.0, op0=ALU.mult, op1=ALU.add))
                    tr.op("dve", [G1_b, gr_b], [G1_b], lambda e: e.tensor_tensor(out=G1[:], in0=G1[:], in1=x, op=ALU.mult))
                    tr.op("act", [G1_b], [G2_b], lambda e: e.activation(out=G2[:], in_=G1[:], func=AF.Tanh, scale=0.7978845608028654))
                    tr.op("dve", [G2_b, gr_b], [G2_b],
                          lambda e: e.scalar_tensor_tensor(out=G2[:], in0=G2[:], scalar=1.0, in1=x, op0=ALU.add, op1=ALU.mult))
                    o, o_b = outs.next()
                    tr.op("dve", [HR_b, G2_b], [o_b],
                          lambda e: e.scalar_tensor_tensor(out=o[:], in0=HR[:], scalar=0.5, in1=G2[:], op0=ALU.mult, op1=ALU.mult))
                    tr.dma("sp", k.LR[n * 128:(n + 1) * 128, ob + pc * 1024:ob + (pc + 1) * 1024], o[:], [o_b], [k.LR_b], o_b)


def attn_jobs():
    tab_u = {-6: 0, -4: 1, -2: 2, 0: 3, 2: 4, 4: 5, 6: 6}
    tab_g = {-4: 7, -2: 2, 0: 3, 2: 4, 4: 8}
    segs = []
    for (pb, n, ob) in ((0, 32, 0), (EXT_P // 128, 16, OWN_P)):
        npq = n // 2
        jobs = []
        for qi in range(npq):
            b = 2 + qi
            top, bot = qi < 2, qi >= npq - 2
            gflag = 1 if top else (3 if bot else 4)
            tiles = [(b + d2, tab_g[2 * d2], gflag) for d2 in (-2, -1, 0, 1, 2)]
            if top:
                tiles += [(a, tab_u[2 * (a - b)], 0) for a in (2, 3, 4, 5)]
            if bot:
                tiles += [(a, tab_u[2 * (a - b)], 2) for a in range(npq - 2, npq + 2)]
            jobs.append((pb, b, ob + qi * 128, tiles))
        segs.append(jobs)
    return segs[0] + segs[1]


def phase_attn(k):
    tr = k.tr
    scale = 128.0 ** -0.5
    HG = 4
    with phase_scope(k):
        sb = k.sb
        q_t = sb("at_q", [128, HG, EXT], BF16); q_b = Buf("at_q")
        k_t = sb("at_k", [128, HG, EXT], BF16); k_b = Buf("at_k")
        v_t = sb("at_v", [128, EXT // 128, HG * 128], BF16); v_b = Buf("at_v")
        tt = sb("at_tt", [128, HG, 9, 128], F32); tt_b = Buf("at_tt")
        ones = sb("at_ones", [128, 128], BF16); ones_b = Buf("at_ones")
        tr.op("dve", [], [ones_b], lambda e: e.memset(ones[:], 1.0))
        exs = Stage(k, "at_ex", [128, 128], F32, n=6)
        ets = Stage(k, "at_et", [128, 128], BF16, n=12)
        import os as _os
        recs = Stage(k, "at_rec", [128, 128], F32, n=int(_os.environ.get("AT_RECS", 2)))
        nas = Stage(k, "at_na", [128, OWN], BF16, n=2)
        s_slots = [(bank, j) for bank in range(4) for j in range(4)]
        s_bufs = [Buf(f"at_s{i}") for i in range(16)]
        s_rr = [0]
        import os
        jobs = attn_jobs()[int(os.environ.get('AT_JOB0', 0)):][:int(os.environ.get('AT_JOBS', 1000))]
        at_mode = int(os.environ.get('AT_MODE', 2))
        for hg in range(int(os.environ.get('AT_HG', NH // HG))):
            for hl in range(HG):
                tr.dma("sp", q_t[:, hl, :], k.QT[(hg * HG + hl) * 128:(hg * HG + hl + 1) * 128, :], [k.QT_b], [q_b], q_b)
                tr.dma("sp", k_t[:, hl, :], k.KT[(hg * HG + hl) * 128:(hg * HG + hl + 1) * 128, :], [k.KT_b], [k_b], k_b)
            for p0 in range(0, EXT // 128, 4):
                tr.dma("sp", v_t[:, p0:p0 + 4, :], k.VS.rearrange("(pr p) c -> p pr c", p=128)[:, p0:p0 + 4, hg * 512:(hg + 1) * 512],
                       [k.VS_b], [v_b], v_b)
            for hl in range(HG):
                tr.dma("sp", tt[:, hl, :, :], k.din("tt")[hg * HG + hl, :, :, :], [k.d_in], [tt_b], tt_b)
            tr.op("act", [tt_b], [tt_b], lambda e: e.activation(out=tt[:], in_=tt[:], func=AF.Exp))
            for hl in range(int(os.environ.get("AT_HL", HG)) if at_mode > 0 else 0):
                h = hg * HG + hl
                na, na_b = nas.next()

                flat = []
                for ji, job in enumerate(jobs):
                    pb, b, o, tiles = job
                    for i, (a, ti, fl) in enumerate(tiles):
                        flat.append((ji, pb, b, o, a, ti, fl, i, len(tiles)))
                LOOK = 3
                for idx in range(len(flat) + LOOK):
                    if idx < len(flat):
                        (ji, pb, b, o, a, ti, fl, i, nt) = flat[idx]
                        sbank = idx % 4
                        tr.pe_group([k_b, q_b], [k.psb[sbank]], [lambda e: e.matmul(
                            k.ps[sbank][:, 0:128], lhsT=k_t[:, hl, (pb + a) * 128:(pb + a + 1) * 128],
                            rhs=q_t[:, hl, (pb + b) * 128:(pb + b + 1) * 128], start=True, stop=True)])
                    j = idx - LOOK
                    if j < 0:
                        continue
                    (ji, pb, b, o, a, ti, fl, i, nt) = flat[j]
                    sbank = j % 4
                    ob, db = 4 + ji % 2, 6 + ji % 2
                    ex, ex_b = exs.next()
                    tr.op("act", [k.psb[sbank]], [ex_b],
                          lambda e: e.activation(out=ex[:], in_=k.ps[sbank][:, 0:128], func=AF.Exp, scale=scale))
                    et, et_b = ets.next()
                    tr.op("dve", [ex_b, tt_b, k.b_flags], [et_b],
                          lambda e: e.scalar_tensor_tensor(out=et[:], in0=ex[:], scalar=k.flags_t[:, fl:fl + 1],
                                                           in1=tt[:, hl, ti, :], op0=ALU.mult, op1=ALU.mult))
                    tr.pe_group([v_b, ones_b, et_b], [k.psb[ob], k.psb[db]], [
                        lambda e: e.matmul(k.ps[ob][:, 0:128], lhsT=v_t[:, pb + a, hl * 128:(hl + 1) * 128], rhs=et[:],
                                           start=(i == 0), stop=(i == nt - 1)),
                        lambda e: e.matmul(k.ps[db][:, 0:128], lhsT=ones[:], rhs=et[:], start=(i == 0), stop=(i == nt - 1))])
                    if i == nt - 1:
                        rc, rc_b = recs.next()
                        tr.op("dve", [k.psb[db]], [rc_b], lambda e: e.reciprocal(out=rc[:], in_=k.ps[db][:, 0:128]))
                        tr.op("dve", [k.psb[ob], rc_b], [na_b],
                              lambda e: e.tensor_tensor(out=na[:, o:o + 128], in0=k.ps[ob][:, 0:128], in1=rc[:], op=ALU.mult))
                tr.dma("sp", k.NA[h * 128:(h + 1) * 128, :], na[:], [na_b], [k.NA_b], na_b)


def evac_sq(k, bank, gc, stg, junk, junk_b, ssq, ssq_b, col):
    tr = k.tr
    t, b = stg.next()
    import os
    if os.environ.get("WO_EVAC", "1") == "0":
        tr.op("act", [k.psb[bank]], [junk_b, ssq_b],
              lambda e: e.activation(out=junk[:, 0:gc], in_=k.ps[bank][:, 0:gc], func=AF.Square, accum_out=ssq[:, col:col + 1]))
        tr.op("dve", [k.psb[bank]], [b], lambda e: e.tensor_copy(out=t[:, 0:gc], in_=k.ps[bank][:, 0:gc]))
    else:
        tr.op("dve", [k.psb[bank]], [b], lambda e: e.tensor_copy(out=t[:, 0:gc], in_=k.ps[bank][:, 0:gc]))
        tr.op("act", [b], [junk_b, ssq_b],
              lambda e: e.activation(out=junk[:, 0:gc], in_=t[:, 0:gc], func=AF.Square, accum_out=ssq[:, col:col + 1]))
    return t, b


def phase_wout(k):
    tr = k.tr
    with phase_scope(k):
        sb = k.sb
        gdT = sb("wo_gdT", [128, KC, 1024], BF16); gdT_b = Buf("wo_gdT")
        import os
        WGC = 512
        slabs = Slabs(k, "wo_slab", KC, WGC, n=2)
        stg = Stage(k, "wo_stg", [128, 512], F32, n=4)
        junk = sb("wo_junk", [128, 512], BF16); junk_b = Buf("wo_junk")
        for blk in range(OWN // 1024):
            o0 = blk * 1024
            load_T(k, gdT, gdT_b, k.GD, k.GD_b, KC, o0, 1024)

            def ep(c0, gc, st, bank):
                col = (blk * 8 + st) * 16 + c0 // WGC
                t, b = evac_sq(k, bank, gc, stg, junk, junk_b, k.ssq1, k.ssq1_b, col)
                tr.dma("sp", k.MIX[o0 + st * 128:o0 + (st + 1) * 128, c0:c0 + gc], t[:, 0:gc], [b], [k.MIX_b], b)
            linear_T(k, gdT, gdT_b, KC, 1024, "w_out", 0, D, slabs, ep, gcols=WGC)


def norm_resid(k, pfx, src, src_b, ssq, ssq_b, g_row, res_fn, dst, dst_b):
    tr = k.tr
    with phase_scope(k):
        sb = k.sb
        gbc = sb(pfx + "_gbc", [128, D], F32); gbc_b = Buf(pfx + "_gbc")
        tr.dma("sp", gbc[:], k.din("g4")[g_row:g_row + 1, :].broadcast_to([128, D]), [k.d_in], [gbc_b], gbc_b)
        ft = [sb(f"{pfx}_f{i}", [128, D], F32) for i in range(2)]; ft_b = [Buf(f"{pfx}_f{i}") for i in range(2)]
        rs = [sb(f"{pfx}_r{i}", [128, D], F32) for i in range(2)]; rs_b = [Buf(f"{pfx}_r{i}") for i in range(2)]
        st = sb(pfx + "_st", [128, 24, 4], F32); st_b = Buf(pfx + "_st")
        for i in range(OWN // 128):
            p = i % 2
            tr.dma("sp", ft[p][:], src[i * 128:(i + 1) * 128, :], [src_b], [ft_b[p]], ft_b[p])
            r_ap, r_buf = res_fn(i)
            tr.dma("sp", rs[p][:], r_ap, [r_buf], [rs_b[p]], rs_b[p])
            tr.op("dve", [ssq_b], [st_b],
                  lambda e: e.tensor_reduce(out=st[:, i, 0:1], in_=ssq[:, i * 16:(i + 1) * 16], axis=mybir.AxisListType.X, op=ALU.add))
            tr.op("act", [st_b, k.b_eps], [st_b],
                  lambda e: e.activation(out=st[:, i, 1:2], in_=st[:, i, 0:1], func=AF.Sqrt, scale=1.0 / D, bias=k.eps_t[:, 0:1]))
            tr.op("dve", [st_b], [st_b], lambda e: e.reciprocal(out=st[:, i, 2:3], in_=st[:, i, 1:2]))
            tr.op("dve", [ft_b[p], st_b, gbc_b], [ft_b[p]],
                  lambda e: e.scalar_tensor_tensor(out=ft[p][:], in0=ft[p][:], scalar=st[:, i, 2:3], in1=gbc[:],
                                                   op0=ALU.mult, op1=ALU.mult))
            tr.op("pool", [ft_b[p], rs_b[p]], [ft_b[p]],
                  lambda e: e.tensor_tensor(out=ft[p][:], in0=ft[p][:], in1=rs[p][:], op=ALU.add))
            tr.dma("sp", dst[i * 128:(i + 1) * 128, :], ft[p][:], [ft_b[p]], [dst_b], ft_b[p])


def phase_ffn_up(k):
    tr = k.tr
    with phase_scope(k):
        sb = k.sb
        rt = RmsT(k, "fu", 2, nx=1)
        xnT = sb("fu_xnT", [128, KC, 1024], BF16); xnT_b = Buf("fu_xnT")
        slabs = Slabs(k, "fu_slab", KC, 256, n=3)
        sgs = Stage(k, "fu_sg", [128, 512], F32, n=3)
        outs = Stage(k, "fu_out", [128, 512], BF16, n=3)
        for blk in range(OWN // 1024):
            o0 = blk * 1024
            for j in range(8):
                rms_tile(k, rt, k.X1[o0 + j * 128:o0 + (j + 1) * 128, :], [k.X1_b], xnT, xnT_b, j * 128)
            for c0 in range(0, DFF, 256):
                s_g, s_g_b = load_slab(k, slabs, "w_fg", 0, c0 // 256)
                s_u, s_u_b = load_slab(k, slabs, "w_fu", 0, c0 // 256)
                for h in range(2):
                    banks = []
                    for _ in range(4):
                        banks.append(k.bank_rr)
                        k.bank_rr = (k.bank_rr + 1) % 8
                    fns = []
                    for (sl, off) in ((s_g, 0), (s_u, 2)):
                        for kc in range(KC):
                            for fc in range(2):
                                fns.append(lambda e, kc=kc, fc=fc, sl=sl, off=off: e.matmul(
                                    k.ps[banks[off + fc]][:], lhsT=sl[:, kc, fc * 128:(fc + 1) * 128],
                                    rhs=xnT[:, kc, h * 512:(h + 1) * 512], start=(kc == 0), stop=(kc == KC - 1)))
                    tr.pe_group([xnT_b, s_g_b, s_u_b], [k.psb[b] for b in banks], fns)
                    for fc in range(2):
                        f = c0 // 128 + fc
                        sg, sg_b = sgs.next()
                        tr.op("act", [k.psb[banks[fc]]], [sg_b],
                              lambda e: e.activation(out=sg[:], in_=k.ps[banks[fc]][:], func=AF.Silu))
                        o, o_b = outs.next()
                        tr.op("dve", [k.psb[banks[2 + fc]], sg_b], [o_b],
                              lambda e: e.tensor_tensor(out=o[:], in0=k.ps[banks[2 + fc]][:], in1=sg[:], op=ALU.mult))
                        tr.dma("sp", k.HH[f * 128:(f + 1) * 128, o0 + h * 512:o0 + (h + 1) * 512], o[:], [o_b], [k.HH_b], o_b)


def phase_ffn_down(k):
    tr = k.tr
    NK = DFF // 128
    with phase_scope(k):
        sb = k.sb
        hT = sb("fd_hT", [128, NK, 512], BF16); hT_b = Buf("fd_hT")
        slabs = Slabs(k, "fd_slab", 8, 512, n=3)
        stg = Stage(k, "fd_stg", [128, 512], F32, n=4)
        junk = sb("fd_junk", [128, 512], BF16); junk_b = Buf("fd_junk")
        for grp in range(OWN // 512):
            o0 = grp * 512
            for q in range(0, NK, 22):
                n = min(22, NK - q)
                k.tr.dma("sp", hT[:, q:q + n, :], k.HH.rearrange("(kc p) t -> p kc t", p=128)[:, q:q + n, o0:o0 + 512],
                         [k.HH_b], [hT_b], hT_b)
            for c0 in range(0, D, 512):
                banks = []
                for _ in range(4):
                    banks.append(k.bank_rr)
                    k.bank_rr = (k.bank_rr + 1) % 8
                for kp in range(0, NK, 8):
                    nk = min(8, NK - kp)
                    sl, sl_b = load_slab(k, slabs, "w_fd", kp // 8, c0 // 512, nkc=nk)
                    fns = []
                    for j in range(nk):
                        for st in range(4):
                            fns.append(lambda e, j=j, st=st: e.matmul(
                                k.ps[banks[st]][:], lhsT=hT[:, kp + j, st * 128:(st + 1) * 128], rhs=sl[:, j, :],
                                start=(kp + j == 0), stop=(kp + j == NK - 1)))
                    tr.pe_group([hT_b, sl_b], [k.psb[b] for b in banks], fns)
                for st in range(4):
                    col = (grp * 4 + st) * 16 + c0 // 512
                    t, b = evac_sq(k, banks[st], 512, stg, junk, junk_b, k.ssq2, k.ssq2_b, col)
                    tr.dma("sp", k.FF[o0 + st * 128:o0 + (st + 1) * 128, c0:c0 + 512], t[:], [b], [k.FF_b], b)


def make_inputs(inp):
    f = lambda a: np.ascontiguousarray(np.asarray(a, dtype=np.float32))
    xp = f(inp["x_prompt"])[0]
    xs = f(inp["x_sample"])[0]
    shared = {
        "xp": xp, "xs": xs,
        "w_in": f(inp["w_in"])[0], "w_merge": f(inp["w_merge"])[0], "w_na": f(inp["w_na_out"])[0],
        "w_lru": f(inp["w_lru_out"])[0], "w_out": f(inp["w_out"])[0], "w_fg": f(inp["w_ffn_gate"])[0],
        "w_fu": f(inp["w_ffn_up"])[0], "w_fd": f(inp["w_ffn_down"])[0],
        "g4": f(np.stack([inp["g_mix_pre"][0], inp["g_mix_post"][0], inp["g_ffn_pre"][0], inp["g_ffn_post"][0]])),
        "b_merge": f(np.asarray(inp["b_merge"])[0].reshape(64, 128).T),
        "ident": np.eye(128, dtype=np.float32),
    }
    wc = np.asarray(inp["w_conv"])[0]
    bc = np.asarray(inp["b_conv"])[0]
    convp = np.concatenate([wc, bc[None]], 0)
    shared["convp"] = f(convp.reshape(5, 16, 128).transpose(2, 1, 0))
    wr = np.asarray(inp["w_rgate"])[0]
    wi = np.asarray(inp["w_igate"])[0]
    wg = np.stack([wr, wi], 0)
    shared["wgate"] = f(wg.transpose(3, 0, 1, 2, 4).reshape(128, 64, 128))
    gp = np.stack([np.asarray(inp["b_rgate"])[0][0], np.asarray(inp["b_rgate"])[0][1],
                   np.asarray(inp["b_igate"])[0][0], np.asarray(inp["b_igate"])[0][1],
                   np.asarray(inp["lru_lambda"])[0][0], np.asarray(inp["lru_lambda"])[0][1]], 0)
    shared["gatep"] = f(gp.reshape(6, 16, 128).transpose(2, 1, 0))
    shared["tt"] = make_tt(np.asarray(inp["rpb"], dtype=np.float32)[0])
    maps = []
    for c in range(NCORES):
        m = dict(shared)
        xo = np.zeros((EXT, D), np.float32)
        for (src, L, own, e0) in ((xp, LP, OWN_P, 0), (xs, LS, OWN_S, EXT_P)):
            lo, hi = c * own - 256, (c + 1) * own + 256
            slo, shi = max(lo, 0), min(hi, L)
            xo[e0 + (slo - lo):e0 + (shi - lo)] = src[slo:shi]
        m["xo"] = xo
        fl = np.zeros((128, 8), np.float32)
        ftop, fbot = float(c == 0), float(c == NCORES - 1)
        fl[:, 0], fl[:, 1], fl[:, 2], fl[:, 3], fl[:, 4] = ftop, 1 - ftop, fbot, 1 - fbot, 1.0
        m["flags"] = fl
        sel = np.zeros((128, 4, 17), np.float32)
        sel[:, 0, 2 * c] = 1.0
        sel[:, 1, 2 * c + 2] = 1.0
        sel[:, 2, c] = 1.0
        sel[:, 3, c + 1] = 1.0
        m["sel"] = sel
        maps.append(m)
    return maps


TT_D = [(-6, False), (-4, False), (-2, False), (0, False), (2, False), (4, False), (6, False), (-4, True), (4, True)]


def make_tt(rpb):
    kc = np.arange(64)[:, None]
    qc = np.arange(64)[None, :]
    cs = np.clip(qc - 8, 0, 48)
    colok = (kc >= cs) & (kc < cs + 16)
    cidx = np.clip(kc - qc + 15, 0, 30)
    out = np.full((NH, 128, 9, 128), NEG, np.float32)
    for ti, (d, generic) in enumerate(TT_D):
        for kr in range(2):
            for qr in range(2):
                dr = d + kr - qr
                if dr < -7 or dr > 7:
                    continue
                if generic and not (-4 <= dr <= 3):
                    continue
                blk = np.where(colok[None], rpb[:, dr + 7][:, cidx], NEG)
                out[:, kr * 64:(kr + 1) * 64, ti, qr * 64:(qr + 1) * 64] = blk
    return out


ALL_PHASES = ("proj", "lru1", "lru2", "attn", "gate", "wout", "ffn")
_NC_CACHE = {}


def kernel(**inputs):
    maps = make_inputs(inputs)
    if "nc" not in _NC_CACHE:
        _NC_CACHE["nc"] = build(phases=ALL_PHASES)
    nc, used = _NC_CACHE["nc"]
    maps = [{n: m[n] for n in used} for m in maps]
    res = run_bass_kernel_spmd(nc, maps, core_ids=list(range(NCORES)))
    yp = np.zeros((1, LP, D), np.float32)
    ys = np.zeros((1, LS, D), np.float32)
    for c in range(NCORES):
        y = res.results[c]["y"]
        yp[0, c * OWN_P:(c + 1) * OWN_P] = y[:OWN_P]
        ys[0, c * OWN_S:(c + 1) * OWN_S] = y[OWN_P:]
    return yp, ys
```

```python
import contextlib
import numpy as np
import concourse.bass as bass
import concourse.mybir as mybir
from concourse.bass_utils import run_bass_kernel_spmd

F32 = mybir.dt.float32
BF16 = mybir.dt.bfloat16
AF = mybir.ActivationFunctionType
ALU = mybir.AluOpType

NCORES = 8
D = 4096
KC = D // 128
LP, LS = 16384, 8192
OWN_P, OWN_S = LP // NCORES, LS // NCORES
OWN = OWN_P + OWN_S
EXT_P, EXT_S = OWN_P + 512, OWN_S + 512
EXT = EXT_P + EXT_S
NAW = 2048
LRW = 2048
DFF = 11008
NH = 16
EPS = 1e-6
SEM_LIMIT = 20000
NEG = -30000.0


class Buf:
    ALL = []

    def __init__(self, name, dram=False):
        if not dram:
            Buf.ALL.append(self)
        self.name = name
        self.dram = dram
        self.w = {}
        self.r = {}
        self.dsem = None
        self.dcount = 0


def _merge(d, ev):
    if ev is None:
        return
    k = id(ev[0])
    if k not in d or d[k][1] < ev[1]:
        d[k] = ev


class TR:
    def __init__(self, nc, stack):
        self.nc = nc
        self.stack = stack
        self.engs = {"pe": nc.tensor, "act": nc.scalar, "dve": nc.vector, "pool": nc.gpsimd, "sp": nc.sync}
        self.cur = {}
        self.seen = {e: {} for e in self.engs}
        self.nsem = 0
        self.keep = []
        self.last_swdge = None

    def new_sem(self, name):
        self.nsem += 1
        s = self.stack.enter_context(self.nc.semaphore(f"{name}_{self.nsem}"))
        self.keep.append(s)
        return s

    def wait(self, e, ev):
        if ev is None:
            return
        s, v = ev
        d = self.seen[e]
        if d.get(id(s), 0) >= v:
            return
        self.engs[e].wait_ge(s, v)
        d[id(s)] = v

    def wait_all(self, e, evs):
        for ev in list(evs.values()):
            self.wait(e, ev)

    def inc(self, e, ins):
        st = self.cur.get(e)
        if st is None or st[1] >= SEM_LIMIT:
            st = [self.new_sem("c" + e), 0]
            self.cur[e] = st
        st[1] += 1
        ins.then_inc(st[0], 1)
        return (st[0], st[1])

    def deps(self, e, reads, writes):
        for b in reads:
            self.wait_all(e, b.w)
        for b in writes:
            if not b.dram:
                self.wait_all(e, b.w)
                self.wait_all(e, b.r)

    def done(self, ev, reads, writes):
        for b in reads:
            if not b.dram:
                _merge(b.r, ev)
        for b in writes:
            if b.dram:
                _merge(b.w, ev)
            else:
                b.w = {id(ev[0]): ev}
                b.r = {}

    def op(self, e, reads, writes, fn):
        self.deps(e, reads, writes)
        ins = fn(self.engs[e])
        ev = self.inc(e, ins)
        self.done(ev, reads, writes)
        return ev

    def pe_group(self, reads, writes, fns):
        self.deps("pe", reads, writes)
        ins = None
        for fn in fns:
            ins = fn(self.engs["pe"])
        ev = self.inc("pe", ins)
        self.done(ev, reads, writes)
        return ev

    def dma(self, q, out_ap, in_ap, reads, writes, slot):
        self.deps(q, reads, writes)
        if q == "pool" and self.last_swdge is not None:
            self.wait("pool", self.last_swdge)
        if slot.dsem is None:
            slot.dsem = self.new_sem("d")
        slot.dcount += 16
        self.engs[q].dma_start(out=out_ap, in_=in_ap).then_inc(slot.dsem, 16)
        ev = (slot.dsem, slot.dcount)
        if q == "pool":
            self.last_swdge = ev
        self.done(ev, reads, writes)
        return ev


class K:
    pass


def barrier(k):
    tr = k.tr
    evs = {}
    for b in Buf.ALL:
        for ev in list(b.w.values()) + list(b.r.values()):
            _merge(evs, ev)
    for e in tr.engs:
        tr.wait_all(e, evs)
    Buf.ALL[:] = list(k.psb) + [k.b_ident, k.b_flags, k.b_eps] + [b for b in (getattr(k, 'ssq1_b', None), getattr(k, 'ssq2_b', None), getattr(k, 'sumA_b', None), getattr(k, 'sumH_b', None)) if b]


def build(debug_outs=(), phases=("proj",)):
    Buf.ALL[:] = []
    nc = bass.Bass("TRN2", target_bir_lowering=False)
    k = K()
    k.nc = nc
    dt_in = lambda name, shape, dt=F32: nc.dram_tensor(name, list(shape), dt, kind="ExternalInput").ap()

    def scratch(name, shape, dt):
        kind = "ExternalOutput" if name in debug_outs else "Internal"
        return nc.dram_tensor(name, list(shape), dt, kind=kind).ap()

    shapes = {
        "xp": [LP, D], "xs": [LS, D], "xo": [EXT, D], "w_in": [D, 10240], "w_merge": [D, 2 * D],
        "w_na": [NAW, D], "w_lru": [LRW, D], "w_out": [D, D], "w_fg": [D, DFF], "w_fu": [D, DFF],
        "w_fd": [DFF, D], "g4": [4, D], "b_merge": [128, 64], "convp": [128, 16, 5],
        "wgate": [128, 64, 128], "gatep": [128, 16, 6], "tt": [NH, 128, 9, 128], "flags": [128, 8],
        "sel": [128, 4, 17], "ident": [128, 128],
    }
    k.used = {}

    def din(name):
        if name not in k.used:
            k.used[name] = dt_in(name, shapes[name])
        return k.used[name]
    k.din = din
    k.y = nc.dram_tensor("y", [OWN, D], F32, kind="ExternalOutput").ap()

    k.QT = scratch("QT", [NAW, EXT], BF16)
    k.KT = scratch("KT", [NAW, EXT], BF16)
    k.VS = scratch("VS", [EXT, NAW], BF16)
    k.XR = scratch("XR", [LRW, EXT], F32)
    k.GR = scratch("GR", [LRW, EXT], F32)
    k.GT = scratch("GT", [2 * D, EXT], BF16)
    k.NA = scratch("NA", [NAW, OWN], BF16)
    k.LR = scratch("LR", [LRW, OWN], BF16)
    k.GD = scratch("GD", [D, OWN], BF16)
    k.MIX = scratch("MIX", [OWN, D], F32)
    k.X1 = scratch("X1", [OWN, D], F32)
    k.XN2 = scratch("XN2", [D, OWN], BF16)
    k.HH = scratch("HH", [DFF, OWN], BF16)
    k.FF = scratch("FF", [OWN, D], F32)

    with contextlib.ExitStack() as stack:
        tr = TR(nc, stack)
        k.tr = tr
        k.stack = stack
        sb = lambda name, shape, dt: stack.enter_context(nc.sbuf_tensor(name, list(shape), dt))
        k.sb = sb
        k.sb_glob = sb
        k.ps = [stack.enter_context(nc.psum_tensor(f"ps{i}", [128, 512], F32)) for i in range(8)]
        k.psb = [Buf(f"ps{i}") for i in range(8)]
        k.bank_rr = 0
        k.ident_f = sb("ident_f", [128, 128], F32)
        k.ident = sb("ident_b", [128, 128], BF16)
        k.b_ident = Buf("ident")
        k.flags_t = sb("flags_sb", [128, 8], F32)
        k.b_flags = Buf("flags")
        k.d_in = Buf("inputs", dram=True)
        tr.dma("sp", k.ident_f[:], k.din("ident")[:, :], [k.d_in], [k.b_ident], k.b_ident)
        tr.op("dve", [k.b_ident], [k.b_ident], lambda e: e.tensor_copy(out=k.ident[:], in_=k.ident_f[:]))
        tr.dma("sp", k.flags_t[:], k.din("flags")[:, :], [k.d_in], [k.b_flags], k.b_flags)
        k.eps_t = sb("eps_t", [128, 2], F32)
        k.b_eps = Buf("eps")
        tr.op("dve", [], [k.b_eps], lambda e: e.memset(k.eps_t[:, 0:1], EPS))
        tr.op("dve", [k.b_eps], [k.b_eps], lambda e: e.memset(k.eps_t[:, 1:2], 1.0))

        k.dbufs = {}
        for nm in ("QT", "KT", "VS", "XR", "GR", "GT", "NA", "LR", "GD", "MIX", "X1", "XN2", "HH", "FF", "y"):
            k.dbufs[nm] = Buf(nm, dram=True)
            setattr(k, nm + "_b", k.dbufs[nm])
        k.ssq1 = sb("ssq1", [128, 384], F32); k.ssq1_b = Buf("ssq1")
        k.ssq2 = sb("ssq2", [128, 384], F32); k.ssq2_b = Buf("ssq2")
        tr.op("dve", [], [k.ssq1_b], lambda e: e.memset(k.ssq1[:], 0.0))
        tr.op("dve", [], [k.ssq2_b], lambda e: e.memset(k.ssq2[:], 0.0))
        need = []
        for ph, nms in (("proj", ("w_in", "w_merge")), ("lru1", ("w_xr",)), ("gate", ("w_na", "w_lru")),
                        ("wout", ("w_out",)), ("ffn", ("w_fg", "w_fu", "w_fd"))):
            if ph in phases:
                need += [n for n in nms if n not in need]
        phase_cast(k, need)
        barrier(k)
        if "proj" in phases:
            phase_proj(k)
            barrier(k)
        if "lru1" in phases:
            phase_lru_summ(k)
            barrier(k)
        if "lru2" in phases:
            phase_lru_own(k)
            barrier(k)
        if "attn" in phases:
            phase_attn(k)
            barrier(k)
        if "gate" in phases:
            phase_gate(k)
            barrier(k)
        if "wout" in phases:
            phase_wout(k)
            barrier(k)
            import os
            if not os.environ.get("WO_SKIP_NORM"):
              norm_resid(k, "n1", k.MIX, k.MIX_b, k.ssq1, k.ssq1_b, 1,
                       lambda i: (k.din("xo")[own_ext(i * 128):own_ext(i * 128) + 128, :], k.d_in), k.X1, k.X1_b)
            barrier(k)
        if "ffn" in phases:
            phase_ffn_up(k)
            barrier(k)
            phase_ffn_down(k)
            barrier(k)
            norm_resid(k, "n2", k.FF, k.FF_b, k.ssq2, k.ssq2_b, 3,
                       lambda i: (k.X1[i * 128:(i + 1) * 128, :], k.X1_b), k.y, k.y_b)
            barrier(k)

        for b in k.dbufs.values():
            tr.wait_all("sp", b.w)
    return nc, sorted(k.used)


_evac_rr = [0]


def evac_engine():
    _evac_rr[0] ^= 1
    return "act" if _evac_rr[0] else "dve"


def copy_op(k, e, out_ap, in_ap, reads, writes):
    if e == "act":
        return k.tr.op("act", reads, writes, lambda g: g.activation(out=out_ap, in_=in_ap, func=AF.Copy))
    return k.tr.op(e, reads, writes, lambda g: g.tensor_copy(out=out_ap, in_=in_ap))


class RmsT:
    def __init__(self, k, pfx, g_row, nx=2):
        sb, tr = k.sb, k.tr
        self.nx = nx
        self.xt = [sb(f"{pfx}_x{i}", [128, D], F32) for i in range(nx)]
        self.xt_b = [Buf(f"{pfx}_x{i}") for i in range(nx)]
        self.xn = sb(f"{pfx}_xn", [128, D], BF16)
        self.xn_b = Buf(f"{pfx}_xn")
        self.gbc = sb(f"{pfx}_gbc", [128, D], F32)
        self.gbc_b = Buf(f"{pfx}_gbc")
        self.st = sb(f"{pfx}_st", [128, 4], F32)
        self.st_b = Buf(f"{pfx}_st")
        self.i = 0
        tr.dma("sp", self.gbc[:], k.din("g4")[g_row:g_row + 1, :].broadcast_to([128, D]), [k.d_in], [self.gbc_b], self.gbc_b)


def rms_tile(k, rt, src_ap, src_bufs, dstT, dstT_b, col0, ncols=128, zero_first=False, srcs=None):
    tr = k.tr
    i = rt.i
    rt.i = (rt.i + 1) % rt.nx
    xt, xb = rt.xt[i], rt.xt_b[i]
    if zero_first:
        tr.op("dve", [], [xb], lambda e: e.memset(xt[:], 0.0))
    if src_ap is not None:
        tr.dma("sp", xt[:, :], src_ap, src_bufs, [xb], xb)
    for (r0, nr, ap) in (srcs or []):
        tr.dma("sp", xt[r0:r0 + nr, :], ap, src_bufs, [xb], xb)
    tr.op("act", [xb], [rt.xn_b, rt.st_b],
          lambda e: e.activation(out=rt.xn[:], in_=xt[:], func=AF.Square, accum_out=rt.st[:, 0:1]))
    tr.op("act", [rt.st_b, k.b_eps], [rt.st_b],
          lambda e: e.activation(out=rt.st[:, 1:2], in_=rt.st[:, 0:1], func=AF.Sqrt, scale=1.0 / D, bias=k.eps_t[:, 0:1]))
    tr.op("dve", [rt.st_b], [rt.st_b], lambda e: e.reciprocal(out=rt.st[:, 2:3], in_=rt.st[:, 1:2]))
    tr.op("dve", [xb, rt.st_b, rt.gbc_b], [rt.xn_b],
          lambda e: e.scalar_tensor_tensor(out=rt.xn[:], in0=xt[:], scalar=rt.st[:, 2:3], in1=rt.gbc[:],
                                           op0=ALU.mult, op1=ALU.mult))
    for g in range(4):
        bi = k.bank_rr
        k.bank_rr = (k.bank_rr + 1) % 8
        pb = k.ps[bi][:].bitcast(BF16)
        fns = []
        for j in range(8):
            kc = g * 8 + j
            fns.append(lambda e, kc=kc, j=j: e.transpose(out=pb[:, j * 128:(j + 1) * 128],
                                                         in_=rt.xn[:, kc * 128:(kc + 1) * 128],
                                                         identity=k.ident[:]))
        tr.pe_group([rt.xn_b, k.b_ident], [k.psb[bi]], fns)
        copy_op(k, evac_engine(), dstT[:, g * 8:(g + 1) * 8, col0:col0 + ncols],
                pb.rearrange("p (j t) -> p j t", j=8)[:, :, 0:ncols], [k.psb[bi]], [dstT_b])


class Slabs:
    def __init__(self, k, pfx, kc, cols, n=2):
        self.t = [k.sb(f"{pfx}{i}", [128, kc, cols], BF16) for i in range(n)]
        self.b = [Buf(f"{pfx}{i}") for i in range(n)]
        self.i = 0
        self.n = n

    def next(self):
        i = self.i
        self.i = (self.i + 1) % self.n
        return self.t[i], self.b[i]


def load_slab(k, slabs, name, kg, cg, nkc=None):
    t, b = slabs.next()
    ap, buf, wnkc, gc = k.wb[name]
    nkc = nkc or wnkc
    k.tr.dma("pool", t[:, 0:nkc, 0:gc], ap[kg, cg, :, 0:nkc, :], [buf], [b], b)
    return t, b


WB_SPECS = {
    "w_in": (D, 10240, 32, 256), "w_merge": (D, 2 * D, 32, 256), "w_na": (NAW, D, 16, 256), "w_lru": (LRW, D, 16, 256),
    "w_out": (D, D, 32, 512), "w_fg": (D, DFF, 32, 256), "w_fu": (D, DFF, 32, 256), "w_fd": (DFF, D, 8, 512),
    "w_xr": (D, LRW, 32, 128),
}
WB_SRC = {"w_xr": ("w_in", 6144)}


def phase_cast(k, names):
    tr = k.tr
    k.wb = {}
    with phase_scope(k):
        sb = k.sb
        fst = Stage(k, "cs_f", [128, 4096], F32, n=3)
        bst = Stage(k, "cs_b", [128, 4096], BF16, n=3)
        items = []
        for nm in names:
            K_, N_, nkc, gc = WB_SPECS[nm]
            tkc = K_ // 128
            n_kg = -(-tkc // nkc)
            n_cg = N_ // gc
            ap = k.nc.dram_tensor("wb_" + nm, [n_kg, n_cg, 128, nkc, gc], BF16, kind="Internal").ap()
            buf = Buf("wb_" + nm, dram=True)
            k.wb[nm] = (ap, buf, nkc, gc)
            k.dbufs["wb_" + nm] = buf
            src, sc0 = WB_SRC.get(nm, (nm, 0))
            wv = k.din(src).rearrange("(kc p) n -> p kc n", p=128)
            for kg in range(n_kg):
                nk = min(nkc, tkc - kg * nkc)
                for cg in range(n_cg):
                    for q in range(0, nk, 8):
                        items.append((wv, ap, buf, kg, cg, q, min(8, nk - q), nkc, gc, sc0))
        LA = 2
        pend = {}
        engs = ("dve", "act", "pool", "act", "dve")
        for i in range(len(items) + LA):
            if i < len(items):
                (wv, ap, buf, kg, cg, q, m, nkc, gc, sc0) = items[i]
                f, f_b = fst.next()
                fv = f[:].rearrange("p (a n) -> p a n", n=gc)
                tr.dma("sp", fv[:, 0:m, :], wv[:, kg * nkc + q:kg * nkc + q + m, sc0 + cg * gc:sc0 + (cg + 1) * gc],
                       [k.d_in], [f_b], f_b)
                pend[i] = (f, f_b)
            j = i - LA
            if j >= 0:
                (wv, ap, buf, kg, cg, q, m, nkc, gc, sc0) = items[j]
                f, f_b = pend.pop(j)
                o, o_b = bst.next()
                copy_op(k, engs[j % len(engs)], o[:, 0:m * gc], f[:, 0:m * gc], [f_b], [o_b])
                ov = o[:].rearrange("p (a n) -> p a n", n=gc)
                tr.dma("sp", ap[kg, cg, :, q:q + m, :], ov[:, 0:m, :], [o_b], [buf], o_b)


def linear_F(k, xT, xT_b, nkc, T, wname, col0, ncols, slabs, epilogue, gcols=256):
    tr = k.tr
    nh = T // 512
    for c0 in range(col0, col0 + ncols, gcols):
        gc = min(gcols, col0 + ncols - c0)
        assert gc == gcols and c0 % gcols == 0
        slab, slab_b = load_slab(k, slabs, wname, 0, c0 // gcols)
        nfc = gc // 128
        for h in range(nh):
            banks = []
            for fc in range(nfc):
                bi = k.bank_rr
                k.bank_rr = (k.bank_rr + 1) % 8
                banks.append(bi)
            fns = []
            for kc in range(nkc):
                for fc in range(nfc):
                    fns.append(lambda e, kc=kc, fc=fc: e.matmul(
                        k.ps[banks[fc]][:], lhsT=slab[:, kc, fc * 128:(fc + 1) * 128],
                        rhs=xT[:, kc, h * 512:(h + 1) * 512], start=(kc == 0), stop=(kc == nkc - 1)))
            tr.pe_group([xT_b, slab_b], [k.psb[b] for b in banks], fns)
            for fc in range(nfc):
                epilogue((c0 + fc * 128) // 128, h, banks[fc])


def linear_T(k, xT, xT_b, nkc, T, wname, col0, ncols, slabs, epilogue, gcols=256):
    tr = k.tr
    nst = T // 128
    for c0 in range(col0, col0 + ncols, gcols):
        gc = min(gcols, col0 + ncols - c0)
        assert gc == gcols and c0 % gcols == 0
        slab, slab_b = load_slab(k, slabs, wname, 0, c0 // gcols)
        for s0 in range(0, nst, 4):
            banks = []
            for st in range(4):
                bi = k.bank_rr
                k.bank_rr = (k.bank_rr + 1) % 8
                banks.append(bi)
            fns = []
            for kc in range(nkc):
                for st in range(4):
                    fns.append(lambda e, kc=kc, st=st: e.matmul(
                        k.ps[banks[st]][:, 0:gc], lhsT=xT[:, kc, (s0 + st) * 128:(s0 + st + 1) * 128],
                        rhs=slab[:, kc, 0:gc], start=(kc == 0), stop=(kc == nkc - 1)))
            tr.pe_group([xT_b, slab_b], [k.psb[b] for b in banks], fns)
            for st in range(4):
                epilogue(c0, gc, s0 + st, banks[st])


class Stage:
    def __init__(self, k, pfx, shape, dt, n=2):
        self.t = [k.sb(f"{pfx}{i}", shape, dt) for i in range(n)]
        self.b = [Buf(f"{pfx}{i}") for i in range(n)]
        self.i = 0
        self.n = n

    def next(self):
        i = self.i
        self.i = (self.i + 1) % self.n
        return self.t[i], self.b[i]


def phase_proj(k):
    tr, sb = k.tr, k.sb
    with contextlib.ExitStack() as ph:
        old_stack, old_sb = k.stack, k.sb
        k.sb = lambda name, shape, dt: ph.enter_context(k.nc.sbuf_tensor(name, list(shape), dt))
        sb = k.sb
        rt = RmsT(k, "pj", 0)
        xnT = sb("pj_xnT", [128, KC, 1024], BF16)
        xnT_b = Buf("pj_xnT")
        slabs = Slabs(k, "pj_slab", KC, 256, n=2)
        stg_f = Stage(k, "pj_sf", [128, 512], F32, n=3)
        stg_h = Stage(k, "pj_sh", [128, 512], BF16, n=3)
        bm = sb("pj_bm", [128, 64], F32)
        bm_b = Buf("pj_bm")
        tr.dma("sp", bm[:], k.din("b_merge")[:, :], [k.d_in], [bm_b], bm_b)

        import os
        for blk in range(int(os.environ.get('PJ_BLOCKS', EXT // 1024))):
            t0 = blk * 1024
            stages = os.environ.get('PJ_STAGES', 'qkvxgm')
            for j in range(8):
                rms_tile(k, rt, k.din("xo")[t0 + j * 128:t0 + (j + 1) * 128, :], [k.d_in], xnT, xnT_b, j * 128)

            def ep_f(dst, dst_b, dt_is_f32, sigm_bias=None):
                def ep(f, h, bank, dst=dst, dst_b=dst_b):
                    stg = stg_f if dt_is_f32 else stg_h
                    t, b = stg.next()
                    if sigm_bias is not None:
                        tr.op("act", [k.psb[bank], bm_b], [b],
                              lambda e: e.activation(out=t[:], in_=k.ps[bank][:], func=AF.Sigmoid,
                                                     bias=bm[:, sigm_bias + f:sigm_bias + f + 1], scale=1.0))
                    else:
                        copy_op(k, evac_engine(), t[:], k.ps[bank][:], [k.psb[bank]], [b])
                    tr.dma("sp", dst[f * 128:(f + 1) * 128, t0 + h * 512:t0 + (h + 1) * 512], t[:], [b], [dst_b], b)
                return ep

            if 'q' in stages:
                linear_F(k, xnT, xnT_b, KC, 1024, "w_in", 0, int(os.environ.get('PJ_QCOLS', 2048)), slabs, ep_f(k.QT, k.QT_b, False))
            kt_ep = ep_f(k.KT, k.KT_b, False)
            if 'k' in stages:
                linear_F(k, xnT, xnT_b, KC, 1024, "w_in", 2048, 2048, slabs,
                         lambda f, h, bank: kt_ep(f - 16, h, bank))
            def ep_v(c0, gc, st, bank):
                t, b = stg_h.next()
                copy_op(k, evac_engine(), t[:, 0:gc], k.ps[bank][:, 0:gc], [k.psb[bank]], [b])
                tr.dma("sp", k.VS[t0 + st * 128:t0 + (st + 1) * 128, c0 - 4096:c0 - 4096 + gc], t[:, 0:gc],
                       [b], [k.VS_b], b)
            if 'v' in stages:
                linear_T(k, xnT, xnT_b, KC, 1024, "w_in", 4096, 2048, slabs, ep_v)
            xr_ep = ep_f(k.XR, k.XR_b, True)
            if 'x' in stages:
                linear_F(k, xnT, xnT_b, KC, 1024, "w_in", 6144, 2048, slabs, lambda f, h, bank: xr_ep(f - 48, h, bank))
            gr_ep = ep_f(k.GR, k.GR_b, True)
            if 'g' in stages:
                linear_F(k, xnT, xnT_b, KC, 1024, "w_in", 8192, 2048, slabs, lambda f, h, bank: gr_ep(f - 64, h, bank))
            if 'm' in stages:
                linear_F(k, xnT, xnT_b, KC, 1024, "w_merge", 0, 2 * D, slabs, ep_f(k.GT, k.GT_b, False, sigm_bias=0))
        k.sb = old_sb


def own_ext(o):
    return 256 + o if o < OWN_P else EXT_P + 256 + (o - OWN_P)


@contextlib.contextmanager
def phase_scope(k):
    with contextlib.ExitStack() as ph:
        old = k.sb
        k.sb = lambda name, shape, dt: ph.enter_context(k.nc.sbuf_tensor(name, list(shape), dt))
        try:
            yield
        finally:
            k.sb = old


def load_T(k, dst, dst_b, src_ap, src_b, nkc, c0, n):
    sv = src_ap.rearrange("(kc p) t -> p kc t", p=128)
    for q in range(0, nkc, 8):
        m = min(8, nkc - q)
        k.tr.dma("sp", dst[:, q:q + m, 0:n], sv[:, q:q + m, c0:c0 + n], [src_b], [dst_b], dst_b)


def phase_gate(k):
    tr = k.tr
    with phase_scope(k):
        sb = k.sb
        naT = sb("gt_naT", [128, 16, 1024], BF16); naT_b = Buf("gt_naT")
        lrT = sb("gt_lrT", [128, 16, 1024], BF16); lrT_b = Buf("gt_lrT")
        slabs = Slabs(k, "gt_slab", 16, 256, n=4)
        gts = Stage(k, "gt_g", [128, 2, 512], BF16, n=4)
        tmp = Stage(k, "gt_tmp", [128, 2, 512], F32, n=3)
        outs = Stage(k, "gt_out", [128, 512], BF16, n=3)
        for blk in range(OWN // 1024):
            o0 = blk * 1024
            e0 = own_ext(o0)
            load_T(k, naT, naT_b, k.NA, k.NA_b, 16, o0, 1024)
            load_T(k, lrT, lrT_b, k.LR, k.LR_b, 16, o0, 1024)
            for c0 in range(0, D, 256):
                s_na, s_na_b = load_slab(k, slabs, "w_na", 0, c0 // 256)
                s_lr, s_lr_b = load_slab(k, slabs, "w_lru", 0, c0 // 256)
                for h in range(2):
                    banks = []
                    for _ in range(4):
                        banks.append(k.bank_rr)
                        k.bank_rr = (k.bank_rr + 1) % 8
                    fns = []
                    for kc in range(16):
                        for fc in range(2):
                            fns.append(lambda e, kc=kc, fc=fc: e.matmul(
                                k.ps[banks[fc]][:], lhsT=s_na[:, kc, fc * 128:(fc + 1) * 128],
                                rhs=naT[:, kc, h * 512:(h + 1) * 512], start=(kc == 0), stop=(kc == 15)))
                    for kc in range(16):
                        for fc in range(2):
                            fns.append(lambda e, kc=kc, fc=fc: e.matmul(
                                k.ps[banks[2 + fc]][:], lhsT=s_lr[:, kc, fc * 128:(fc + 1) * 128],
                                rhs=lrT[:, kc, h * 512:(h + 1) * 512], start=(kc == 0), stop=(kc == 15)))
                    tr.pe_group([naT_b, lrT_b, s_na_b, s_lr_b], [k.psb[b] for b in banks], fns)
                    for fc in range(2):
                        f = c0 // 128 + fc
                        g, g_b = gts.next()
                        tr.dma("sp", g[:, 0, :], k.GT[f * 128:(f + 1) * 128, e0 + h * 512:e0 + (h + 1) * 512],
                               [k.GT_b], [g_b], g_b)
                        tr.dma("sp", g[:, 1, :], k.GT[(32 + f) * 128:(33 + f) * 128, e0 + h * 512:e0 + (h + 1) * 512],
                               [k.GT_b], [g_b], g_b)
                        t, t_b = tmp.next()
                        tr.op("dve", [k.psb[banks[fc]], g_b], [t_b],
                              lambda e: e.tensor_tensor(out=t[:, 0, :], in0=k.ps[banks[fc]][:], in1=g[:, 0, :], op=ALU.mult))
                        tr.op("dve", [k.psb[banks[2 + fc]], g_b], [t_b],
                              lambda e: e.tensor_tensor(out=t[:, 1, :], in0=k.ps[banks[2 + fc]][:], in1=g[:, 1, :], op=ALU.mult))
                        o, o_b = outs.next()
                        tr.op("pool", [t_b], [o_b],
                              lambda e: e.tensor_tensor(out=o[:], in0=t[:, 0, :], in1=t[:, 1, :], op=ALU.add))
                        tr.dma("sp", k.GD[f * 128:(f + 1) * 128, o0 + h * 512:o0 + (h + 1) * 512], o[:], [o_b], [k.GD_b], o_b)


class LruCommon:
    def rotate_piece(self):
        self.pi = (self.pi + 1) % self.nb
        for kk, v in self.psets[self.pi].items():
            setattr(self, kk, v)

    def rotate_conv(self):
        self.ci = (self.ci + 1) % self.nb
        for kk, v in self.csets[self.ci].items():
            setattr(self, kk, v)

    def __init__(self, k, pfx, T, nb=1):
        sb, tr = k.sb, k.tr
        self.nb = nb
        self.wgf = None
        self.wg = sb(pfx + "_wg", [128, 64, 128], BF16); self.wg_b = Buf(pfx + "_wg")
        tr.dma("pool", self.wg[:], k.din("wgate")[:, :, :], [k.d_in], [self.wg_b], self.wg_b)
        self.gp = sb(pfx + "_gp", [128, 16, 6], F32); self.gp_b = Buf(pfx + "_gp")
        tr.dma("sp", self.gp[:], k.din("gatep")[:, :, :], [k.d_in], [self.gp_b], self.gp_b)
        self.cp = sb(pfx + "_cp", [128, 16, 5], F32); self.cp_b = Buf(pfx + "_cp")
        tr.dma("sp", self.cp[:], k.din("convp")[:, :, :], [k.d_in], [self.cp_b], self.cp_b)
        self.cs = sb(pfx + "_cs", [128, 16, 2], F32); self.cs_b = Buf(pfx + "_cs")
        tr.op("act", [self.gp_b], [self.cs_b],
              lambda e: e.activation(out=self.cs[:], in_=self.gp[:, :, 4:6], func=AF.Exp, scale=-1.0))
        tr.op("act", [self.cs_b, k.b_eps], [self.cs_b],
              lambda e: e.activation(out=self.cs[:], in_=self.cs[:], func=AF.Ln, bias=k.eps_t[:, 1:2], scale=1.0))
        tr.op("dve", [self.cs_b], [self.cs_b], lambda e: e.tensor_scalar_mul(out=self.cs[:], in0=self.cs[:], scalar1=-8.0))
        self.psets, self.csets = [], []
        for i in range(nb):
            sfx = f"{pfx}_{i}"
            self.csets.append({"xc": sb(sfx + "_xc", [128, T], F32), "xc_b": Buf(sfx + "_xc"),
                               "xcb": sb(sfx + "_xcb", [128, T], BF16), "xcb_b": Buf(sfx + "_xcb")})
            self.psets.append({"A": sb(sfx + "_A", [128, 1024], F32), "A_b": Buf(sfx + "_A"),
                               "B": sb(sfx + "_B", [128, 1024], F32), "B_b": Buf(sfx + "_B"),
                               "C": sb(sfx + "_C", [128, 1024], F32), "C_b": Buf(sfx + "_C"),
                               "sr": sb(sfx + "_sr", [128, 4], F32), "sr_b": Buf(sfx + "_sr")})
        self.pi = self.ci = nb - 1
        self.rotate_piece()
        self.rotate_conv()


def lru_conv(k, L, xp, xp_b, n, T):
    tr = k.tr
    cp = L.cp
    tr.op("dve", [xp_b, L.cp_b], [L.xc_b],
          lambda e: e.tensor_scalar(out=L.xc[:, 0:T], in0=xp[:, 0:T], scalar1=cp[:, n, 0:1], scalar2=cp[:, n, 4:5],
                                    op0=ALU.mult, op1=ALU.add))
    for j in (1, 2, 3):
        tr.op("dve", [xp_b, L.cp_b, L.xc_b], [L.xc_b],
              lambda e: e.scalar_tensor_tensor(out=L.xc[:, 0:T], in0=xp[:, j:j + T], scalar=cp[:, n, j:j + 1],
                                               in1=L.xc[:, 0:T], op0=ALU.mult, op1=ALU.add))
    tr.op("pool", [L.xc_b], [L.xcb_b], lambda e: e.tensor_copy(out=L.xcb[:, 0:T], in_=L.xc[:, 0:T]))


def lru_piece_gen(k, L, W, n, d, t0, first_fix, init_ap, init_bufs, out_tile, out_b, out_col0=0):
    tr = k.tr
    banks = []
    for _ in range(4):
        banks.append(k.bank_rr)
        k.bank_rr = (k.bank_rr + 1) % 8
    fns = []
    for g in range(2):
        for h in range(2):
            fns.append(lambda e, g=g, h=h: e.matmul(
                k.ps[banks[g * 2 + h]][:], lhsT=L.wg[:, g * 32 + d * 16 + n, :],
                rhs=W.xcb[:, t0 + h * 512:t0 + (h + 1) * 512], start=True, stop=True))
    tr.pe_group([L.wg_b, W.xcb_b], [k.psb[b] for b in banks], fns)
    yield
    for h in range(2):
        tr.op("act", [k.psb[banks[h]], L.gp_b], [W.A_b, W.sr_b],
              lambda e: e.activation(out=W.A[:, h * 512:(h + 1) * 512], in_=k.ps[banks[h]][:], func=AF.Sigmoid,
                                     bias=L.gp[:, n, d:d + 1], scale=1.0, accum_out=W.sr[:, h:h + 1]))
    yield
    for h in range(2):
        tr.op("act", [k.psb[banks[2 + h]], L.gp_b], [W.B_b],
              lambda e: e.activation(out=W.B[:, h * 512:(h + 1) * 512], in_=k.ps[banks[2 + h]][:], func=AF.Sigmoid,
                                     bias=L.gp[:, n, 2 + d:3 + d], scale=1.0))
    yield
    tr.op("act", [W.A_b, L.cs_b], [W.A_b],
          lambda e: e.activation(out=W.A[:], in_=W.A[:], func=AF.Exp, scale=L.cs[:, n, d:d + 1]))
    yield
    tr.op("pool", [W.A_b], [W.C_b], lambda e: e.tensor_tensor(out=W.C[:], in0=W.A[:], in1=W.A[:], op=ALU.mult))
    yield
    tr.op("act", [W.C_b, k.b_eps], [W.C_b],
          lambda e: e.activation(out=W.C[:], in_=W.C[:], func=AF.Sqrt, scale=-1.0, bias=k.eps_t[:, 1:2]))
    if first_fix is not None:
        first_fix(W)
    yield
    tr.op("dve", [W.B_b, W.xc_b], [W.B_b],
          lambda e: e.tensor_tensor(out=W.B[:], in0=W.B[:], in1=W.xc[:, t0:t0 + 1024], op=ALU.mult))
    yield
    tr.op("pool", [W.B_b, W.C_b], [W.B_b], lambda e: e.tensor_tensor(out=W.B[:], in0=W.B[:], in1=W.C[:], op=ALU.mult))
    yield
    o = out_tile[:, out_col0:out_col0 + 1024]
    if d == 0:
        tr.op("dve", [W.A_b, W.B_b] + init_bufs, [out_b],
              lambda e: e.tensor_tensor_scan(out=o, data0=W.A[:], data1=W.B[:], initial=init_ap, op0=ALU.mult, op1=ALU.add))
    else:
        tr.op("dve", [W.A_b, W.B_b] + init_bufs, [out_b],
              lambda e: e.tensor_tensor_scan(out=o[:, ::-1], data0=W.A[:, ::-1], data1=W.B[:, ::-1], initial=init_ap,
                                             op0=ALU.mult, op1=ALU.add))


class WS:
    def __init__(self, L):
        for nm in ("A", "B", "C", "sr", "xc", "xcb"):
            setattr(self, nm, getattr(L, nm))
            setattr(self, nm + "_b", getattr(L, nm + "_b"))


def lru_piece(k, L, n, d, t0, first_fix, init_ap, init_bufs, out_tile, out_b, out_col0=0):
    for _ in lru_piece_gen(k, L, WS(L), n, d, t0, first_fix, init_ap, init_bufs, out_tile, out_b, out_col0):
        pass


def phase_lru_summ(k):
    tr = k.tr
    k.sumA = k.sb_glob("sumA", [128, 2, 24, 16], F32); k.sumA_b = Buf("sumA")
    k.sumH = k.sb_glob("sumH", [128, 2, 24, 16], F32); k.sumH_b = Buf("sumH")
    with phase_scope(k):
        sb = k.sb
        rt = RmsT(k, "ls", 0, nx=1)
        xnT = sb("ls_xnT", [128, KC, 1028], BF16); xnT_b = Buf("ls_xnT")
        slabs = Slabs(k, "ls_slab", KC, 128, n=2)
        xps = Stage(k, "ls_xp", [128, 1028], F32, n=2)
        L = LruCommon(k, "ls", 1024, nb=2)
        slots = [("xp", s, 16) for s in range(16)] + [("xs", s, 8) for s in range(8)]
        import os
        nslots = int(os.environ.get("LS_SLOTS", 24))
        def drain(gens):
            gens = list(gens)
            while gens:
                for g in list(gens):
                    try:
                        next(g)
                    except StopIteration:
                        gens.remove(g)

        pending = []
        for si, (nm, s, ns) in enumerate(slots[:nslots]):
            X = k.din(nm)
            base = s * 1024
            for j in range(8):
                rms_tile(k, rt, X[base + j * 128:base + (j + 1) * 128, :], [k.d_in], xnT, xnT_b, j * 128)
            srcs = []
            if s > 0:
                srcs.append((0, 2, X[base - 2:base, :]))
            if s < ns - 1:
                srcs.append((2, 1, X[base + 1024:base + 1025, :]))
            rms_tile(k, rt, None, [k.d_in], xnT, xnT_b, 1024, ncols=3, zero_first=True, srcs=srcs)
            for n in range(16):
                slab, slab_b = load_slab(k, slabs, "w_xr", 0, n)
                hc = 0
                banks = []
                for _ in range(3):
                    banks.append(k.bank_rr)
                    k.bank_rr = (k.bank_rr + 1) % 8
                fns = []
                for kc in range(KC):
                    for h in range(2):
                        fns.append(lambda e, kc=kc, h=h: e.matmul(
                            k.ps[banks[h]][:], lhsT=slab[:, kc, hc:hc + 128], rhs=xnT[:, kc, h * 512:(h + 1) * 512],
                            start=(kc == 0), stop=(kc == KC - 1)))
                    fns.append(lambda e, kc=kc: e.matmul(
                        k.ps[banks[2]][:, 0:3], lhsT=slab[:, kc, hc:hc + 128], rhs=xnT[:, kc, 1024:1027],
                        start=(kc == 0), stop=(kc == KC - 1)))
                tr.pe_group([xnT_b, slab_b], [k.psb[b] for b in banks], fns)
                xp, xp_b = xps.next()
                copy_op(k, "act", xp[:, 2:514], k.ps[banks[0]][:], [k.psb[banks[0]]], [xp_b])
                copy_op(k, "dve", xp[:, 514:1026], k.ps[banks[1]][:], [k.psb[banks[1]]], [xp_b])
                copy_op(k, "dve", xp[:, 0:2], k.ps[banks[2]][:, 0:2], [k.psb[banks[2]]], [xp_b])
                copy_op(k, "dve", xp[:, 1026:1027], k.ps[banks[2]][:, 2:3], [k.psb[banks[2]]], [xp_b])
                L.rotate_conv()
                lru_conv(k, L, xp, xp_b, n, 1024)
                drain(pending)
                pending = []
                for d in range(2):
                    L.rotate_piece()

                    def gen(W=WS(L), d=d, n=n, si=si, s=s, ns=ns):
                        fix = None
                        if (d == 0 and s == 0) or (d == 1 and s == ns - 1):
                            col = 0 if d == 0 else 1023
                            fix = lambda W, col=col: tr.op("dve", [], [W.C_b], lambda e: e.memset(W.C[:, col:col + 1], 1.0))
                        yield from lru_piece_gen(k, L, W, n, d, 0, fix, 0.0, [], W.C, W.C_b)
                        hcol = 1023 if d == 0 else 0
                        tr.op("dve", [W.C_b], [k.sumH_b],
                              lambda e: e.tensor_copy(out=k.sumH[:, d, si, n:n + 1], in_=W.C[:, hcol:hcol + 1]))
                        tr.op("dve", [W.sr_b], [W.sr_b],
                              lambda e: e.tensor_tensor(out=W.sr[:, 2:3], in0=W.sr[:, 0:1], in1=W.sr[:, 1:2], op=ALU.add))
                        tr.op("act", [W.sr_b, L.cs_b], [k.sumA_b],
                              lambda e: e.activation(out=k.sumA[:, d, si, n:n + 1], in_=W.sr[:, 2:3], func=AF.Exp,
                                                     scale=L.cs[:, n, d:d + 1]))
                    pending.append(gen())
            drain(pending)
            pending = []


def phase_lru_own(k):
    tr = k.tr
    with phase_scope(k):
        sb = k.sb
        selt = sb("lo_sel", [128, 4, 17], F32); sel_b = Buf("lo_sel")
        tr.dma("sp", selt[:], k.din("sel")[:, :, :], [k.d_in], [sel_b], sel_b)
        car = sb("lo_car", [128, 4, 16], F32); car_b = Buf("lo_car")
        S = sb("lo_S", [128, 16], F32); S_b = Buf("lo_S")
        tr.op("dve", [], [car_b], lambda e: e.memset(car[:], 0.0))
        for (q, slot0, nb) in ((0, 0, 16), (1, 16, 8)):
            tr.op("dve", [], [S_b], lambda e: e.memset(S[:], 0.0))
            for j in range(nb):
                tr.op("dve", [S_b, k.sumA_b], [S_b], lambda e: e.tensor_tensor(out=S[:], in0=S[:], in1=k.sumA[:, 0, slot0 + j, :], op=ALU.mult))
                tr.op("dve", [S_b, k.sumH_b], [S_b], lambda e: e.tensor_tensor(out=S[:], in0=S[:], in1=k.sumH[:, 0, slot0 + j, :], op=ALU.add))
                tr.op("dve", [S_b, sel_b, car_b], [car_b],
                      lambda e: e.scalar_tensor_tensor(out=car[:, 2 * q, :], in0=S[:], scalar=selt[:, 2 * q, j + 1:j + 2],
                                                       in1=car[:, 2 * q, :], op0=ALU.mult, op1=ALU.add))
            tr.op("dve", [], [S_b], lambda e: e.memset(S[:], 0.0))
            for j in range(nb - 1, -1, -1):
                tr.op("dve", [S_b, k.sumA_b], [S_b], lambda e: e.tensor_tensor(out=S[:], in0=S[:], in1=k.sumA[:, 1, slot0 + j, :], op=ALU.mult))
                tr.op("dve", [S_b, k.sumH_b], [S_b], lambda e: e.tensor_tensor(out=S[:], in0=S[:], in1=k.sumH[:, 1, slot0 + j, :], op=ALU.add))
                tr.op("dve", [S_b, sel_b, car_b], [car_b],
                      lambda e: e.scalar_tensor_tensor(out=car[:, 2 * q + 1, :], in0=S[:], scalar=selt[:, 2 * q + 1, j:j + 1],
                                                       in1=car[:, 2 * q + 1, :], op0=ALU.mult, op1=ALU.add))
        L = LruCommon(k, "lo", 2048)
        xps = Stage(k, "lo_xp", [128, 2052], F32, n=2)
        grs = Stage(k, "lo_gr", [128, 2048], F32, n=2)
        HF = sb("lo_HF", [128, 2048], F32); HF_b = Buf("lo_HF")
        HR = sb("lo_HR", [128, 1024], F32); HR_b = Buf("lo_HR")
        G1 = sb("lo_G1", [128, 1024], F32); G1_b = Buf("lo_G1")
        G2 = sb("lo_G2", [128, 1024], F32); G2_b = Buf("lo_G2")
        outs = Stage(k, "lo_out", [128, 1024], BF16, n=2)
        tmp1 = sb("lo_t1", [128, 2], F32); tmp1_b = Buf("lo_t1")
        for n in range(16):
            for (q, e0, T, ob) in ((0, 256, OWN_P, 0), (1, EXT_P + 256, OWN_S, OWN_P)):
                xp, xp_b = xps.next()
                tr.dma("sp", xp[:, 0:T + 3], k.XR[n * 128:(n + 1) * 128, e0 - 2:e0 + T + 1], [k.XR_b], [xp_b], xp_b)
                gr, gr_b = grs.next()
                tr.dma("sp", gr[:, 0:T], k.GR[n * 128:(n + 1) * 128, e0:e0 + T], [k.GR_b], [gr_b], gr_b)
                lru_conv(k, L, xp, xp_b, n, T)
                npc = T // 1024

                def mkfix(col, fcol):
                    def fix(L):
                        tr.op("dve", [L.C_b], [tmp1_b],
                              lambda e: e.tensor_scalar(out=tmp1[:, 0:1], in0=L.C[:, col:col + 1], scalar1=-1.0, scalar2=1.0,
                                                        op0=ALU.mult, op1=ALU.add))
                        tr.op("dve", [tmp1_b, k.b_flags, L.C_b], [L.C_b],
                              lambda e: e.scalar_tensor_tensor(out=L.C[:, col:col + 1], in0=tmp1[:, 0:1],
                                                               scalar=k.flags_t[:, fcol:fcol + 1], in1=L.C[:, col:col + 1],
                                                               op0=ALU.mult, op1=ALU.add))
                    return fix
                for pc in range(npc):
                    init = car[:, 2 * q, n:n + 1] if pc == 0 else HF[:, pc * 1024 - 1:pc * 1024]
                    lru_piece(k, L, n, 0, pc * 1024, mkfix(0, 0) if pc == 0 else None, init,
                              [car_b] if pc == 0 else [HF_b], HF, HF_b, out_col0=pc * 1024)
                for pi, pc in enumerate(range(npc - 1, -1, -1)):
                    if pi == 0:
                        init, ib = car[:, 2 * q + 1, n:n + 1], [car_b]
                    else:
                        init, ib = tmp1[:, 1:2], [tmp1_b]
                    lru_piece(k, L, n, 1, pc * 1024, mkfix(1023, 2) if pi == 0 else None, init, ib, HR, HR_b)
                    if pi + 1 < npc:
                        tr.op("dve", [HR_b], [tmp1_b], lambda e: e.tensor_copy(out=tmp1[:, 1:2], in_=HR[:, 0:1]))
                    x = gr[:, pc * 1024:(pc + 1) * 1024]
                    tr.op("pool", [HR_b, HF_b], [HR_b],
                          lambda e: e.tensor_tensor(out=HR[:], in0=HR[:], in1=HF[:, pc * 1024:(pc + 1) * 1024], op=ALU.add))
                    tr.op("pool", [gr_b], [G1_b], lambda e: e.tensor_tensor(out=G1[:], in0=x, in1=x, op=ALU.mult))
                    tr.op("dve", [G1_b], [G1_b],
                          lambda e: e.tensor_scalar(out=G1[:], in0=G1[:], scalar1=0.044715, scalar2=1.0, op0=ALU.mult, op1=ALU.add))
                    tr.op("dve", [G1_b, gr_b], [G1_b], lambda e: e.tensor_tensor(out=G1[:], in0=G1[:], in1=x, op=ALU.mult))
                    tr.op("act", [G1_b], [G2_b], lambda e: e.activation(out=G2[:], in_=G1[:], func=AF.Tanh, scale=0.7978845608028654))
                    tr.op("dve", [G2_b, gr_b], [G2_b],
                          lambda e: e.scalar_tensor_tensor(out=G2[:], in0=G2[:], scalar=1.0, in1=x, op0=ALU.add, op1=ALU.mult))
                    o, o_b = outs.next()
                    tr.op("dve", [HR_b, G2_b], [o_b],
                          lambda e: e.scalar_tensor_tensor(out=o[:], in0=HR[:], scalar=0.5, in1=G2[:], op0=ALU.mult, op1=ALU.mult))
                    tr.dma("sp", k.LR[n * 128:(n + 1) * 128, ob + pc * 1024:ob + (pc + 1) * 1024], o[:], [o_b], [k.LR_b], o_b)


def attn_jobs():
    tab_u = {-6: 0, -4: 1, -2: 2, 0: 3, 2: 4, 4: 5, 6: 6}
    tab_g = {-4: 7, -2: 2, 0: 3, 2: 4, 4: 8}
    segs = []
    for (pb, n, ob) in ((0, 32, 0), (EXT_P // 128, 16, OWN_P)):
        npq = n // 2
        jobs = []
        for qi in range(npq):
            b = 2 + qi
            top, bot = qi < 2, qi >= npq - 2
            gflag = 1 if top else (3 if bot else 4)
            tiles = [(b + d2, tab_g[2 * d2], gflag) for d2 in (-2, -1, 0, 1, 2)]
            if top:
                tiles += [(a, tab_u[2 * (a - b)], 0) for a in (2, 3, 4, 5)]
            if bot:
                tiles += [(a, tab_u[2 * (a - b)], 2) for a in range(npq - 2, npq + 2)]
            jobs.append((pb, b, ob + qi * 128, tiles))
        segs.append(jobs)
    return segs[0] + segs[1]


def phase_attn(k):
    tr = k.tr
    scale = 128.0 ** -0.5
    HG = 4
    with phase_scope(k):
        sb = k.sb
        q_t = sb("at_q", [128, HG, EXT], BF16); q_b = Buf("at_q")
        k_t = sb("at_k", [128, HG, EXT], BF16); k_b = Buf("at_k")
        v_t = sb("at_v", [128, EXT // 128, HG * 128], BF16); v_b = Buf("at_v")
        tt = sb("at_tt", [128, HG, 9, 128], F32); tt_b = Buf("at_tt")
        ones = sb("at_ones", [128, 128], BF16); ones_b = Buf("at_ones")
        tr.op("dve", [], [ones_b], lambda e: e.memset(ones[:], 1.0))
        exs = Stage(k, "at_ex", [128, 128], F32, n=6)
        ets = Stage(k, "at_et", [128, 128], BF16, n=12)
        import os as _os
        recs = Stage(k, "at_rec", [128, 128], F32, n=int(_os.environ.get("AT_RECS", 2)))
        nas = Stage(k, "at_na", [128, OWN], BF16, n=2)
        s_slots = [(bank, j) for bank in range(4) for j in range(4)]
        s_bufs = [Buf(f"at_s{i}") for i in range(16)]
        s_rr = [0]
        import os
        jobs = attn_jobs()[int(os.environ.get('AT_JOB0', 0)):][:int(os.environ.get('AT_JOBS', 1000))]
        at_mode = int(os.environ.get('AT_MODE', 2))
        for hg in range(int(os.environ.get('AT_HG', NH // HG))):
            for hl in range(HG):
                tr.dma("sp", q_t[:, hl, :], k.QT[(hg * HG + hl) * 128:(hg * HG + hl + 1) * 128, :], [k.QT_b], [q_b], q_b)
                tr.dma("sp", k_t[:, hl, :], k.KT[(hg * HG + hl) * 128:(hg * HG + hl + 1) * 128, :], [k.KT_b], [k_b], k_b)
            for p0 in range(0, EXT // 128, 4):
                tr.dma("sp", v_t[:, p0:p0 + 4, :], k.VS.rearrange("(pr p) c -> p pr c", p=128)[:, p0:p0 + 4, hg * 512:(hg + 1) * 512],
                       [k.VS_b], [v_b], v_b)
            for hl in range(HG):
                tr.dma("sp", tt[:, hl, :, :], k.din("tt")[hg * HG + hl, :, :, :], [k.d_in], [tt_b], tt_b)
            tr.op("act", [tt_b], [tt_b], lambda e: e.activation(out=tt[:], in_=tt[:], func=AF.Exp))
            for hl in range(int(os.environ.get("AT_HL", HG)) if at_mode > 0 else 0):
                h = hg * HG + hl
                na, na_b = nas.next()

                flat = []
                for ji, job in enumerate(jobs):
                    pb, b, o, tiles = job
                    for i, (a, ti, fl) in enumerate(tiles):
                        flat.append((ji, pb, b, o, a, ti, fl, i, len(tiles)))
                LOOK = 3
                for idx in range(len(flat) + LOOK):
                    if idx < len(flat):
                        (ji, pb, b, o, a, ti, fl, i, nt) = flat[idx]
                        sbank = idx % 4
                        tr.pe_group([k_b, q_b], [k.psb[sbank]], [lambda e: e.matmul(
                            k.ps[sbank][:, 0:128], lhsT=k_t[:, hl, (pb + a) * 128:(pb + a + 1) * 128],
                            rhs=q_t[:, hl, (pb + b) * 128:(pb + b + 1) * 128], start=True, stop=True)])
                    j = idx - LOOK
                    if j < 0:
                        continue
                    (ji, pb, b, o, a, ti, fl, i, nt) = flat[j]
                    sbank = j % 4
                    ob, db = 4 + ji % 2, 6 + ji % 2
                    ex, ex_b = exs.next()
                    tr.op("act", [k.psb[sbank]], [ex_b],
                          lambda e: e.activation(out=ex[:], in_=k.ps[sbank][:, 0:128], func=AF.Exp, scale=scale))
                    et, et_b = ets.next()
                    tr.op("dve", [ex_b, tt_b, k.b_flags], [et_b],
                          lambda e: e.scalar_tensor_tensor(out=et[:], in0=ex[:], scalar=k.flags_t[:, fl:fl + 1],
                                                           in1=tt[:, hl, ti, :], op0=ALU.mult, op1=ALU.mult))
                    tr.pe_group([v_b, ones_b, et_b], [k.psb[ob], k.psb[db]], [
                        lambda e: e.matmul(k.ps[ob][:, 0:128], lhsT=v_t[:, pb + a, hl * 128:(hl + 1) * 128], rhs=et[:],
                                           start=(i == 0), stop=(i == nt - 1)),
                        lambda e: e.matmul(k.ps[db][:, 0:128], lhsT=ones[:], rhs=et[:], start=(i == 0), stop=(i == nt - 1))])
                    if i == nt - 1:
                        rc, rc_b = recs.next()
                        tr.op("dve", [k.psb[db]], [rc_b], lambda e: e.reciprocal(out=rc[:], in_=k.ps[db][:, 0:128]))
                        tr.op("dve", [k.psb[ob], rc_b], [na_b],
                              lambda e: e.tensor_tensor(out=na[:, o:o + 128], in0=k.ps[ob][:, 0:128], in1=rc[:], op=ALU.mult))
                tr.dma("sp", k.NA[h * 128:(h + 1) * 128, :], na[:], [na_b], [k.NA_b], na_b)


def evac_sq(k, bank, gc, stg, junk, junk_b, ssq, ssq_b, col):
    tr = k.tr
    t, b = stg.next()
    import os
    if os.environ.get("WO_EVAC", "1") == "0":
        tr.op("act", [k.psb[bank]], [junk_b, ssq_b],
              lambda e: e.activation(out=junk[:, 0:gc], in_=k.ps[bank][:, 0:gc], func=AF.Square, accum_out=ssq[:, col:col + 1]))
        tr.op("dve", [k.psb[bank]], [b], lambda e: e.tensor_copy(out=t[:, 0:gc], in_=k.ps[bank][:, 0:gc]))
    else:
        tr.op("dve", [k.psb[bank]], [b], lambda e: e.tensor_copy(out=t[:, 0:gc], in_=k.ps[bank][:, 0:gc]))
        tr.op("act", [b], [junk_b, ssq_b],
              lambda e: e.activation(out=junk[:, 0:gc], in_=t[:, 0:gc], func=AF.Square, accum_out=ssq[:, col:col + 1]))
    return t, b


def phase_wout(k):
    tr = k.tr
    with phase_scope(k):
        sb = k.sb
        gdT = sb("wo_gdT", [128, KC, 1024], BF16); gdT_b = Buf("wo_gdT")
        import os
        WGC = 512
        slabs = Slabs(k, "wo_slab", KC, WGC, n=2)
        stg = Stage(k, "wo_stg", [128, 512], F32, n=4)
        junk = sb("wo_junk", [128, 512], BF16); junk_b = Buf("wo_junk")
        for blk in range(OWN // 1024):
            o0 = blk * 1024
            load_T(k, gdT, gdT_b, k.GD, k.GD_b, KC, o0, 1024)

            def ep(c0, gc, st, bank):
                col = (blk * 8 + st) * 16 + c0 // WGC
                t, b = evac_sq(k, bank, gc, stg, junk, junk_b, k.ssq1, k.ssq1_b, col)
                tr.dma("sp", k.MIX[o0 + st * 128:o0 + (st + 1) * 128, c0:c0 + gc], t[:, 0:gc], [b], [k.MIX_b], b)
            linear_T(k, gdT, gdT_b, KC, 1024, "w_out", 0, D, slabs, ep, gcols=WGC)


def norm_resid(k, pfx, src, src_b, ssq, ssq_b, g_row, res_fn, dst, dst_b):
    tr = k.tr
    with phase_scope(k):
        sb = k.sb
        gbc = sb(pfx + "_gbc", [128, D], F32); gbc_b = Buf(pfx + "_gbc")
        tr.dma("sp", gbc[:], k.din("g4")[g_row:g_row + 1, :].broadcast_to([128, D]), [k.d_in], [gbc_b], gbc_b)
        ft = [sb(f"{pfx}_f{i}", [128, D], F32) for i in range(2)]; ft_b = [Buf(f"{pfx}_f{i}") for i in range(2)]
        rs = [sb(f"{pfx}_r{i}", [128, D], F32) for i in range(2)]; rs_b = [Buf(f"{pfx}_r{i}") for i in range(2)]
        st = sb(pfx + "_st", [128, 24, 4], F32); st_b = Buf(pfx + "_st")
        for i in range(OWN // 128):
            p = i % 2
            tr.dma("sp", ft[p][:], src[i * 128:(i + 1) * 128, :], [src_b], [ft_b[p]], ft_b[p])
            r_ap, r_buf = res_fn(i)
            tr.dma("sp", rs[p][:], r_ap, [r_buf], [rs_b[p]], rs_b[p])
            tr.op("dve", [ssq_b], [st_b],
                  lambda e: e.tensor_reduce(out=st[:, i, 0:1], in_=ssq[:, i * 16:(i + 1) * 16], axis=mybir.AxisListType.X, op=ALU.add))
            tr.op("act", [st_b, k.b_eps], [st_b],
                  lambda e: e.activation(out=st[:, i, 1:2], in_=st[:, i, 0:1], func=AF.Sqrt, scale=1.0 / D, bias=k.eps_t[:, 0:1]))
            tr.op("dve", [st_b], [st_b], lambda e: e.reciprocal(out=st[:, i, 2:3], in_=st[:, i, 1:2]))
            tr.op("dve", [ft_b[p], st_b, gbc_b], [ft_b[p]],
                  lambda e: e.scalar_tensor_tensor(out=ft[p][:], in0=ft[p][:], scalar=st[:, i, 2:3], in1=gbc[:],
                                                   op0=ALU.mult, op1=ALU.mult))
            tr.op("pool", [ft_b[p], rs_b[p]], [ft_b[p]],
                  lambda e: e.tensor_tensor(out=ft[p][:], in0=ft[p][:], in1=rs[p][:], op=ALU.add))
            tr.dma("sp", dst[i * 128:(i + 1) * 128, :], ft[p][:], [ft_b[p]], [dst_b], ft_b[p])


def phase_ffn_up(k):
    tr = k.tr
    with phase_scope(k):
        sb = k.sb
        rt = RmsT(k, "fu", 2, nx=1)
        xnT = sb("fu_xnT", [128, KC, 1024], BF16); xnT_b = Buf("fu_xnT")
        slabs = Slabs(k, "fu_slab", KC, 256, n=3)
        sgs = Stage(k, "fu_sg", [128, 512], F32, n=3)
        outs = Stage(k, "fu_out", [128, 512], BF16, n=3)
        for blk in range(OWN // 1024):
            o0 = blk * 1024
            for j in range(8):
                rms_tile(k, rt, k.X1[o0 + j * 128:o0 + (j + 1) * 128, :], [k.X1_b], xnT, xnT_b, j * 128)
            for c0 in range(0, DFF, 256):
                s_g, s_g_b = load_slab(k, slabs, "w_fg", 0, c0 // 256)
                s_u, s_u_b = load_slab(k, slabs, "w_fu", 0, c0 // 256)
                for h in range(2):
                    banks = []
                    for _ in range(4):
                        banks.append(k.bank_rr)
                        k.bank_rr = (k.bank_rr + 1) % 8
                    fns = []
                    for (sl, off) in ((s_g, 0), (s_u, 2)):
                        for kc in range(KC):
                            for fc in range(2):
                                fns.append(lambda e, kc=kc, fc=fc, sl=sl, off=off: e.matmul(
                                    k.ps[banks[off + fc]][:], lhsT=sl[:, kc, fc * 128:(fc + 1) * 128],
                                    rhs=xnT[:, kc, h * 512:(h + 1) * 512], start=(kc == 0), stop=(kc == KC - 1)))
                    tr.pe_group([xnT_b, s_g_b, s_u_b], [k.psb[b] for b in banks], fns)
                    for fc in range(2):
                        f = c0 // 128 + fc
                        sg, sg_b = sgs.next()
                        tr.op("act", [k.psb[banks[fc]]], [sg_b],
                              lambda e: e.activation(out=sg[:], in_=k.ps[banks[fc]][:], func=AF.Silu))
                        o, o_b = outs.next()
                        tr.op("dve", [k.psb[banks[2 + fc]], sg_b], [o_b],
                              lambda e: e.tensor_tensor(out=o[:], in0=k.ps[banks[2 + fc]][:], in1=sg[:], op=ALU.mult))
                        tr.dma("sp", k.HH[f * 128:(f + 1) * 128, o0 + h * 512:o0 + (h + 1) * 512], o[:], [o_b], [k.HH_b], o_b)


def phase_ffn_down(k):
    tr = k.tr
    NK = DFF // 128
    with phase_scope(k):
        sb = k.sb
        hT = sb("fd_hT", [128, NK, 512], BF16); hT_b = Buf("fd_hT")
        slabs = Slabs(k, "fd_slab", 8, 512, n=3)
        stg = Stage(k, "fd_stg", [128, 512], F32, n=4)
        junk = sb("fd_junk", [128, 512], BF16); junk_b = Buf("fd_junk")
        for grp in range(OWN // 512):
            o0 = grp * 512
            for q in range(0, NK, 22):
                n = min(22, NK - q)
                k.tr.dma("sp", hT[:, q:q + n, :], k.HH.rearrange("(kc p) t -> p kc t", p=128)[:, q:q + n, o0:o0 + 512],
                         [k.HH_b], [hT_b], hT_b)
            for c0 in range(0, D, 512):
                banks = []
                for _ in range(4):
                    banks.append(k.bank_rr)
                    k.bank_rr = (k.bank_rr + 1) % 8
                for kp in range(0, NK, 8):
                    nk = min(8, NK - kp)
                    sl, sl_b = load_slab(k, slabs, "w_fd", kp // 8, c0 // 512, nkc=nk)
                    fns = []
                    for j in range(nk):
                        for st in range(4):
                            fns.append(lambda e, j=j, st=st: e.matmul(
                                k.ps[banks[st]][:], lhsT=hT[:, kp + j, st * 128:(st + 1) * 128], rhs=sl[:, j, :],
                                start=(kp + j == 0), stop=(kp + j == NK - 1)))
                    tr.pe_group([hT_b, sl_b], [k.psb[b] for b in banks], fns)
                for st in range(4):
                    col = (grp * 4 + st) * 16 + c0 // 512
                    t, b = evac_sq(k, banks[st], 512, stg, junk, junk_b, k.ssq2, k.ssq2_b, col)
                    tr.dma("sp", k.FF[o0 + st * 128:o0 + (st + 1) * 128, c0:c0 + 512], t[:], [b], [k.FF_b], b)


def make_inputs(inp):
    f = lambda a: np.ascontiguousarray(np.asarray(a, dtype=np.float32))
    xp = f(inp["x_prompt"])[0]
    xs = f(inp["x_sample"])[0]
    shared = {
        "xp": xp, "xs": xs,
        "w_in": f(inp["w_in"])[0], "w_merge": f(inp["w_merge"])[0], "w_na": f(inp["w_na_out"])[0],
        "w_lru": f(inp["w_lru_out"])[0], "w_out": f(inp["w_out"])[0], "w_fg": f(inp["w_ffn_gate"])[0],
        "w_fu": f(inp["w_ffn_up"])[0], "w_fd": f(inp["w_ffn_down"])[0],
        "g4": f(np.stack([inp["g_mix_pre"][0], inp["g_mix_post"][0], inp["g_ffn_pre"][0], inp["g_ffn_post"][0]])),
        "b_merge": f(np.asarray(inp["b_merge"])[0].reshape(64, 128).T),
        "ident": np.eye(128, dtype=np.float32),
    }
    wc = np.asarray(inp["w_conv"])[0]
    bc = np.asarray(inp["b_conv"])[0]
    convp = np.concatenate([wc, bc[None]], 0)
    shared["convp"] = f(convp.reshape(5, 16, 128).transpose(2, 1, 0))
    wr = np.asarray(inp["w_rgate"])[0]
    wi = np.asarray(inp["w_igate"])[0]
    wg = np.stack([wr, wi], 0)
    shared["wgate"] = f(wg.transpose(3, 0, 1, 2, 4).reshape(128, 64, 128))
    gp = np.stack([np.asarray(inp["b_rgate"])[0][0], np.asarray(inp["b_rgate"])[0][1],
                   np.asarray(inp["b_igate"])[0][0], np.asarray(inp["b_igate"])[0][1],
                   np.asarray(inp["lru_lambda"])[0][0], np.asarray(inp["lru_lambda"])[0][1]], 0)
    shared["gatep"] = f(gp.reshape(6, 16, 128).transpose(2, 1, 0))
    shared["tt"] = make_tt(np.asarray(inp["rpb"], dtype=np.float32)[0])
    maps = []
    for c in range(NCORES):
        m = dict(shared)
        xo = np.zeros((EXT, D), np.float32)
        for (src, L, own, e0) in ((xp, LP, OWN_P, 0), (xs, LS, OWN_S, EXT_P)):
            lo, hi = c * own - 256, (c + 1) * own + 256
            slo, shi = max(lo, 0), min(hi, L)
            xo[e0 + (slo - lo):e0 + (shi - lo)] = src[slo:shi]
        m["xo"] = xo
        fl = np.zeros((128, 8), np.float32)
        ftop, fbot = float(c == 0), float(c == NCORES - 1)
        fl[:, 0], fl[:, 1], fl[:, 2], fl[:, 3], fl[:, 4] = ftop, 1 - ftop, fbot, 1 - fbot, 1.0
        m["flags"] = fl
        sel = np.zeros((128, 4, 17), np.float32)
        sel[:, 0, 2 * c] = 1.0
        sel[:, 1, 2 * c + 2] = 1.0
        sel[:, 2, c] = 1.0
        sel[:, 3, c + 1] = 1.0
        m["sel"] = sel
        maps.append(m)
    return maps


TT_D = [(-6, False), (-4, False), (-2, False), (0, False), (2, False), (4, False), (6, False), (-4, True), (4, True)]


def make_tt(rpb):
    kc = np.arange(64)[:, None]
    qc = np.arange(64)[None, :]
    cs = np.clip(qc - 8, 0, 48)
    colok = (kc >= cs) & (kc < cs + 16)
    cidx = np.clip(kc - qc + 15, 0, 30)
    out = np.full((NH, 128, 9, 128), NEG, np.float32)
    for ti, (d, generic) in enumerate(TT_D):
        for kr in range(2):
            for qr in range(2):
                dr = d + kr - qr
                if dr < -7 or dr > 7:
                    continue
                if generic and not (-4 <= dr <= 3):
                    continue
                blk = np.where(colok[None], rpb[:, dr + 7][:, cidx], NEG)
                out[:, kr * 64:(kr + 1) * 64, ti, qr * 64:(qr + 1) * 64] = blk
    return out


ALL_PHASES = ("proj", "lru1", "lru2", "attn", "gate", "wout", "ffn")
_NC_CACHE = {}


def kernel(**inputs):
    maps = make_inputs(inputs)
    if "nc" not in _NC_CACHE:
        _NC_CACHE["nc"] = build(phases=ALL_PHASES)
    nc, used = _NC_CACHE["nc"]
    maps = [{n: m[n] for n in used} for m in maps]
    res = run_bass_kernel_spmd(nc, maps, core_ids=list(range(NCORES)))
    yp = np.zeros((1, LP, D), np.float32)
    ys = np.zeros((1, LS, D), np.float32)
    for c in range(NCORES):
        y = res.results[c]["y"]
        yp[0, c * OWN_P:(c + 1) * OWN_P] = y[:OWN_P]
        ys[0, c * OWN_S:(c + 1) * OWN_S] = y[OWN_P:]
    return yp, ys
```
